# Optimizing a Trainium2 kernel written in Bass

```python
import math
import jax, jax.numpy as jnp
from jax import lax
import numpy as np

D_MODEL = 1024
BATCH = 16
SEQ = 4096
DEPTH = 1

MEM_LEN = 256
EPS = 1e-6
GM_WIDTH = D_MODEL
GM_CHUNK = 128
GM_GROUP_CH = 128
GM_GROUPS = GM_WIDTH // GM_GROUP_CH
S5_WIDTH = D_MODEL // 2
S5_GROUP_CH = 16
S5_GROUPS = S5_WIDTH // S5_GROUP_CH
S5_STATE = 64
CA_HEADS = 4
CA_HEAD_DIM = D_MODEL // CA_HEADS
FFN_HIDDEN = ((8 * D_MODEL + 3 * 256 - 1) // (3 * 256)) * 256
IN_COLS = 2 * GM_WIDTH + S5_WIDTH + 2 * D_MODEL
SPLIT_POINTS = (2 * GM_WIDTH, 2 * GM_WIDTH + S5_WIDTH, 2 * GM_WIDTH + S5_WIDTH + D_MODEL)

kernel_name = 'hybrid_gmlp_s5_memxattn_block'


def rms_norm(x, g):
    xf = x.astype(jnp.float32)
    y = xf * lax.rsqrt(jnp.mean(xf * xf, axis=-1, keepdims=True) + EPS)
    return (y * g.astype(jnp.float32)).astype(x.dtype)


def layer_norm(x, g, b):
    xf = x.astype(jnp.float32)
    mu = jnp.mean(xf, axis=-1, keepdims=True)
    xc = xf - mu
    y = xc * lax.rsqrt(jnp.mean(xc * xc, axis=-1, keepdims=True) + EPS)
    return (y * g.astype(jnp.float32) + b.astype(jnp.float32)).astype(x.dtype)


def gmlp_branch(z, ln_g, ln_b, w_s, b_s):
    bsz, seq, _ = z.shape
    z = jax.nn.gelu(z)
    u, v = jnp.split(z, 2, axis=-1)
    v = layer_norm(v, ln_g, ln_b)
    v = v.reshape(bsz, seq // GM_CHUNK, GM_CHUNK, GM_GROUPS, GM_GROUP_CH)
    mask = jnp.tril(jnp.ones((GM_CHUNK, GM_CHUNK), dtype=bool))
    w = jnp.where(mask[None], w_s, jnp.zeros((), w_s.dtype))
    sv = jnp.einsum('gts,bnsgc->bntgc', w, v) + b_s.T[:, :, None]
    return u * sv.reshape(bsz, seq, GM_WIDTH)


def _ssm_combine(e1, e2):
    a1r, a1i, b1r, b1i = e1
    a2r, a2i, b2r, b2i = e2
    ar = a2r * a1r - a2i * a1i
    ai = a2r * a1i + a2i * a1r
    br = a2r * b1r - a2i * b1i + b2r
    bi = a2r * b1i + a2i * b1r + b2i
    return (ar, ai, br, bi)


def s5_branch(u, lam_re, lam_im, log_step, b_re, b_im, c_re, c_im, d, w_glu):
    f32 = jnp.float32
    dt = u.dtype
    bsz, seq, _ = u.shape
    uf = u.astype(f32).reshape(bsz, seq, S5_GROUPS, S5_GROUP_CH)
    lr = lam_re.astype(f32)
    li = lam_im.astype(f32)
    step = jnp.exp(log_step.astype(f32))[:, None]
    mag = jnp.exp(lr * step)
    ab_re = mag * jnp.cos(li * step)
    ab_im = mag * jnp.sin(li * step)
    den = lr * lr + li * li
    nr = ab_re - 1.0
    co_re = (nr * lr + ab_im * li) / den
    co_im = (ab_im * lr - nr * li) / den
    br_ = b_re.astype(f32)
    bi_ = b_im.astype(f32)
    bb_re = co_re[..., None] * br_ - co_im[..., None] * bi_
    bb_im = co_re[..., None] * bi_ + co_im[..., None] * br_
    bu_re = jnp.einsum('bsgh,gph->bsgp', uf, bb_re)
    bu_im = jnp.einsum('bsgh,gph->bsgp', uf, bb_im)
    a_re = jnp.broadcast_to(ab_re, (seq, S5_GROUPS, S5_STATE))
    a_im = jnp.broadcast_to(ab_im, (seq, S5_GROUPS, S5_STATE))

    def scan_one(br, bi):
        _, _, sr, si = lax.associative_scan(_ssm_combine, (a_re, a_im, br, bi), axis=0)
        return sr, si

    s_re, s_im = jax.vmap(scan_one)(bu_re, bu_im)
    y = (jnp.einsum('bsgp,ghp->bsgh', s_re, c_re.astype(f32))
         - jnp.einsum('bsgp,ghp->bsgh', s_im, c_im.astype(f32))
         + d.astype(f32) * uf)
    y = jax.nn.gelu(y.reshape(bsz, seq, S5_WIDTH))
    y = y * jax.nn.sigmoid(y @ w_glu.astype(f32))
    return y.astype(dt)


def cross_attention(h, mem_n, w_q, w_kv, w_o):
    bsz, seq, _ = h.shape
    q = (h @ w_q).reshape(bsz, seq, CA_HEADS, CA_HEAD_DIM)
    k, v = jnp.split(mem_n @ w_kv, 2, axis=-1)
    k = k.reshape(bsz, -1, CA_HEADS, CA_HEAD_DIM)
    v = v.reshape(bsz, -1, CA_HEADS, CA_HEAD_DIM)
    s = jnp.einsum('bshd,bmhd->bhsm', q, k).astype(jnp.float32) * (CA_HEAD_DIM ** -0.5)
    p = jax.nn.softmax(s, axis=-1).astype(h.dtype)
    o = jnp.einsum('bhsm,bmhd->bshd', p, v).reshape(bsz, seq, D_MODEL)
    return o @ w_o


def swiglu(h, w_gu, w_down):
    g, u = jnp.split(h @ w_gu, 2, axis=-1)
    return (jax.nn.silu(g) * u) @ w_down


def setup_inputs(seed: int = 0) -> dict:
    key = jax.random.key(seed)
    ks = jax.random.split(key, 32)
    L = DEPTH
    f32 = jnp.float32

    def nrm(k, shape, scale):
        return jax.random.normal(k, shape, f32) * scale

    def gain(k, shape):
        return 1.0 + 0.01 * jax.random.normal(k, shape, f32)

    lam_re = -0.5 * jnp.exp(0.05 * jax.random.normal(ks[8], (L, S5_GROUPS, S5_STATE), f32))
    lam_im = (math.pi * jnp.arange(S5_STATE, dtype=f32))[None, None, :] + 0.01 * jax.random.normal(ks[9], (L, S5_GROUPS, S5_STATE), f32)
    log_step = jax.random.uniform(ks[10], (L, S5_GROUPS), f32, math.log(1e-3), math.log(1e-1))
    return {
        'x': jax.random.normal(ks[0], (BATCH, SEQ, D_MODEL), f32),
        'mem': jax.random.normal(ks[1], (BATCH, MEM_LEN, D_MODEL), f32),
        'g_mix_pre': gain(ks[2], (L, D_MODEL)),
        'w_in': nrm(ks[3], (L, D_MODEL, IN_COLS), D_MODEL ** -0.5),
        'gm_ln_g': gain(ks[4], (L, GM_WIDTH)),
        'gm_ln_b': nrm(ks[5], (L, GM_WIDTH), 0.01),
        'gm_w_s': nrm(ks[6], (L, GM_GROUPS, GM_CHUNK, GM_CHUNK), GM_CHUNK ** -0.5),
        'gm_b_s': gain(ks[7], (L, GM_GROUPS, GM_CHUNK)),
        's5_lam_re': lam_re,
        's5_lam_im': lam_im,
        's5_log_step': log_step,
        's5_b_re': nrm(ks[11], (L, S5_GROUPS, S5_STATE, S5_GROUP_CH), (2 * S5_GROUP_CH) ** -0.5),
        's5_b_im': nrm(ks[12], (L, S5_GROUPS, S5_STATE, S5_GROUP_CH), (2 * S5_GROUP_CH) ** -0.5),
        's5_c_re': nrm(ks[13], (L, S5_GROUPS, S5_GROUP_CH, S5_STATE), (2 * S5_STATE) ** -0.5),
        's5_c_im': nrm(ks[14], (L, S5_GROUPS, S5_GROUP_CH, S5_STATE), (2 * S5_STATE) ** -0.5),
        's5_d': nrm(ks[15], (L, S5_GROUPS, S5_GROUP_CH), 1.0),
        's5_w_glu': nrm(ks[16], (L, S5_WIDTH, S5_WIDTH), S5_WIDTH ** -0.5),
        'w_br_gm': nrm(ks[17], (L, GM_WIDTH, D_MODEL), GM_WIDTH ** -0.5),
        'w_br_s5': nrm(ks[18], (L, S5_WIDTH, D_MODEL), S5_WIDTH ** -0.5),
        'w_mix_out': nrm(ks[19], (L, D_MODEL, D_MODEL), D_MODEL ** -0.5),
        'g_mix_post': gain(ks[20], (L, D_MODEL)),
        'g_ca_pre': gain(ks[21], (L, D_MODEL)),
        'g_mem': gain(ks[22], (L, D_MODEL)),
        'ca_w_q': nrm(ks[23], (L, D_MODEL, D_MODEL), D_MODEL ** -0.5),
        'ca_w_kv': nrm(ks[24], (L, D_MODEL, 2 * D_MODEL), D_MODEL ** -0.5),
        'ca_w_o': nrm(ks[25], (L, D_MODEL, D_MODEL), D_MODEL ** -0.5),
        'g_ca_post': gain(ks[26], (L, D_MODEL)),
        'g_ffn_pre': gain(ks[27], (L, D_MODEL)),
        'ffn_w_gu': nrm(ks[28], (L, D_MODEL, 2 * FFN_HIDDEN), D_MODEL ** -0.5),
        'ffn_w_down': nrm(ks[29], (L, FFN_HIDDEN, D_MODEL), FFN_HIDDEN ** -0.5),
        'g_ffn_post': gain(ks[30], (L, D_MODEL)),
    }


def reference(x, mem, g_mix_pre, w_in, gm_ln_g, gm_ln_b, gm_w_s, gm_b_s,
              s5_lam_re, s5_lam_im, s5_log_step, s5_b_re, s5_b_im, s5_c_re, s5_c_im,
              s5_d, s5_w_glu, w_br_gm, w_br_s5, w_mix_out, g_mix_post,
              g_ca_pre, g_mem, ca_w_q, ca_w_kv, ca_w_o, g_ca_post,
              g_ffn_pre, ffn_w_gu, ffn_w_down, g_ffn_post):
    for l in range(DEPTH):
        h = rms_norm(x, g_mix_pre[l])
        z = h @ w_in[l]
        z_gm, z_s5, z_ga, z_gb = jnp.split(z, SPLIT_POINTS, axis=-1)
        y_gm = gmlp_branch(z_gm, gm_ln_g[l], gm_ln_b[l], gm_w_s[l], gm_b_s[l])
        y_s5 = s5_branch(z_s5, s5_lam_re[l], s5_lam_im[l], s5_log_step[l],
                         s5_b_re[l], s5_b_im[l], s5_c_re[l], s5_c_im[l], s5_d[l], s5_w_glu[l])
        merged = (jax.nn.sigmoid(z_ga) * (y_gm @ w_br_gm[l])
                  + jax.nn.sigmoid(z_gb) * (y_s5 @ w_br_s5[l]))
        x = x + rms_norm(merged @ w_mix_out[l], g_mix_post[l])
        hc = rms_norm(x, g_ca_pre[l])
        mem_n = rms_norm(mem, g_mem[l])
        x = x + rms_norm(cross_attention(hc, mem_n, ca_w_q[l], ca_w_kv[l], ca_w_o[l]), g_ca_post[l])
        hf = rms_norm(x, g_ffn_pre[l])
        x = x + rms_norm(swiglu(hf, ffn_w_gu[l], ffn_w_down[l]), g_ffn_post[l])
    return x
```

```python
import contextlib
import numpy as np
import concourse.bass as bass
import concourse.mybir as mybir
from concourse.bass_utils import run_bass_kernel_spmd

F32 = mybir.dt.float32
BF16 = mybir.dt.bfloat16
I32 = mybir.dt.int32
AF = mybir.ActivationFunctionType
ALU = mybir.AluOpType

NCORES = 8
SEQ = 4096
D = 1024
TT = 512
EPS = 1e-6
ENGS = ("pe", "act", "dve", "pool", "sp")


class Sched:
    def __init__(self, nc):
        self.nc = nc
        self.ops = []
        self.last_w = {}
        self.readers = {}
        self.dma_cnt = {}

    def op(self, eng, fn, reads=(), writes=(), dma_sem=None):
        idx = len(self.ops)
        deps = set()
        for k in reads:
            w = self.last_w.get(k)
            if w is not None:
                deps.add(w)
        for k in writes:
            w = self.last_w.get(k)
            if w is not None:
                deps.add(w)
            for r in self.readers.get(k, ()):
                deps.add(r)
        if eng == "pe":
            deps = {d for d in deps if self.ops[d]["eng"] != "pe"}
        o = dict(eng=eng, fn=fn, deps=sorted(deps), pub=False, dma=dma_sem, val=None)
        if dma_sem is not None:
            c = self.dma_cnt.get(dma_sem, 0) + 16
            self.dma_cnt[dma_sem] = c
            o["val"] = c
            o["pub"] = True
        self.ops.append(o)
        for d in deps:
            self.ops[d]["pub"] = True
        for k in reads:
            self.readers.setdefault(k, []).append(idx)
        for k in writes:
            self.last_w[k] = idx
            self.readers[k] = []
        return idx

    def emit(self, finals=()):
        nc = self.nc
        fin = {}
        for (ename, d) in finals:
            fin.setdefault(ename, []).append(d)
            self.ops[d]["pub"] = True
        cnt = {e: 0 for e in ENGS}
        for o in self.ops:
            if o["dma"] is None and o["pub"]:
                cnt[o["eng"]] += 1
                o["val"] = cnt[o["eng"]]
        with contextlib.ExitStack() as es:
            esem = {e: es.enter_context(nc.semaphore("s_" + e)) for e in ENGS}
            dsem = {n: es.enter_context(nc.semaphore("d_" + n)) for n in self.dma_cnt}
            block = es.enter_context(nc.Block())

            def semof(o):
                if o["dma"] is not None:
                    return ("d", o["dma"]), dsem[o["dma"]]
                return ("e", o["eng"]), esem[o["eng"]]

            def run(ename, h):
                known = {}

                def wait(d):
                    p = self.ops[d]
                    key, sh = semof(p)
                    if known.get(key, 0) < p["val"]:
                        h.wait_ge(sh, p["val"])
                        known[key] = p["val"]

                for o in self.ops:
                    if o["eng"] != ename:
                        continue
                    for d in o["deps"]:
                        wait(d)
                    inst = o["fn"](h)
                    if o["pub"]:
                        _, sh = semof(o)
                        inst.then_inc(sh, 16 if o["dma"] is not None else 1)
                for d in fin.get(ename, ()):
                    wait(d)

            @block.tensor
            def _(h):
                run("pe", h)

            @block.scalar
            def _(h):
                run("act", h)

            @block.vector
            def _(h):
                run("dve", h)

            @block.gpsimd
            def _(h):
                run("pool", h)

            @block.sync
            def _(h):
                run("sp", h)


def _slab_catalog():
    cat = {}
    off = 0

    def add(name, n, fold=None):
        nonlocal off
        cat[name] = (off, n, fold)
        off += n

    for s in range(9):
        add(f"in{s}", 4096, 0)
    for s in range(2):
        add(f"gm{s}", 4096)
    for s in range(2):
        add(f"bs{s}", 2048, "half")
    for s in range(2):
        add(f"mo{s}", 4096, "half")
    add("glu", 2048)
    for s in range(2):
        add(f"q{s}", 4096, 1)
    for s in range(4):
        add(f"kv{s}", 4096, 3)
    for s in range(2):
        add(f"o{s}", 4096)
    for s in range(11):
        add(f"gu{s}", 4096, 2)
    for h in range(2):
        for kg in range(3):
            add(f"dn{h}{kg}", 4096 if kg < 2 else 3072)
    nf32 = off
    for m in range(6):
        add(f"at{m}", 4096)
    add("wa0", 4096)
    add("wa1", 4096)
    add("kd", 4096)
    add("wc", 4096)
    return cat, nf32, off


CAT, NF32, NTOT = _slab_catalog()


def _slabify(W, c0, c1):
    K = W.shape[0]
    kc = K // 128
    return np.ascontiguousarray(
        W[:, c0:c1].reshape(kc, 128, c1 - c0).transpose(1, 0, 2)).reshape(128, kc * (c1 - c0))


def _host_weights(inp):
    parts = {}
    w_in = inp["w_in"][0]
    for s in range(9):
        parts[f"in{s}"] = _slabify(w_in, 512 * s, 512 * s + 512)
    for nm, key in (("gm", "w_br_gm"), ("bs", "w_br_s5"), ("mo", "w_mix_out"), ("q", "ca_w_q"),
                    ("o", "ca_w_o")):
        W = inp[key][0]
        for s in range(2):
            parts[f"{nm}{s}"] = _slabify(W, 512 * s, 512 * s + 512)
    parts["glu"] = _slabify(inp["s5_w_glu"][0], 0, 512)
    W = inp["ca_w_kv"][0]
    for s in range(4):
        parts[f"kv{s}"] = _slabify(W, 512 * s, 512 * s + 512)
    W = inp["ffn_w_gu"][0]
    for s in range(11):
        Wc = np.concatenate([W[:, 256 * s:256 * s + 256], W[:, 2816 + 256 * s:2816 + 256 * s + 256]], axis=1)
        parts[f"gu{s}"] = _slabify(Wc, 0, 512)
    W = inp["ffn_w_down"][0]
    for h in range(2):
        for kg in range(3):
            r0 = kg * 1024
            r1 = min(r0 + 1024, 2816)
            parts[f"dn{h}{kg}"] = _slabify(W[r0:r1], 512 * h, 512 * h + 512)
    out = np.empty((128, NF32), np.float32)
    for nm, (off, n, _) in CAT.items():
        if off >= NF32:
            continue
        assert parts[nm].shape == (128, n), (nm, parts[nm].shape, n)
        out[:, off:off + n] = parts[nm]
    return out


def _host_small(inp):
    f = np.float32
    sm = {}
    sm["gpost"] = np.ascontiguousarray(np.broadcast_to(
        np.concatenate([inp["g_mix_post"][0], inp["g_ca_post"][0], inp["g_ffn_post"][0]])[None, :], (128, 3072))).astype(f)
    gp = np.stack([inp["g_mix_pre"][0], inp["g_ca_pre"][0], inp["g_ffn_pre"][0], inp["g_mem"][0]], 0)
    sm["gpre"] = np.ascontiguousarray(gp.reshape(4, 8, 128).transpose(2, 0, 1)).reshape(128, 32).astype(f)
    sm["glng"] = np.ascontiguousarray(inp["gm_ln_g"][0].reshape(8, 128).T).astype(f)
    sm["rows"] = np.ascontiguousarray(np.stack([inp["gm_ln_b"][0], inp["gm_b_s"][0].reshape(1024)], 0)).astype(f)
    sm["wst"] = np.ascontiguousarray(inp["gm_w_s"][0].transpose(2, 0, 1)).reshape(128, 1024).astype(f)
    lr = inp["s5_lam_re"][0].T
    li = inp["s5_lam_im"][0].T
    ls = np.broadcast_to(inp["s5_log_step"][0][None, :], (128, 32))
    sm["lam"] = np.ascontiguousarray(np.concatenate(
        [np.concatenate([lr, lr], 0), np.concatenate([li, li], 0), ls], 1)).astype(f)
    b1 = inp["s5_b_re"][0].transpose(1, 0, 2).reshape(64, 512)
    b2 = inp["s5_b_im"][0].transpose(1, 0, 2).reshape(64, 512)
    sm["b12"] = np.ascontiguousarray(np.concatenate(
        [np.concatenate([b1, b1], 0), np.concatenate([b2, b2], 0)], 1)).astype(f)
    c1 = inp["s5_c_re"][0].transpose(2, 0, 1).reshape(64, 512)
    c2 = inp["s5_c_im"][0].transpose(2, 0, 1).reshape(64, 512)
    sm["c12"] = np.ascontiguousarray(np.concatenate(
        [np.concatenate([c1, c1], 0), np.concatenate([c2, c2], 0)], 1)).astype(f)
    sm["dcol"] = np.ascontiguousarray(inp["s5_d"][0].reshape(4, 128).T).astype(f)
    return sm


SMALL_SHAPES = dict(gpost=[128, 3072], gpre=[128, 32], glng=[128, 8], rows=[2, 1024], wst=[128, 1024],
                    lam=[128, 96], b12=[128, 1024], c12=[128, 1024], dcol=[128, 4])


def sig_g(sig):
    q, r = divmod(sig, 8)
    ct, gq = divmod(r, 2)
    return ct * 8 + 2 * q + gq


def build(nseq=2, tps=8, stop_after="ffn"):
    nc = bass.Bass("TRN2", target_bir_lowering=False)
    ntok = nseq * SEQ
    x_d = nc.dram_tensor("x", [ntok, D], F32, kind="ExternalInput").ap()
    mem_d = nc.dram_tensor("mem", [nseq * 256, D], F32, kind="ExternalInput").ap()
    wf_d = nc.dram_tensor("wf", [128, NF32], F32, kind="ExternalInput").ap()
    sm_d = {k: nc.dram_tensor(k, shp, F32, kind="ExternalInput").ap() for k, shp in SMALL_SHAPES.items()}
    y_d = nc.dram_tensor("y", [ntok, D], F32, kind="ExternalOutput").ap()
    wb_d = nc.dram_tensor("wb", [128, NTOT], BF16, kind="Internal").ap()

    S = Sched(nc)
    es = contextlib.ExitStack()
    with es:
        def sbt(name, shape, dt):
            return es.enter_context(nc.sbuf_tensor("sb_" + name, shape, dt))

        KB = 1024
        ARENA_KB = 156
        A = sbt("arena", [128, ARENA_KB * KB // 2], BF16)
        PS = es.enter_context(nc.psum_tensor("ps", [128, 4096], F32))

        class Buf:
            def __init__(self, off_kb, size_b, dt, shape_tail, chunk_b=None):
                self.off = int(off_kb * KB)
                self.size = size_b
                self.dt = dt
                self.chunk_b = chunk_b or size_b
                v = A[:, self.off // 2:(self.off + size_b) // 2]
                if dt == F32:
                    v = v.bitcast(F32)
                elif dt == I32:
                    v = v.bitcast(I32)
                if len(shape_tail) == 2:
                    v = v.rearrange("p (a b) -> p a b", b=shape_tail[1])
                elif len(shape_tail) == 3:
                    v = v.rearrange("p (a b c) -> p a b c", b=shape_tail[1], c=shape_tail[2])
                elif len(shape_tail) == 4:
                    v = v.rearrange("p (a b c d) -> p a b c d", b=shape_tail[1], c=shape_tail[2], d=shape_tail[3])
                self.v = v

            def k(self, c=None, n=1):
                if c is None:
                    lo, hi = self.off, self.off + self.size
                else:
                    lo = self.off + c * self.chunk_b
                    hi = lo + n * self.chunk_b
                return [("A", p) for p in range(lo // KB, (hi - 1) // KB + 1)]

        def psk(b, n=1):
            return [("ps", i) for i in range(b, b + n)]

        def psv(b, n=1):
            return PS[:, 512 * b:512 * (b + n)]

        def psbf(b):
            return PS[:, 512 * b:512 * (b + 1)].bitcast(BF16)

        bank_ctr = [0]

        def bank(n=1):
            b = bank_ctr[0]
            if n > 1 and b % n:
                b += n - b % n
            if b + n > 8:
                b = 0
            bank_ctr[0] = (b + n) % 8
            return b

        XS = [Buf(0, 16 * KB, F32, [4, 1024], 4 * KB), Buf(16, 16 * KB, F32, [4, 1024], 4 * KB)]
        RING = [Buf(32 + 8 * i, 8 * KB, BF16, [4096]) for i in range(4)]
        U0 = 64
        HT = Buf(144, 8 * KB, BF16, [8, 512], KB)
        HTOK = [Buf(152, 2 * KB, BF16, [1024]), Buf(154, 2 * KB, BF16, [1024])]
        UT = Buf(U0 + 0, 8 * KB, BF16, [8, 512], KB)
        ZS5 = Buf(U0 + 8, 4 * KB, BF16, [4, 512], KB)
        VT = [Buf(U0 + 12, 4 * KB, F32, [1024]), Buf(U0 + 16, 4 * KB, F32, [1024])]
        ZS5J = Buf(U0 + 12, 4 * KB, BF16, [4, 512], KB)
        VHAT = [Buf(U0 + 20, 2 * KB, BF16, [1024]), Buf(U0 + 22, 2 * KB, BF16, [1024])]
        YGM = Buf(U0 + 24, 8 * KB, BF16, [8, 512], KB)
        XA = Buf(U0 + 32, 4 * KB, BF16, [32, 64], 128)
        YTOK = [Buf(U0 + 36, 2 * KB, BF16, [1024]), Buf(U0 + 38, 2 * KB, BF16, [1024])]
        YGT = Buf(U0 + 40, 4 * KB, BF16, [4, 512], KB)
        SGL = Buf(U0 + 44, 4 * KB, BF16, [4, 512], KB)
        YS5 = Buf(U0 + 48, 4 * KB, BF16, [4, 512], KB)
        SGA = Buf(U0 + 52, 8 * KB, BF16, [8, 512], KB)
        MRG = Buf(U0 + 60, 8 * KB, BF16, [8, 512], KB)
        TMPA = [Buf(U0 + 68, 4 * KB, F32, [1024]), Buf(U0 + 72, 4 * KB, F32, [1024])]
        M2 = [Buf(U0 + 76, 2 * KB, F32, [512]), Buf(U0 + 78, 2 * KB, F32, [512])]
        QT = Buf(U0 + 0, 8 * KB, BF16, [8, 512], KB)
        PT = Buf(U0 + 8, 8 * KB, BF16, [8, 512], KB)
        RS = [Buf(U0 + 16, 2 * KB, F32, [512]), Buf(U0 + 18, 2 * KB, F32, [512])]
        OT = Buf(U0 + 24, 8 * KB, BF16, [8, 512], KB)
        ACTB = Buf(U0 + 0, 22 * KB, BF16, [22, 512], KB)
        SILU = [Buf(U0 + 22, KB, BF16, [512]), Buf(U0 + 23, KB, BF16, [512])]
        YBUF = Buf(U0 + 24, 16 * KB, F32, [4, 1024], 4 * KB)
        JUNK = Buf(U0 + 40, 2 * KB, BF16, [1024])
        JUNK2 = Buf(U0 + 42, 2 * KB, BF16, [1024])
        HTOK4 = [Buf(U0 + 44 + 2 * i, 2 * KB, BF16, [1024]) for i in range(4)]

        gpost = sbt("gpost", [128, 3, 1024], F32)
        gpre = sbt("gpre", [128, 4, 8], F32)
        glng = sbt("glng", [128, 8], F32)
        ident_bf = sbt("ident_bf", [128, 128], BF16)
        ident_f = sbt("ident_f", [128, 128], F32)
        swap_f = sbt("swap_f", [128, 128], F32)
        swap_bf = sbt("swap_bf", [128, 128], BF16)
        ones_bf = sbt("ones_bf", [128, 128], BF16)
        ones_f = sbt("ones_f", [128, 1], F32)
        wst_bf = sbt("wst_bf", [128, 8, 128], BF16)
        tb = sbt("tb", [128, 8, 128], F32)
        KT = sbt("KT", [128, nseq, 8, 256], BF16)
        VV = sbt("VV", [128, nseq, 2, 1024], BF16)
        EALL = [sbt("eall0", [128, 32, 65], BF16), sbt("eall1", [128, 32, 65], BF16)]
        stat = sbt("stat", [128, 64], F32)
        dcol = sbt("dcol", [128, 4], F32)
        half = sbt("half", [128, 4], F32)
        epsc = sbt("epsc", [128, 1], F32)
        mhalf = sbt("mhalf", [128, 1], F32)

        stat_ctr = [0]

        def stat_slot(n=1):
            s = stat_ctr[0]
            if s + n > 64:
                s = 0
            stat_ctr[0] = s + n
            return s

        def stk(s, n=1):
            return [("stat", i) for i in range(s, s + n)]

        def mm(out, lhsT, rhs, start, stop, reads, writes, tp=None):
            def f(h, out=out, lhsT=lhsT, rhs=rhs, start=start, stop=stop, tp=tp):
                if tp is None:
                    return h.matmul(out, lhsT=lhsT, rhs=rhs, start=start, stop=stop, skip_group_check=True)
                return h.matmul(out, lhsT=lhsT, rhs=rhs, start=start, stop=stop, tile_position=tp,
                                skip_group_check=True)
            return S.op("pe", f, reads, writes)

        def tr(out, in_, reads, writes, ident=None):
            ident = ident_bf[:] if ident is None else ident
            return S.op("pe", lambda h, out=out, in_=in_, ident=ident: h.transpose(out=out, in_=in_, identity=ident),
                        list(reads) + ["ident"], writes)

        def act(out, in_, func, reads, writes, scale=None, bias=None, accum=None):
            def f(h, out=out, in_=in_, func=func, scale=scale, bias=bias, accum=accum):
                kw = {}
                if scale is not None:
                    kw["scale"] = scale
                if bias is not None:
                    kw["bias"] = bias
                if accum is not None:
                    kw["accum_out"] = accum
                return h.activation(out=out, in_=in_, func=func, **kw)
            return S.op("act", f, reads, writes)

        def ts(eng, out, in0, s1, s2, op0, op1, reads, writes):
            def f(h, out=out, in0=in0, s1=s1, s2=s2, op0=op0, op1=op1):
                if op1 is None:
                    return h.tensor_scalar(out=out, in0=in0, scalar1=s1, scalar2=None, op0=op0)
                return h.tensor_scalar(out=out, in0=in0, scalar1=s1, scalar2=s2, op0=op0, op1=op1)
            return S.op(eng, f, reads, writes)

        def tt(eng, out, in0, in1, op, reads, writes):
            return S.op(eng, lambda h, out=out, in0=in0, in1=in1, op=op: h.tensor_tensor(out=out, in0=in0, in1=in1, op=op),
                        reads, writes)

        def stt(eng, out, in0, scalar, in1, op0, op1, reads, writes):
            return S.op(eng, lambda h, out=out, in0=in0, scalar=scalar, in1=in1, op0=op0, op1=op1:
                        h.scalar_tensor_tensor(out=out, in0=in0, scalar=scalar, in1=in1, op0=op0, op1=op1),
                        reads, writes)

        def cp(eng, out, in_, reads, writes):
            if eng == "act":
                return S.op("act", lambda h, out=out, in_=in_: h.activation(out=out, in_=in_, func=AF.Copy), reads, writes)
            return S.op(eng, lambda h, out=out, in_=in_: h.tensor_copy(out=out, in_=in_), reads, writes)

        def memset(eng, ap, val, writes):
            return S.op(eng, lambda h, ap=ap, val=val: h.memset(ap, val), (), writes)

        def dma(eng, out, in_, reads, writes, sem):
            return S.op(eng, lambda h, out=out, in_=in_: h.dma_start(out=out, in_=in_), reads, writes, dma_sem=sem)

        def wk(name):
            if CAT[name][0] < NF32:
                return [("W", name, i) for i in range((CAT[name][1] + 2047) // 2048)]
            return [("W", name)]

        def wb_view(name):
            off, n, _ = CAT[name]
            return wb_d[:, off:off + n]

        dma("sp", gpost[:].rearrange("p a b -> p (a b)"), sm_d["gpost"][:, :], (), ["gpost"], "c_gpost")
        dma("sp", gpre[:].rearrange("p a b -> p (a b)"), sm_d["gpre"][:, :], (), ["gpre"], "c_gpre")
        dma("sp", glng[:], sm_d["glng"][:, :], (), ["glng"], "c_glng")
        dma("sp", dcol[:], sm_d["dcol"][:, :], (), ["dcol"], "c_dcol")
        memset("pool", ident_f[:], 0.0, ["ident_f0"])
        S.op("pool", lambda h: h.affine_select(out=ident_f[:], in_=ident_f[:], pattern=[[-1, 128]],
                                               compare_op=ALU.not_equal, fill=1.0, base=0, channel_multiplier=1),
             ["ident_f0"], ["ident_f"])
        cp("dve", ident_bf[:], ident_f[:], ["ident_f"], ["ident"])
        cp("dve", swap_f[:, 0:64], ident_f[:, 64:128], ["ident_f"], ["swap_a"])
        cp("dve", swap_f[:, 64:128], ident_f[:, 0:64], ["ident_f"], ["swap_b"])
        cp("dve", swap_bf[:], swap_f[:], ["swap_a", "swap_b"], ["swap_bf"])
        memset("pool", ones_bf[:], 1.0, ["ones_bf"])
        memset("pool", ones_f[:], 1.0, ["ones_f"])
        memset("pool", epsc[:], EPS, ["epsc"])
        memset("pool", mhalf[:], -0.5, ["mhalf"])
        memset("pool", half[0:64, 0:1], 1.0, ["half_a"])
        memset("pool", half[64:128, 0:1], 0.0, ["half_b"])
        memset("pool", half[0:64, 1:2], 0.0, ["half_c"])
        memset("pool", half[64:128, 1:2], 1.0, ["half_d"])
        HK = ["half_a", "half_b", "half_c", "half_d"]
        tt("dve", half[:, 2:3], half[:, 0:1], half[:, 1:2], ALU.subtract, HK, ["half_s"])
        ts("dve", half[:, 3:4], half[:, 0:1], -1.0, None, ALU.mult, None, HK, ["half_n"])
        HK = HK + ["half_s", "half_n"]
        LO, HI, SGN, NLO = half[:, 0:1], half[:, 1:2], half[:, 2:3], half[:, 3:4]
        for e in EALL:
            memset("pool", e[:], 0.0, ["eall%d" % EALL.index(e)])

        NST = 4
        STG_F = [Buf(8 * i, 8 * KB, F32, [2048]) for i in range(NST)]
        STG_B = [Buf(32 + 4 * i, 4 * KB, BF16, [2048]) for i in range(NST)]
        conv_engs = ["act", "dve"]
        pieces = []
        for nm, (off, n, fold) in CAT.items():
            if off >= NF32:
                continue
            for p0 in range(0, n, 2048):
                pieces.append((nm, off + p0, min(2048, n - p0), fold, p0 // 512))
        ci = [0]

        def conv_load(i):
            nm, off, n, fold, kc0 = pieces[i]
            sf = STG_F[i % NST]
            dma("sp", sf.v[:, 0:n], wf_d[:, off:off + n], (), sf.k(), "stgf%d" % (i % NST))

        def conv_rest(i):
            nm, off, n, fold, kc0 = pieces[i]
            sf, sb_ = STG_F[i % NST], STG_B[i % NST]
            for c in range(n // 512):
                eng = conv_engs[ci[0] % 2]
                ci[0] += 1
                o_ = sb_.v[:, c * 512:(c + 1) * 512]
                i_ = sf.v[:, c * 512:(c + 1) * 512]
                if fold is None:
                    cp(eng, o_, i_, sf.k(), sb_.k())
                elif fold == "half":
                    if eng == "act":
                        act(o_, i_, AF.Copy, sf.k(), sb_.k(), scale=0.5)
                    else:
                        ts(eng, o_, i_, 0.5, None, ALU.mult, None, sf.k(), sb_.k())
                else:
                    sc = gpre[:, fold, kc0 + c:kc0 + c + 1]
                    if eng == "act":
                        act(o_, i_, AF.Copy, sf.k() + ["gpre"], sb_.k(), scale=sc)
                    else:
                        ts(eng, o_, i_, sc, None, ALU.mult, None, sf.k() + ["gpre"], sb_.k())
            dma("sp", wb_d[:, off:off + n], sb_.v[:, 0:n], sb_.k(), [("W", nm, kc0 // 4)], "stgb%d" % (i % NST))

        def gen_conv():
            conv_load(0)
            conv_load(1)
            conv_load(2)
            for i in range(len(pieces)):
                if i + 3 < len(pieces):
                    conv_load(i + 3)
                conv_rest(i)
                yield


        def gen_s5():
            WSF = Buf(56, 4 * KB, F32, [8, 128])
            dma("sp", WSF.v.rearrange("p a b -> p (a b)"), sm_d["wst"][:, :], (), WSF.k(), "c_wst")
            for g in range(8):
                S.op("pool", lambda h, g=g: h.affine_select(out=WSF.v[:, g, :], in_=WSF.v[:, g, :], pattern=[[1, 128]],
                                                            compare_op=ALU.is_ge, fill=0.0, base=0, channel_multiplier=-1),
                     WSF.k(), WSF.k())
            cp("dve", wst_bf[:], WSF.v, WSF.k(), ["wst_bf"])
            R2Lb = Buf(48, 4 * KB, F32, [1024])
            R2Rb = Buf(52, 4 * KB, F32, [1024])
            R2L, R2R = R2Lb.v, R2Rb.v
            memset("pool", R2L[0:2, :], 1.0, R2Lb.k())
            dma("sp", R2L[0:1, :], sm_d["rows"][0:1, :], R2Lb.k(), R2Lb.k(), "c_r2l")
            dma("sp", R2R[1:2, :], sm_d["rows"][1:2, :], (), R2Rb.k(), "c_r2r")
            b0 = bank(2)
            for g in range(8):
                mm(PS[0:1, 512 * b0 + 128 * g:512 * b0 + 128 * g + 128], ones_f[:, 0:1], WSF.v[:, g, :], True, True,
                   WSF.k() + ["ones_f"], psk(b0 + g // 4))
            cp("dve", R2R[0:1, :], PS[0:1, 512 * b0:512 * b0 + 1024], psk(b0, 2) + R2Rb.k(), R2Rb.k())
            b1 = bank(2)
            for g in range(8):
                mm(PS[:, 512 * b1 + 128 * g:512 * b1 + 128 * g + 128], R2L[0:2, 128 * g:128 * g + 128],
                   R2R[0:2, 128 * g:128 * g + 128], True, True, R2Lb.k() + R2Rb.k(), psk(b1 + g // 4))
            cp("dve", tb[:].rearrange("p a b -> p (a b)"), PS[:, 512 * b1:512 * b1 + 1024], psk(b1, 2), ["tb"])
            yield

            s5 = {}
            soff = [60]

            def salloc(name, nbytes, dt, tail):
                b = Buf(soff[0], nbytes, dt, tail)
                soff[0] += (nbytes + KB - 1) // KB
                s5[name] = b
                return b

            LAM = salloc("lam", 96 * 4, F32, [3, 32])
            B12 = salloc("b12", 4 * KB, F32, [2, 32, 16])
            C12 = salloc("c12", 4 * KB, F32, [2, 32, 16])
            KG = salloc("kg", 40 * 128, F32, [40, 32])
            PW = salloc("pw", 14 * 2 * 128, F32, [14, 2, 32])
            DER = salloc("der", 14 * 4 * 128, F32, [14, 4, 32])
            YB = [salloc("yb%d" % i, 512, F32, [128]) for i in range(2)]
            MASK = salloc("mask", 512, F32, [128])
            BBP = salloc("bbp", 16 * KB, F32, [8, 32, 16])
            WCF = salloc("wcf", 2 * KB, F32, [32, 16])
            WCP = salloc("wcp", 16 * KB, F32, [32, 128])
            YB += [salloc("yb%d" % i, 512, F32, [128]) for i in range(2, 8)]
            T1 = [salloc("t1a", 512, F32, [128]), salloc("t1b", 512, F32, [128])]
            ATST = [salloc("atst0", 8 * KB, BF16, [32, 128])]
            ATST.append(ATST[0])
            WAST = Buf(WCP.off // KB, 16 * KB, BF16, [4, 2, 8, 128])
            WC2ST = salloc("wc2st", 8 * KB, BF16, [32, 8, 16])
            KDST = salloc("kdst", 8 * KB, BF16, [4, 8, 128])
            TMP3 = salloc("tmp3", 2 * KB, F32, [32, 16])
            assert soff[0] <= ARENA_KB, soff[0]

            dma("sp", LAM.v.rearrange("p a b -> p (a b)"), sm_d["lam"][:, :], (), LAM.k(), "c_lam")
            dma("sp", B12.v.rearrange("p a b c -> p (a b c)"), sm_d["b12"][:, :], (), B12.k(), "c_b12")
            dma("sp", C12.v.rearrange("p a b c -> p (a b c)"), sm_d["c12"][:, :], (), C12.k(), "c_c12")

            kgi = [0]

            def kgt():
                i = kgi[0]
                kgi[0] += 1
                assert i < 40
                return KG.v[:, i, :]

            KK = KG.k() + LAM.k() + PW.k() + DER.k()
            lr_, li_, ls_ = LAM.v[:, 0, :], LAM.v[:, 1, :], LAM.v[:, 2, :]

            def k_tt(out, a, b, op):
                tt("dve", out, a, b, op, KK + HK, KK)

            def k_ts(out, a, s1, s2, op0, op1=None):
                ts("dve", out, a, s1, s2, op0, op1, KK + HK, KK)

            def k_act(out, a, func, scale=None):
                act(out, a, func, KK, KK, scale=scale)

            step = kgt(); k_act(step, ls_, AF.Exp)
            lrs = kgt(); k_tt(lrs, lr_, step, ALU.mult)
            lis = kgt(); k_tt(lis, li_, step, ALU.mult)
            mag = kgt(); k_act(mag, lrs, AF.Exp)

            def sincos(shift):
                t = kgt(); k_ts(t, lis, 1.0 / (2 * np.pi), shift, ALU.mult, ALU.add)
                ti = s5["tmp3"].v[:, 0:2, :].rearrange("p a b -> p (a b)").bitcast(I32)
                cp("dve", ti, t, KK, KK + TMP3.k())
                tf = kgt(); cp("dve", tf, ti, KK + TMP3.k(), KK)
                fr = kgt(); k_tt(fr, t, tf, ALU.subtract)
                g1 = kgt(); k_ts(g1, fr, 0.5, None, ALU.is_gt)
                fr2 = kgt(); k_tt(fr2, fr, g1, ALU.subtract)
                g2 = kgt(); k_ts(g2, fr2, -0.5, None, ALU.is_lt)
                fr3 = kgt(); k_tt(fr3, fr2, g2, ALU.add)
                fr4 = kgt(); k_ts(fr4, fr3, -0.49999994, 0.49999994, ALU.max, ALU.min)
                sv = kgt(); k_act(sv, fr4, AF.Sin, scale=float(2 * np.pi))
                return sv

            sinv = sincos(0.0)
            yield
            cosv = sincos(0.25)
            yield
            a_re, a_im = PW.v[:, 1, 0, :], PW.v[:, 1, 1, :]
            k_tt(a_re, mag, cosv, ALU.mult)
            k_tt(a_im, mag, sinv, ALU.mult)
            S.op("pool", lambda h: h.memset(PW.v[:, 0, 0, :], 1.0), (), KK)
            S.op("pool", lambda h: h.memset(PW.v[:, 0, 1, :], 0.0), (), KK)
            t1_, t2_ = kgt(), kgt()

            def cmul(dst, xr, xi, yr, yi):
                k_tt(t1_, xr, yr, ALU.mult)
                k_tt(t2_, xi, yi, ALU.mult)
                k_tt(dst[0], t1_, t2_, ALU.subtract)
                k_tt(t1_, xr, yi, ALU.mult)
                k_tt(t2_, xi, yr, ALU.mult)
                k_tt(dst[1], t1_, t2_, ALU.add)

            def pwv(n):
                return PW.v[:, n, 0, :], PW.v[:, n, 1, :]

            for n in range(2, 9):
                cmul(pwv(n), *pwv(n - 1), a_re, a_im)
                yield
            cmul(pwv(9), *pwv(8), *pwv(8))
            for n in range(10, 14):
                cmul(pwv(n), *pwv(n - 1), *pwv(n - 1))
                yield
            den = kgt(); k_tt(t1_, lr_, lr_, ALU.mult); k_tt(t2_, li_, li_, ALU.mult); k_tt(den, t1_, t2_, ALU.add)
            rden = kgt(); S.op("dve", lambda h: h.reciprocal(out=rden, in_=den), KK, KK)
            nr = kgt(); k_ts(nr, a_re, -1.0, None, ALU.add)
            co_re, co_im = kgt(), kgt()
            k_tt(t1_, nr, lr_, ALU.mult); k_tt(t2_, a_im, li_, ALU.mult); k_tt(co_re, t1_, t2_, ALU.add)
            k_tt(co_re, co_re, rden, ALU.mult)
            k_tt(t1_, a_im, lr_, ALU.mult); k_tt(t2_, nr, li_, ALU.mult); k_tt(co_im, t1_, t2_, ALU.subtract)
            k_tt(co_im, co_im, rden, ALU.mult)
            Z1, Z2 = kgt(), kgt()
            k_ts(t1_, co_im, HI, None, ALU.mult)
            stt("dve", Z1, co_re, LO, t1_, ALU.mult, ALU.add, KK + HK, KK)
            k_ts(t1_, co_re, HI, None, ALU.mult)
            stt("dve", Z2, co_im, NLO, t1_, ALU.mult, ALU.add, KK + HK, KK)
            for n in range(14):
                pr, pi_ = pwv(n)
                k_ts(DER.v[:, n, 0, :], pi_, SGN, None, ALU.mult)
                k_ts(t1_, pi_, HI, None, ALU.mult)
                stt("dve", DER.v[:, n, 1, :], pr, LO, t1_, ALU.mult, ALU.subtract, KK + HK, KK)
                k_ts(t1_, pr, HI, None, ALU.mult)
                stt("dve", DER.v[:, n, 2, :], pi_, NLO, t1_, ALU.mult, ALU.subtract, KK + HK, KK)
                yield

            def bc16(v):
                return v.unsqueeze(2).to_broadcast([128, 32, 16])

            XN = BBP
            cr_, ci_, z1n, z2n = kgt(), kgt(), kgt(), kgt()
            for n in range(8):
                if n == 0:
                    za, zb = Z1, Z2
                else:
                    cmul((cr_, ci_), *pwv(n), co_re, co_im)
                    k_ts(t1_, ci_, HI, None, ALU.mult)
                    stt("dve", z1n, cr_, LO, t1_, ALU.mult, ALU.add, KK + HK, KK)
                    k_ts(t1_, cr_, HI, None, ALU.mult)
                    stt("dve", z2n, ci_, NLO, t1_, ALU.mult, ALU.add, KK + HK, KK)
                    za, zb = z1n, z2n
                tt("dve", TMP3.v, B12.v[:, 0, :, :], bc16(za), ALU.mult, KK + B12.k() + TMP3.k(), TMP3.k())
                tt("dve", XN.v[:, n, :, :], B12.v[:, 1, :, :], bc16(zb), ALU.mult, KK + B12.k(), XN.k())
                tt("dve", XN.v[:, n, :, :], XN.v[:, n, :, :], TMP3.v, ALU.add, XN.k() + TMP3.k(), XN.k())
                yield
            MK3 = MASK.v.rearrange("p (a b) -> p a b", b=16)
            memset("pool", MASK.v, 1.0, MASK.k())
            S.op("pool", lambda h: h.affine_select(out=MK3, in_=MK3, pattern=[[-16, 8], [0, 16]], compare_op=ALU.is_ge,
                                                   fill=0.0, base=0, channel_multiplier=1), MASK.k(), MASK.k())
            S.op("pool", lambda h: h.affine_select(out=MK3, in_=MK3, pattern=[[16, 8], [0, 16]], compare_op=ALU.is_ge,
                                                   fill=0.0, base=15, channel_multiplier=-1), MASK.k(), MASK.k())
            for n in range(9):
                tt("dve", TMP3.v, C12.v[:, 0, :, :], bc16(DER.v[:, n, 1, :]), ALU.mult, KK + C12.k() + TMP3.k(), TMP3.k())
                tt("dve", WCF.v, C12.v[:, 1, :, :], bc16(DER.v[:, n, 2, :]), ALU.mult, KK + C12.k() + WCF.k(), WCF.k())
                tt("dve", WCF.v, WCF.v, TMP3.v, ALU.add, WCF.k() + TMP3.k(), WCF.k())
                yield
                if n >= 1:
                    for q in range(4):
                        for gq in range(2):
                            s0 = q * 8 + gq
                            g0 = 2 * q + gq
                            cp("dve",
                               WC2ST.v[:, s0:s0 + 7:2, n - 1, :], WCF.v[:, g0::8, :],
                               WCF.k() + WC2ST.k(), WC2ST.k())
                if n <= 7:
                    for ct in range(4):
                        bq = bank()
                        mm(PS[:, 512 * bq:512 * bq + 128], XN.v[:, 0, 8 * ct:8 * ct + 8, :].rearrange("p a b -> p (a b)"),
                           WCF.v[:, 8 * ct:8 * ct + 8, :].rearrange("p a b -> p (a b)"), True, True, XN.k() + WCF.k(), psk(bq))
                        if n == 0:
                            tk = T1[ct % 2]
                            tt("dve", tk.v, PS[:, 512 * bq:512 * bq + 128], MASK.v, ALU.mult, psk(bq) + MASK.k() + tk.k(), tk.k())
                            stt("dve", KDST.v[:, ct, n, :], ident_f[:], dcol[:, ct:ct + 1], tk.v,
                                ALU.mult, ALU.add, tk.k() + ["ident_f", "dcol"], KDST.k())
                        else:
                            tt("dve", KDST.v[:, ct, n, :], PS[:, 512 * bq:512 * bq + 128], MASK.v, ALU.mult,
                               psk(bq) + MASK.k(), KDST.k())
                        yield
            dma("sp", wb_view("kd"), KDST.v.rearrange("p a b c -> p (a b c)"), KDST.k(), wk("kd"), "s_kd")
            dma("sp", wb_view("wc"), WC2ST.v.rearrange("p a b c -> p (a b c)"), WC2ST.k(), wk("wc"), "s_wc0")
            for yb in YB:
                memset("pool", yb.v, 0.0, yb.k())
            for ct in range(4):
                for gq in range(2):
                    for jb in range(2):
                        bq = bank()
                        for jj in range(4):
                            j = 4 * jb + jj
                            yb = YB[4 * gq + jj]
                            cp("dve", yb.v.rearrange("p (q c) -> p q c", c=32)[:, :, 16 * gq:16 * gq + 16],
                               XN.v[:, 7 - j, ct * 8 + gq:ct * 8 + gq + 7:2, :], XN.k() + yb.k(), yb.k())
                            tr(PS[:, 512 * bq + 128 * jj:512 * bq + 128 * jj + 128], yb.v, yb.k() + ["ident_f"], psk(bq),
                               ident=ident_f[:])
                        cp("act", WAST.v[:, ct, gq, 4 * jb:4 * jb + 4, :].rearrange("p a b -> p (a b)"), psv(bq), psk(bq), WAST.k())
                        yield
            dma("sp", wb_view("wa0"), WAST.v[:, 0:2].rearrange("p a b c d -> p (a b c d)"), WAST.k(), wk("wa0"), "s_wa0")
            dma("sp", wb_view("wa1"), WAST.v[:, 2:4].rearrange("p a b c d -> p (a b c d)"), WAST.k(), wk("wa1"), "s_wa1")

            lvl_pw = [8, 9, 10, 11, 12, 13]
            T1B = Buf(WCP.off // KB, 8 * KB, BF16, [32, 128])
            T2B = Buf(WCP.off // KB + 8, 8 * KB, BF16, [32, 128])
            rs_, ss_ = kgt(), kgt()
            idb = ident_bf[:].unsqueeze(1).to_broadcast([128, 32, 128])
            swb = swap_bf[:].unsqueeze(1).to_broadcast([128, 32, 128])
            for m in range(6):
                n = lvl_pw[m]
                st = ATST[0]
                cp("dve", rs_.rearrange("p (q c g) -> p q c g", q=4, c=4, g=2),
                   PW.v[:, n, 0, :].rearrange("p (c q g) -> p q c g", c=4, q=4, g=2), KK, KK)
                cp("dve", ss_.rearrange("p (q c g) -> p q c g", q=4, c=4, g=2),
                   DER.v[:, n, 0, :].rearrange("p (c q g) -> p q c g", c=4, q=4, g=2), KK, KK)
                tt("dve", T1B.v, idb, rs_.unsqueeze(2).to_broadcast([128, 32, 128]), ALU.mult,
                   KK + ["ident"] + T1B.k(), T1B.k())
                tt("dve", T2B.v, swb, ss_.unsqueeze(2).to_broadcast([128, 32, 128]), ALU.mult,
                   KK + ["swap_bf"] + T2B.k(), T2B.k())
                tt("dve", st.v, T1B.v, T2B.v, ALU.add, T1B.k() + T2B.k() + st.k(), st.k())
                dma("sp", wb_view(f"at{m}"), st.v.rearrange("p a b -> p (a b)"), st.k(), wk(f"at{m}"), "s_at%d" % (m % 2))
                yield

        g1, g2 = gen_conv(), gen_s5()
        live = [g1, g2, g2]
        while live:
            for g in list(live):
                try:
                    next(g)
                except StopIteration:
                    live = [x for x in live if x is not g]

        tile_order = (["in0", "in1", "in2", "in3", "in4", "in5", "in6", "wa0", "wa1", "at0", "gm0", "at1", "at2", "gm1", "at3", "at4", "in7", "at5",
                       "kd", "wc", "glu", "bs0", "in8", "bs1",
                       "mo0", "mo1", "q0", "q1", "o0", "o1"] + [f"gu{s}" for s in range(11)]
                      + ["dn00", "dn01", "dn02", "dn10", "dn11", "dn12"])
        if stop_after == "mixer":
            tile_order = tile_order[:tile_order.index("q0")]
        elif stop_after == "attn":
            tile_order = tile_order[:tile_order.index("gu0")]
        seq_order = ["kv0", "kv1", "kv2", "kv3"] + tile_order * (nseq * tps)
        ring_pos = [0]
        ring_loaded = [0]

        def ring_load(k):
            nm = seq_order[k]
            off, n, _ = CAT[nm]
            slot = RING[k % 4]
            dma("sp", slot.v[:, 0:n], wb_d[:, off:off + n], wk(nm), slot.k(), "ring%d" % (k % 4))

        ring_held = {}

        def ring_get(expect, hold=False):
            k = ring_pos[0]
            assert seq_order[k] == expect, (seq_order[k], expect)
            assert k < (min(ring_held.values()) if ring_held else k) + 4, (expect, ring_held)
            while ring_loaded[0] <= k:
                ring_load(ring_loaded[0])
                ring_loaded[0] += 1
            ring_pos[0] += 1
            if hold:
                ring_held[expect] = k
            return RING[k % 4].v, RING[k % 4].k()

        def ring_unhold(name):
            ring_held.pop(name)

        def ring_prefetch():
            lim = (min(ring_held.values()) if ring_held else ring_pos[0]) + 4
            while ring_loaded[0] < min(len(seq_order), lim):
                ring_load(ring_loaded[0])
                ring_loaded[0] += 1

        def rms_rstd(src_ap, src_keys, ncols, junk):
            s = stat_slot(2)
            act(junk.v[:, 0:ncols], src_ap, AF.Square, src_keys + junk.k(), junk.k() + stk(s), accum=stat[:, s:s + 1])
            ts("pool", stat[:, s:s + 1], stat[:, s:s + 1], 1.0 / ncols, EPS, ALU.mult, ALU.add, stk(s), stk(s))
            tt("pool", stat[:, s + 1:s + 2], stat[:, s:s + 1], mhalf[:, 0:1], ALU.pow, stk(s) + ["mhalf"], stk(s + 1))
            return s + 1

        def norm_sub_a(xs, sub, junk, ht):
            r = rms_rstd(xs.v[:, sub, :], xs.k(sub), 1024, junk)
            ts("dve", ht.v, xs.v[:, sub, :], stat[:, r:r + 1], None, ALU.mult, None, xs.k(sub) + stk(r), ht.k())

        def norm_sub_b(sub, ht):
            b = bank()
            for kc in range(8):
                tr(psbf(b)[:, 128 * kc:128 * kc + 128], ht.v[:, 128 * kc:128 * kc + 128], ht.k(), psk(b))
            cp("act" if sub % 2 else "dve", HT.v[:, :, 128 * sub:128 * sub + 128],
               psbf(b).rearrange("p (a b) -> p a b", b=128), psk(b), HT.k())

        def norm_to_hT(xs, hT_cols_fn, nsub, src_fn, junk):
            for sub in range(nsub):
                src, skeys = src_fn(sub)
                r = rms_rstd(src, skeys, 1024, junk)
                ht = HTOK[sub % 2]
                ts("dve", ht.v, src, stat[:, r:r + 1], None, ALU.mult, None, skeys + stk(r), ht.k())
                b = bank()
                for kc in range(8):
                    tr(psbf(b)[:, 128 * kc:128 * kc + 128], ht.v[:, 128 * kc:128 * kc + 128], ht.k(), psk(b))
                dst, dkeys = hT_cols_fn(sub)
                cp("act" if sub % 2 else "dve", dst, psbf(b).rearrange("p (a b) -> p a b", b=128), psk(b), dkeys)

        def post_norm_residual(xs, sub, ps_ap, ps_keys, gi):
            r = rms_rstd(ps_ap, ps_keys, 1024, JUNK)
            tmp = TMPA[sub % 2]
            stt("dve", tmp.v, ps_ap, stat[:, r:r + 1], gpost[:, gi, :], ALU.mult, ALU.mult, ps_keys + stk(r) + ["gpost"], tmp.k())
            tt("dve", xs.v[:, sub, :], xs.v[:, sub, :], tmp.v, ALU.add, xs.k(sub) + tmp.k(), xs.k(sub))

        def fm_proj(slab, skeys, ncol_chunks, col0, rhsbuf, nk, consume):
            sv = slab.rearrange("p (a b) -> p a b", b=512)
            for c in range(ncol_chunks):
                b = bank()
                for kc in range(nk):
                    mm(psv(b), sv[:, kc, col0 + 128 * c:col0 + 128 * c + 128], rhsbuf.v[:, kc, :], kc == 0, kc == nk - 1,
                       skeys + rhsbuf.k(kc), psk(b))
                consume(c, b)

        MEMT = Buf(U0 + 0, nseq * 4 * KB, BF16, [nseq, 8, 256], 4 * KB)
        MTOK = [Buf(U0 + 24, 4 * KB, F32, [1024]), Buf(U0 + 28, 4 * KB, F32, [1024])]
        for sq in range(nseq):
            for mc in range(2):
                mt = MTOK[mc]
                dma("pool", mt.v, mem_d[sq * 256 + mc * 128:sq * 256 + mc * 128 + 128, :], (), mt.k(), "mtok%d" % mc)
                r = rms_rstd(mt.v, mt.k(), 1024, JUNK)
                ht = HTOK[mc]
                ts("dve", ht.v, mt.v, stat[:, r:r + 1], None, ALU.mult, None, mt.k() + stk(r), ht.k())
                b = bank()
                for kc in range(8):
                    tr(psbf(b)[:, 128 * kc:128 * kc + 128], ht.v[:, 128 * kc:128 * kc + 128], ht.k(), psk(b))
                cp("dve", MEMT.v[:, sq, :, 128 * mc:128 * mc + 128], psbf(b).rearrange("p (a b) -> p a b", b=128),
                   psk(b), MEMT.k(sq))
        for s_ in range(4):
            slab, skeys = ring_get(f"kv{s_}")
            sv = slab.rearrange("p (a b) -> p a b", b=512)
            for sq in range(nseq):
                if s_ < 2:
                    for c in range(4):
                        b = bank()
                        for kc in range(8):
                            mm(PS[:, 512 * b:512 * b + 256], sv[:, kc, 128 * c:128 * c + 128], MEMT.v[:, sq, kc, :],
                               kc == 0, kc == 7, skeys + MEMT.k(sq), psk(b))
                        cp("act" if c % 2 else "dve", KT[:, sq, 4 * s_ + c, :], PS[:, 512 * b:512 * b + 256], psk(b),
                           ["KT%d_%d" % (sq, 4 * s_ + c)])
                else:
                    hf = s_ - 2
                    for mc in range(2):
                        b = bank()
                        for kc in range(8):
                            mm(psv(b), MEMT.v[:, sq, kc, 128 * mc:128 * mc + 128], sv[:, kc, :], kc == 0, kc == 7,
                               skeys + MEMT.k(sq), psk(b))
                        cp("act" if mc % 2 else "dve", VV[:, sq, mc, 512 * hf:512 * hf + 512], psv(b), psk(b),
                           ["VV%d_%d_%d" % (sq, mc, hf)])
            ring_prefetch()
        KTK = lambda sq: ["KT%d_%d" % (sq, c) for c in range(8)]
        VVK = lambda sq: ["VV%d_%d_%d" % (sq, mc, hf) for mc in range(2) for hf in range(2)]

        store_ops = []
        ntiles = nseq * tps

        def x_rows(ti):
            sq, i = divmod(ti, tps)
            r0 = sq * SEQ + i * TT
            return r0

        def load_x(ti):
            xs = XS[ti % 2]
            r0 = x_rows(ti)
            dma("pool", xs.v, x_d[r0:r0 + TT, :].rearrange("(s p) d -> p s d", p=128), (), xs.k(), "xs%d" % (ti % 2))

        mixer_norm_done = [False]

        def mixer_norm(tj):
            xj = XS[tj % 2]
            for sub in range(4):
                norm_sub_a(xj, sub, JUNK2, HTOK[sub % 2])
                norm_sub_b(sub, HTOK[sub % 2])

        load_x(0)
        for ti in range(ntiles):
            sq, itile = divmod(ti, tps)
            xs = XS[ti % 2]
            if ti + 1 < ntiles:
                load_x(ti + 1)
            e_prev, e_cur = EALL[ti % 2], EALL[(ti + 1) % 2]
            ek_prev, ek_cur = ["eall%d" % (ti % 2)], ["eall%d" % ((ti + 1) % 2)]
            if itile == 0:
                memset("pool", e_prev[:, :, 64:65], 0.0, ek_prev)

            if not mixer_norm_done[0]:
                mixer_norm(ti)
            mixer_norm_done[0] = False
            for s_ in range(2):
                slab, skeys = ring_get(f"in{s_}")
                fm_proj(slab, skeys, 4, 0, HT, 8,
                        lambda c, b, s_=s_: act(UT.v[:, 4 * s_ + c, :], psv(b), AF.Gelu, psk(b), UT.k(4 * s_ + c)))
                ring_prefetch()
            slab2, sk2 = ring_get("in2")
            slab3, sk3 = ring_get("in3")
            vstat = {}

            def v_mm(sub):
                vt = VT[sub % 2]
                for hf, (slab, skeys) in enumerate(((slab2, sk2), (slab3, sk3))):
                    sv = slab.rearrange("p (a b) -> p a b", b=512)
                    b = bank()
                    for kc in range(8):
                        mm(psv(b), HT.v[:, kc, 128 * sub:128 * sub + 128], sv[:, kc, :], kc == 0, kc == 7,
                           skeys + HT.k(kc), psk(b))
                    act(vt.v[:, 512 * hf:512 * hf + 512], psv(b), AF.Gelu, psk(b), vt.k())
                s = stat_slot(16)
                vstat[sub] = s
                bsv = stat[:, s:s + 12].rearrange("p (a b) -> p a b", b=6)
                for hf in range(2):
                    S.op("dve", lambda h, hf=hf, vt=vt, bsv=bsv: h.bn_stats(out=bsv[:, hf, :], in_=vt.v[:, 512 * hf:512 * hf + 512]),
                         vt.k(), stk(s + 6 * hf, 6))
                S.op("dve", lambda h, s=s, bsv=bsv: h.bn_aggr(out=stat[:, s + 12:s + 14], in_=bsv), stk(s, 12), stk(s + 12, 2))

            def v_ln(sub):
                vt = VT[sub % 2]
                s = vstat[sub]
                ts("pool", stat[:, s + 14:s + 15], stat[:, s + 13:s + 14], EPS, None, ALU.add, None, stk(s + 13), stk(s + 14))
                tt("pool", stat[:, s + 15:s + 16], stat[:, s + 14:s + 15], mhalf[:, 0:1], ALU.pow, stk(s + 14) + ["mhalf"], stk(s + 15))
                vh = VHAT[sub % 2]
                ts("dve", vh.v, vt.v, stat[:, s + 12:s + 13], stat[:, s + 15:s + 16], ALU.subtract, ALU.mult,
                   vt.k() + stk(s + 12) + stk(s + 15), vh.k())

            def gmlp(sub):
                vh = VHAT[sub % 2]
                b = bank(2)
                for g in range(8):
                    mm(PS[:, 512 * b + 128 * g:512 * b + 128 * g + 128], vh.v[:, 128 * g:128 * g + 128], wst_bf[:, g, :],
                       True, True, vh.k() + ["wst_bf"], psk(b + g // 4))
                tmp = TMPA[sub % 2]
                for g in range(8):
                    stt("dve", tmp.v[:, 128 * g:128 * g + 128], PS[:, 512 * b + 128 * g:512 * b + 128 * g + 128],
                        glng[:, g:g + 1], tb[:, g, :], ALU.mult, ALU.add, psk(b + g // 4) + ["glng", "tb"], tmp.k())
                tt("dve", YGM.v[:, :, 128 * sub:128 * sub + 128], tmp.v.rearrange("p (a b) -> p a b", b=128),
                   UT.v[:, :, 128 * sub:128 * sub + 128], ALU.mult, tmp.k() + UT.k(), YGM.k())

            def fm_chunk(slab, skeys, c, rhsbuf, nk, consume, t0=0, tn=512, b=None):
                sv = slab.rearrange("p (a b) -> p a b", b=512)
                b = bank() if b is None else b
                for kc in range(nk):
                    mm(PS[:, 512 * b:512 * b + tn], sv[:, kc, 128 * c:128 * c + 128], rhsbuf.v[:, kc, t0:t0 + tn],
                       kc == 0, kc == nk - 1, skeys + rhsbuf.k(kc), psk(b))
                consume(c, b)

            def z_cons(c, b):
                cp("act", ZS5.v[:, c, :], psv(b), psk(b), ZS5.k(c))

            def ga_cons(s_):
                return lambda c, b: act(SGA.v[:, 4 * s_ + c, :], psv(b), AF.Tanh, psk(b), SGA.k(4 * s_ + c), scale=0.5)

            v_mm(0)
            v_mm(1)
            slab4, sk4 = ring_get("in4")
            fm_chunk(slab4, sk4, 0, HT, 8, z_cons)
            fm_chunk(slab4, sk4, 1, HT, 8, z_cons)
            v_ln(0)
            gmlp(0)
            v_mm(2)
            fm_chunk(slab4, sk4, 2, HT, 8, z_cons)
            fm_chunk(slab4, sk4, 3, HT, 8, z_cons)
            slab5, sk5 = ring_get("in5")
            v_ln(1)
            gmlp(1)
            v_mm(3)
            for c in range(4):
                fm_chunk(slab5, sk5, c, HT, 8, ga_cons(0))
            v_ln(2)
            gmlp(2)
            slab6, sk6 = ring_get("in6")
            fm_chunk(slab6, sk6, 0, HT, 8, ga_cons(1))
            fm_chunk(slab6, sk6, 1, HT, 8, ga_cons(1))
            v_ln(3)
            gmlp(3)
            fm_chunk(slab6, sk6, 2, HT, 8, ga_cons(1))
            fm_chunk(slab6, sk6, 3, HT, 8, ga_cons(1))
            ring_prefetch()

            for c in range(4):
                cp("dve", ZS5J.v[:, c, :].rearrange("p (j t) -> p j t", j=8), ZS5.v[:, c, :].rearrange("p (t j) -> p j t", j=8),
                   ZS5.k(c), ZS5J.k(c))
            wa = []
            for s_ in range(2):
                slab, skeys = ring_get(f"wa{s_}")
                wa.append((slab.rearrange("p (a b c d) -> p a b c d", b=2, c=8, d=128), skeys))
            bx = bank(4)
            assert bx % 4 == 0
            for r_ in range(8):
                ct, gq = divmod(r_, 2)
                wv, wkeys = wa[ct // 2]
                for j in range(8):
                    for q in range(4):
                        o_ = PS[:, 512 * (bx + q) + 64 * r_:512 * (bx + q) + 64 * r_ + 64]
                        mm(o_, wv[32 * q:32 * q + 32, ct % 2, gq, j, :], ZS5J.v[32 * q:32 * q + 32, ct, 64 * j:64 * j + 64],
                           j == 0, j == 7, wkeys + ZS5J.k(ct), psk(bx + q), tp=(32 * q, 0))
            for q in range(4):
                cp("act" if q % 2 else "dve", XA.v[:, 8 * q:8 * q + 8, :], psv(bx + q).rearrange("p (a b) -> p a b", b=64),
                   psk(bx + q), XA.k(8 * q, 8))
            ring_prefetch()
            at0, at0k = ring_get("at0")
            at0v = at0.rearrange("p (a b) -> p a b", b=128)
            bc_ = bank()
            for sg in range(32):
                mm(PS[:, 512 * bc_ + sg:512 * bc_ + sg + 1], at0v[:, sg, :], e_prev[:, sg, 64:65], True, True,
                   at0k + ek_prev, psk(bc_))
            tt("dve", XA.v[:, :, 0], XA.v[:, :, 0], PS[:, 512 * bc_:512 * bc_ + 32], ALU.add, XA.k() + psk(bc_), XA.k())
            cp("pool", e_cur[:, :, 0:1], e_prev[:, :, 64:65], ek_prev, ek_cur)
            fb = (bx + 4) % 8
            fctr = [0]
            pa_slab = {}

            def pa_chunk(c8):
                s_, c = divmod(c8, 4)
                slab, skeys = pa_slab[s_]
                bsel = fb + fctr[0] % 4
                fctr[0] += 1
                fm_chunk(slab, skeys, c, YGM, 8,
                         lambda c, b, s_=s_: stt("dve", MRG.v[:, 4 * s_ + c, :], SGA.v[:, 4 * s_ + c, :], 1.0, psv(b), ALU.add, ALU.mult,
                                                 psk(b) + SGA.k(4 * s_ + c), MRG.k(4 * s_ + c)), b=bsel)

            for m in range(6):
                if m == 0:
                    atv, atk = at0v, at0k
                    pa_slab[0] = ring_get("gm0", hold=True)
                else:
                    a_, atk = ring_get(f"at{m}")
                    atv = a_.rearrange("p (a b) -> p a b", b=128)
                    if m == 2:
                        pa_slab[1] = ring_get("gm1", hold=True)
                    if m == 4:
                        gb_slab = ring_get("in7", hold=True)
                sh = 1 << m
                for q in range(4):
                    b = bx + q
                    for r_ in range(8):
                        sg = 8 * q + r_
                        base = 512 * b + 64 * r_
                        mm(PS[:, base + sh:base + 64], atv[:, sg, :], XA.v[:, sg, 0:64 - sh], True, True,
                           atk + XA.k(sg), psk(b))
                    tt("dve", XA.v[:, 8 * q:8 * q + 8, sh:64], XA.v[:, 8 * q:8 * q + 8, sh:64],
                       psv(b).rearrange("p (a b) -> p a b", b=64)[:, :, sh:64], ALU.add, psk(b) + XA.k(8 * q, 8), XA.k(8 * q, 8))
                if m < 4:
                    pa_chunk(2 * m)
                    pa_chunk(2 * m + 1)
                if m >= 4:
                    for c in (2 * (m - 4), 2 * (m - 4) + 1):
                        bsel = fb + fctr[0] % 4
                        fctr[0] += 1
                        fm_chunk(gb_slab[0], gb_slab[1], c, HT, 8,
                                 lambda c, b: act(SGA.v[:, c, :], psv(b), AF.Tanh, psk(b), SGA.k(c), scale=0.5), b=bsel)
                    if m == 5:
                        ring_unhold("in7")
                if m == 1:
                    ring_unhold("gm0")
                if m == 3:
                    ring_unhold("gm1")
                ring_prefetch()
            cp("act", e_cur[:, 0:16, 1:65], XA.v[:, 0:16, :], XA.k(0, 16), ek_cur)
            cp("dve", e_cur[:, 16:32, 1:65], XA.v[:, 16:32, :], XA.k(16, 16), ek_cur)
            kd, kdk = ring_get("kd")
            kdv = kd.rearrange("p (a b c) -> p a b c", b=8, c=128)
            wcs, wck = ring_get("wc")
            wcv = wcs.rearrange("p (a b) -> p a b", b=128)
            for ct in range(4):
                b = bx + ct
                for d_ in range(8):
                    mm(PS[:, 512 * b + 64 * d_:512 * b + 512], kdv[:, ct, d_, :],
                       ZS5J.v[:, ct, 0:64 * (8 - d_)], d_ == 0, False, kdk + ZS5J.k(ct), psk(b))
            def c_state(ct):
                by = fb + 2 * (ct % 2)
                for g8 in range(8):
                    q, gq = divmod(g8, 2)
                    sg = q * 8 + ct * 2 + gq
                    mm(PS[0:64, 512 * by + 128 * g8:512 * by + 128 * g8 + 128], e_cur[:, sg, 0:64], wcv[:, sg, :], True, True,
                       wck + ek_cur, psk(by + g8 // 4))
                yt = YTOK[ct % 2]
                cp("act" if ct % 2 else "dve", yt.v[0:64, :].rearrange("p (j g h) -> p g j h", j=8, g=8, h=16),
                   PS[0:64, 512 * by:512 * by + 1024].rearrange("p (g j h) -> p g j h", g=8, j=8, h=16),
                   psk(by, 2), yt.k())

            def c_back(ct):
                b = bx + ct
                yt = YTOK[ct % 2]
                for jp in range(8):
                    mm(PS[:, 512 * b + 64 * jp:512 * b + 64 * jp + 64], yt.v[0:64, 128 * jp:128 * jp + 128],
                       ident_bf[0:64, 0:64], False, jp == 7, yt.k() + ["ident"], psk(b))
                act(YGT.v[:, ct, :], psv(b), AF.Gelu, psk(b), YGT.k(ct))

            c_state(0)
            c_state(1)
            c_back(0)
            c_state(2)
            c_back(1)
            c_state(3)
            c_back(2)
            c_back(3)
            ring_prefetch()
            slab, skeys = ring_get("glu")
            fm_proj(slab, skeys, 4, 0, YGT, 4,
                    lambda c, b: act(SGL.v[:, c, :], psv(b), AF.Tanh, psk(b), SGL.k(c), scale=0.5))
            for c in range(4):
                stt("dve", YS5.v[:, c, :], SGL.v[:, c, :], 1.0, YGT.v[:, c, :], ALU.add, ALU.mult, YGT.k(c) + SGL.k(c), YS5.k(c))
            ring_prefetch()
            for s_ in range(2):
                if s_ == 1:
                    gslab, gkeys = ring_get("in8")
                bslab, bkeys = ring_get(f"bs{s_}")
                for c in range(4):
                    if s_ == 1:
                        fm_chunk(gslab, gkeys, c, HT, 8,
                                 lambda c, b: act(SGA.v[:, 4 + c, :], psv(b), AF.Tanh, psk(b), SGA.k(4 + c), scale=0.5))

                    def cons(c, b, s_=s_):
                        m2 = M2[c % 2]
                        stt("dve", m2.v.rearrange("p (t j) -> p t j", j=8), SGA.v[:, 4 * s_ + c, :].rearrange("p (t j) -> p t j", j=8), 1.0,
                            psv(b).rearrange("p (j t) -> p t j", j=8), ALU.add, ALU.mult, psk(b) + SGA.k(4 * s_ + c), m2.k())
                        tt("dve", MRG.v[:, 4 * s_ + c, :], MRG.v[:, 4 * s_ + c, :], m2.v, ALU.add,
                           MRG.k(4 * s_ + c) + m2.k(), MRG.k(4 * s_ + c))
                    fm_chunk(bslab, bkeys, c, YS5, 4, cons)
                ring_prefetch()

            def stage_b(sub):
                ht = HTOK[sub % 2]
                norm_sub_a(xs, sub, JUNK, ht)
                norm_sub_b(sub, ht)

            def tm_out(names, srcbuf, gi, follow=False, nxt=None):
                s0, k0 = ring_get(names[0])
                s1, k1 = ring_get(names[1])
                def m_a(sub):
                    b = bank(2)
                    for hf, (slab, skeys) in enumerate(((s0, k0), (s1, k1))):
                        sv = slab.rearrange("p (a b) -> p a b", b=512)
                        for kc in range(8):
                            mm(psv(b + hf), srcbuf.v[:, kc, 128 * sub:128 * sub + 128], sv[:, kc, :], kc == 0, kc == 7,
                               skeys + srcbuf.k(kc), psk(b + hf))
                    return b

                def a_(sub, b):
                    post_norm_residual(xs, sub, psv(b, 2), psk(b, 2), gi)

                if not follow:
                    for sub in range(4):
                        a_(sub, m_a(sub))
                    ring_prefetch()
                    return
                a_(0, m_a(0))
                a_(1, m_a(1))
                b2 = m_a(2)
                norm_sub_a(xs, 0, JUNK, HTOK4[0])
                a_(2, b2)
                b3 = m_a(3)
                norm_sub_a(xs, 1, JUNK, HTOK4[1])
                a_(3, b3)
                norm_sub_a(xs, 2, JUNK, HTOK4[2])
                norm_sub_a(xs, 3, JUNK, HTOK4[3])
                for sub in range(4):
                    norm_sub_b(sub, HTOK4[sub])
                if nxt is not None:
                    nxt()
                ring_prefetch()

            def q_first():
                slab, skeys = ring_get("q0")
                fm_proj(slab, skeys, 4, 0, HT, 8, lambda c, b: cp("act", QT.v[:, c, :], psv(b), psk(b), QT.k(c)))

            def gu_chunk(slab, skeys, s_, c, t0=0, tn=512):
                sv = slab.rearrange("p (a b) -> p a b", b=512)
                j = 2 * s_ + c
                bg = bank()
                for kc in range(8):
                    mm(PS[:, 512 * bg:512 * bg + tn], sv[:, kc, 128 * c:128 * c + 128], HT.v[:, kc, t0:t0 + tn], kc == 0, kc == 7,
                       skeys + HT.k(kc), psk(bg))
                si = SILU[j % 2]
                act(si.v[:, 0:tn], PS[:, 512 * bg:512 * bg + tn], AF.Silu, psk(bg), si.k())
                bu = bank()
                for kc in range(8):
                    mm(PS[:, 512 * bu:512 * bu + tn], sv[:, kc, 256 + 128 * c:256 + 128 * c + 128], HT.v[:, kc, t0:t0 + tn],
                       kc == 0, kc == 7, skeys + HT.k(kc), psk(bu))
                tt("dve", ACTB.v[:, j, t0:t0 + tn], PS[:, 512 * bu:512 * bu + tn], si.v[:, 0:tn], ALU.mult,
                   psk(bu) + si.k(), ACTB.k(j))

            def gu_first():
                slab, skeys = ring_get("gu0")
                for c in range(2):
                    gu_chunk(slab, skeys, 0, c)

            q_slab, gu_slab = [], []

            tm_out(("mo0", "mo1"), MRG, 0, follow=(stop_after != "mixer"), nxt=(q_first if stop_after != "mixer" else None))

            if stop_after != "mixer":
                slab, skeys = ring_get("q1")
                fm_proj(slab, skeys, 4, 0, HT, 8,
                        lambda c, b: cp("act", QT.v[:, 4 + c, :], psv(b), psk(b), QT.k(4 + c)))
                ring_prefetch()
                for hh in range(4):
                    for mc in range(2):
                        b = bank()
                        for dc in range(2):
                            mm(psv(b), KT[:, sq, 2 * hh + dc, 128 * mc:128 * mc + 128], QT.v[:, 2 * hh + dc, :], dc == 0, dc == 1,
                               KTK(sq) + QT.k(2 * hh + dc), psk(b))
                        act(PT.v[:, 2 * hh + mc, :], psv(b), AF.Exp, psk(b), PT.k(2 * hh + mc), scale=1.0 / 16.0)
                for hh in range(4):
                    b = bank()
                    for mc in range(2):
                        mm(psv(b), ones_bf[:], PT.v[:, 2 * hh + mc, :], mc == 0, mc == 1, ["ones_bf"] + PT.k(2 * hh + mc), psk(b))
                    rs = RS[hh % 2]
                    act(rs.v, psv(b), AF.Ln, psk(b), rs.k())
                    act(rs.v, rs.v, AF.Exp, rs.k(), rs.k(), scale=-1.0)
                    for dc in range(2):
                        b = bank()
                        for mc in range(2):
                            mm(psv(b), VV[:, sq, mc, 256 * hh + 128 * dc:256 * hh + 128 * dc + 128], PT.v[:, 2 * hh + mc, :],
                               mc == 0, mc == 1, VVK(sq) + PT.k(2 * hh + mc), psk(b))
                        tt("dve", OT.v[:, 2 * hh + dc, :], psv(b), rs.v, ALU.mult, psk(b) + rs.k(), OT.k(2 * hh + dc))
                tm_out(("o0", "o1"), OT, 1, follow=(stop_after == "ffn"), nxt=(gu_first if stop_after == "ffn" else None))

            if stop_after == "ffn":
                if ti + 1 < ntiles:
                    xn = XS[(ti + 1) % 2]
                    for sub in range(4):
                        norm_sub_a(xn, sub, JUNK2, HTOK4[sub])
                for s_ in range(1, 11):
                    slab, skeys = ring_get(f"gu{s_}")
                    for c in range(2):
                        gu_chunk(slab, skeys, s_, c)
                    ring_prefetch()
                for hf in range(2):
                    slabs = [ring_get(f"dn{hf}{kg}") for kg in range(3)]
                    for sub in range(4):
                        b = bank()
                        for j in range(22):
                            slab, skeys = slabs[j // 8]
                            sv = slab.rearrange("p (a b) -> p a b", b=512)
                            mm(psv(b), ACTB.v[:, j, 128 * sub:128 * sub + 128], sv[:, j % 8, :], j == 0, j == 21,
                               skeys + ACTB.k(j), psk(b))
                        cp("act" if sub % 2 else "dve", YBUF.v[:, sub, 512 * hf:512 * hf + 512], psv(b), psk(b), YBUF.k(sub))
                    ring_prefetch()
                if ti + 1 < ntiles:
                    for sub in range(4):
                        norm_sub_b(sub, HTOK4[sub])
                    mixer_norm_done[0] = True
                for sub in range(4):
                    post_norm_residual(xs, sub, YBUF.v[:, sub, :], YBUF.k(sub), 2)

            r0 = x_rows(ti)
            store_ops.append(dma("pool", y_d[r0:r0 + TT, :].rearrange("(s p) d -> p s d", p=128), xs.v, xs.k(), (),
                                 "ys%d" % (ti % 2)))

        S.emit(finals=[("pool", i) for i in store_ops])
    return nc


_NC_CACHE = {}


def kernel(**inputs):
    inp = {k: np.asarray(v) for k, v in inputs.items()}
    x = inp["x"]
    mem = inp["mem"]
    B = x.shape[0]
    nseq = B // NCORES
    wf = _host_weights(inp)
    sm = _host_small(inp)
    key = ("full", nseq)
    if key not in _NC_CACHE:
        _NC_CACHE[key] = build(nseq=nseq, tps=SEQ // TT)
    nc = _NC_CACHE[key]
    in_maps = []
    for c in range(NCORES):
        m = {"x": np.ascontiguousarray(x[c * nseq:(c + 1) * nseq].reshape(nseq * SEQ, D)),
             "mem": np.ascontiguousarray(mem[c * nseq:(c + 1) * nseq].reshape(nseq * 256, D)),
             "wf": wf}
        m.update(sm)
        in_maps.append(m)
    res = run_bass_kernel_spmd(nc, in_maps, core_ids=list(range(NCORES)))
    out = np.concatenate([np.asarray(r["y"]).reshape(nseq, SEQ, D) for r in res.results], axis=0)
    return out.astype(np.float32)
```

```python
import contextlib
import numpy as np
import concourse.bass as bass
import concourse.mybir as mybir
from concourse.bass_utils import run_bass_kernel_spmd

F32 = mybir.dt.float32
BF16 = mybir.dt.bfloat16
I32 = mybir.dt.int32
AF = mybir.ActivationFunctionType
ALU = mybir.AluOpType

NCORES = 8
SEQ = 4096
D = 1024
TT = 512
EPS = 1e-6
ENGS = ("pe", "act", "dve", "pool", "sp")


class Sched:
    def __init__(self, nc):
        self.nc = nc
        self.ops = []
        self.last_w = {}
        self.readers = {}
        self.dma_cnt = {}

    def op(self, eng, fn, reads=(), writes=(), dma_sem=None):
        idx = len(self.ops)
        deps = set()
        for k in reads:
            w = self.last_w.get(k)
            if w is not None:
                deps.add(w)
        for k in writes:
            w = self.last_w.get(k)
            if w is not None:
                deps.add(w)
            for r in self.readers.get(k, ()):
                deps.add(r)
        if eng == "pe":
            deps = {d for d in deps if self.ops[d]["eng"] != "pe"}
        o = dict(eng=eng, fn=fn, deps=sorted(deps), pub=False, dma=dma_sem, val=None)
        if dma_sem is not None:
            c = self.dma_cnt.get(dma_sem, 0) + 16
            self.dma_cnt[dma_sem] = c
            o["val"] = c
            o["pub"] = True
        self.ops.append(o)
        for d in deps:
            self.ops[d]["pub"] = True
        for k in reads:
            self.readers.setdefault(k, []).append(idx)
        for k in writes:
            self.last_w[k] = idx
            self.readers[k] = []
        return idx

    def emit(self, finals=()):
        nc = self.nc
        fin = {}
        for (ename, d) in finals:
            fin.setdefault(ename, []).append(d)
            self.ops[d]["pub"] = True
        cnt = {e: 0 for e in ENGS}
        for o in self.ops:
            if o["dma"] is None and o["pub"]:
                cnt[o["eng"]] += 1
                o["val"] = cnt[o["eng"]]
        with contextlib.ExitStack() as es:
            esem = {e: es.enter_context(nc.semaphore("s_" + e)) for e in ENGS}
            dsem = {n: es.enter_context(nc.semaphore("d_" + n)) for n in self.dma_cnt}
            block = es.enter_context(nc.Block())

            def semof(o):
                if o["dma"] is not None:
                    return ("d", o["dma"]), dsem[o["dma"]]
                return ("e", o["eng"]), esem[o["eng"]]

            def run(ename, h):
                known = {}

                def wait(d):
                    p = self.ops[d]
                    key, sh = semof(p)
                    if known.get(key, 0) < p["val"]:
                        h.wait_ge(sh, p["val"])
                        known[key] = p["val"]

                for o in self.ops:
                    if o["eng"] != ename:
                        continue
                    for d in o["deps"]:
                        wait(d)
                    inst = o["fn"](h)
                    if o["pub"]:
                        _, sh = semof(o)
                        inst.then_inc(sh, 16 if o["dma"] is not None else 1)
                for d in fin.get(ename, ()):
                    wait(d)

            @block.tensor
            def _(h):
                run("pe", h)

            @block.scalar
            def _(h):
                run("act", h)

            @block.vector
            def _(h):
                run("dve", h)

            @block.gpsimd
            def _(h):
                run("pool", h)

            @block.sync
            def _(h):
                run("sp", h)


def _slab_catalog():
    cat = {}
    off = 0

    def add(name, n, fold=None):
        nonlocal off
        cat[name] = (off, n, fold)
        off += n

    for s in range(9):
        add(f"in{s}", 4096, 0)
    for s in range(2):
        add(f"gm{s}", 4096)
    for s in range(2):
        add(f"bs{s}", 2048, "half")
    for s in range(2):
        add(f"mo{s}", 4096, "half")
    add("glu", 2048)
    for s in range(2):
        add(f"q{s}", 4096, 1)
    for s in range(4):
        add(f"kv{s}", 4096, 3)
    for s in range(2):
        add(f"o{s}", 4096)
    for s in range(11):
        add(f"gu{s}", 4096, 2)
    for h in range(2):
        for kg in range(3):
            add(f"dn{h}{kg}", 4096 if kg < 2 else 3072)
    nf32 = off
    for m in range(6):
        add(f"at{m}", 4096)
    add("wa0", 4096)
    add("wa1", 4096)
    add("kd", 4096)
    add("wc", 4096)
    return cat, nf32, off


CAT, NF32, NTOT = _slab_catalog()


def _slabify(W, c0, c1):
    K = W.shape[0]
    kc = K // 128
    return np.ascontiguousarray(
        W[:, c0:c1].reshape(kc, 128, c1 - c0).transpose(1, 0, 2)).reshape(128, kc * (c1 - c0))


def _host_weights(inp):
    parts = {}
    w_in = inp["w_in"][0]
    for s in range(9):
        parts[f"in{s}"] = _slabify(w_in, 512 * s, 512 * s + 512)
    for nm, key in (("gm", "w_br_gm"), ("bs", "w_br_s5"), ("mo", "w_mix_out"), ("q", "ca_w_q"),
                    ("o", "ca_w_o")):
        W = inp[key][0]
        for s in range(2):
            parts[f"{nm}{s}"] = _slabify(W, 512 * s, 512 * s + 512)
    parts["glu"] = _slabify(inp["s5_w_glu"][0], 0, 512)
    W = inp["ca_w_kv"][0]
    for s in range(4):
        parts[f"kv{s}"] = _slabify(W, 512 * s, 512 * s + 512)
    W = inp["ffn_w_gu"][0]
    for s in range(11):
        Wc = np.concatenate([W[:, 256 * s:256 * s + 256], W[:, 2816 + 256 * s:2816 + 256 * s + 256]], axis=1)
        parts[f"gu{s}"] = _slabify(Wc, 0, 512)
    W = inp["ffn_w_down"][0]
    for h in range(2):
        for kg in range(3):
            r0 = kg * 1024
            r1 = min(r0 + 1024, 2816)
            parts[f"dn{h}{kg}"] = _slabify(W[r0:r1], 512 * h, 512 * h + 512)
    out = np.empty((128, NF32), np.float32)
    for nm, (off, n, _) in CAT.items():
        if off >= NF32:
            continue
        assert parts[nm].shape == (128, n), (nm, parts[nm].shape, n)
        out[:, off:off + n] = parts[nm]
    return out


def _host_small(inp):
    f = np.float32
    sm = {}
    sm["gpost"] = np.ascontiguousarray(np.broadcast_to(
        np.concatenate([inp["g_mix_post"][0], inp["g_ca_post"][0], inp["g_ffn_post"][0]])[None, :], (128, 3072))).astype(f)
    gp = np.stack([inp["g_mix_pre"][0], inp["g_ca_pre"][0], inp["g_ffn_pre"][0], inp["g_mem"][0]], 0)
    sm["gpre"] = np.ascontiguousarray(gp.reshape(4, 8, 128).transpose(2, 0, 1)).reshape(128, 32).astype(f)
    sm["glng"] = np.ascontiguousarray(inp["gm_ln_g"][0].reshape(8, 128).T).astype(f)
    sm["rows"] = np.ascontiguousarray(np.stack([inp["gm_ln_b"][0], inp["gm_b_s"][0].reshape(1024)], 0)).astype(f)
    sm["wst"] = np.ascontiguousarray(inp["gm_w_s"][0].transpose(2, 0, 1)).reshape(128, 1024).astype(f)
    lr = inp["s5_lam_re"][0].T
    li = inp["s5_lam_im"][0].T
    ls = np.broadcast_to(inp["s5_log_step"][0][None, :], (128, 32))
    sm["lam"] = np.ascontiguousarray(np.concatenate(
        [np.concatenate([lr, lr], 0), np.concatenate([li, li], 0), ls], 1)).astype(f)
    b1 = inp["s5_b_re"][0].transpose(1, 0, 2).reshape(64, 512)
    b2 = inp["s5_b_im"][0].transpose(1, 0, 2).reshape(64, 512)
    sm["b12"] = np.ascontiguousarray(np.concatenate(
        [np.concatenate([b1, b1], 0), np.concatenate([b2, b2], 0)], 1)).astype(f)
    c1 = inp["s5_c_re"][0].transpose(2, 0, 1).reshape(64, 512)
    c2 = inp["s5_c_im"][0].transpose(2, 0, 1).reshape(64, 512)
    sm["c12"] = np.ascontiguousarray(np.concatenate(
        [np.concatenate([c1, c1], 0), np.concatenate([c2, c2], 0)], 1)).astype(f)
    sm["dcol"] = np.ascontiguousarray(inp["s5_d"][0].reshape(4, 128).T).astype(f)
    return sm


SMALL_SHAPES = dict(gpost=[128, 3072], gpre=[128, 32], glng=[128, 8], rows=[2, 1024], wst=[128, 1024],
                    lam=[128, 96], b12=[128, 1024], c12=[128, 1024], dcol=[128, 4])


def sig_g(sig):
    q, r = divmod(sig, 8)
    ct, gq = divmod(r, 2)
    return ct * 8 + 2 * q + gq


def build(nseq=2, tps=8, stop_after="ffn"):
    nc = bass.Bass("TRN2", target_bir_lowering=False)
    ntok = nseq * SEQ
    x_d = nc.dram_tensor("x", [ntok, D], F32, kind="ExternalInput").ap()
    mem_d = nc.dram_tensor("mem", [nseq * 256, D], F32, kind="ExternalInput").ap()
    wf_d = nc.dram_tensor("wf", [128, NF32], F32, kind="ExternalInput").ap()
    sm_d = {k: nc.dram_tensor(k, shp, F32, kind="ExternalInput").ap() for k, shp in SMALL_SHAPES.items()}
    y_d = nc.dram_tensor("y", [ntok, D], F32, kind="ExternalOutput").ap()
    wb_d = nc.dram_tensor("wb", [128, NTOT], BF16, kind="Internal").ap()

    S = Sched(nc)
    es = contextlib.ExitStack()
    with es:
        def sbt(name, shape, dt):
            return es.enter_context(nc.sbuf_tensor("sb_" + name, shape, dt))

        KB = 1024
        ARENA_KB = 156
        A = sbt("arena", [128, ARENA_KB * KB // 2], BF16)
        PS = es.enter_context(nc.psum_tensor("ps", [128, 4096], F32))

        class Buf:
            def __init__(self, off_kb, size_b, dt, shape_tail, chunk_b=None):
                self.off = int(off_kb * KB)
                self.size = size_b
                self.dt = dt
                self.chunk_b = chunk_b or size_b
                v = A[:, self.off // 2:(self.off + size_b) // 2]
                if dt == F32:
                    v = v.bitcast(F32)
                elif dt == I32:
                    v = v.bitcast(I32)
                if len(shape_tail) == 2:
                    v = v.rearrange("p (a b) -> p a b", b=shape_tail[1])
                elif len(shape_tail) == 3:
                    v = v.rearrange("p (a b c) -> p a b c", b=shape_tail[1], c=shape_tail[2])
                elif len(shape_tail) == 4:
                    v = v.rearrange("p (a b c d) -> p a b c d", b=shape_tail[1], c=shape_tail[2], d=shape_tail[3])
                self.v = v

            def k(self, c=None, n=1):
                if c is None:
                    lo, hi = self.off, self.off + self.size
                else:
                    lo = self.off + c * self.chunk_b
                    hi = lo + n * self.chunk_b
                return [("A", p) for p in range(lo // KB, (hi - 1) // KB + 1)]

        def psk(b, n=1):
            return [("ps", i) for i in range(b, b + n)]

        def psv(b, n=1):
            return PS[:, 512 * b:512 * (b + n)]

        def psbf(b):
            return PS[:, 512 * b:512 * (b + 1)].bitcast(BF16)

        bank_ctr = [0]

        def bank(n=1):
            b = bank_ctr[0]
            if n > 1 and b % n:
                b += n - b % n
            if b + n > 8:
                b = 0
            bank_ctr[0] = (b + n) % 8
            return b

        XS = [Buf(0, 16 * KB, F32, [4, 1024], 4 * KB), Buf(16, 16 * KB, F32, [4, 1024], 4 * KB)]
        RING = [Buf(32 + 8 * i, 8 * KB, BF16, [4096]) for i in range(4)]
        U0 = 64
        HT = Buf(144, 8 * KB, BF16, [8, 512], KB)
        HTOK = [Buf(152, 2 * KB, BF16, [1024]), Buf(154, 2 * KB, BF16, [1024])]
        UT = Buf(U0 + 0, 8 * KB, BF16, [8, 512], KB)
        ZS5 = Buf(U0 + 8, 4 * KB, BF16, [4, 512], KB)
        VT = [Buf(U0 + 12, 4 * KB, F32, [1024]), Buf(U0 + 16, 4 * KB, F32, [1024])]
        ZS5J = Buf(U0 + 12, 4 * KB, BF16, [4, 512], KB)
        VHAT = [Buf(U0 + 20, 2 * KB, BF16, [1024]), Buf(U0 + 22, 2 * KB, BF16, [1024])]
        YGM = Buf(U0 + 24, 8 * KB, BF16, [8, 512], KB)
        XA = Buf(U0 + 32, 4 * KB, BF16, [32, 64], 128)
        YTOK = [Buf(U0 + 36, 2 * KB, BF16, [1024]), Buf(U0 + 38, 2 * KB, BF16, [1024])]
        YGT = Buf(U0 + 40, 4 * KB, BF16, [4, 512], KB)
        SGL = Buf(U0 + 44, 4 * KB, BF16, [4, 512], KB)
        YS5 = Buf(U0 + 48, 4 * KB, BF16, [4, 512], KB)
        SGA = Buf(U0 + 52, 8 * KB, BF16, [8, 512], KB)
        MRG = Buf(U0 + 60, 8 * KB, BF16, [8, 512], KB)
        TMPA = [Buf(U0 + 68, 4 * KB, F32, [1024]), Buf(U0 + 72, 4 * KB, F32, [1024])]
        M2 = [Buf(U0 + 76, 2 * KB, F32, [512]), Buf(U0 + 78, 2 * KB, F32, [512])]
        QT = Buf(U0 + 0, 8 * KB, BF16, [8, 512], KB)
        PT = Buf(U0 + 8, 8 * KB, BF16, [8, 512], KB)
        RS = [Buf(U0 + 16, 2 * KB, F32, [512]), Buf(U0 + 18, 2 * KB, F32, [512])]
        OT = Buf(U0 + 24, 8 * KB, BF16, [8, 512], KB)
        ACTB = Buf(U0 + 0, 22 * KB, BF16, [22, 512], KB)
        SILU = [Buf(U0 + 22, KB, BF16, [512]), Buf(U0 + 23, KB, BF16, [512])]
        YBUF = Buf(U0 + 24, 16 * KB, F32, [4, 1024], 4 * KB)
        JUNK = Buf(U0 + 40, 2 * KB, BF16, [1024])
        JUNK2 = Buf(U0 + 42, 2 * KB, BF16, [1024])
        HTOK4 = [Buf(U0 + 44 + 2 * i, 2 * KB, BF16, [1024]) for i in range(4)]

        gpost = sbt("gpost", [128, 3, 1024], F32)
        gpre = sbt("gpre", [128, 4, 8], F32)
        glng = sbt("glng", [128, 8], F32)
        ident_bf = sbt("ident_bf", [128, 128], BF16)
        ident_f = sbt("ident_f", [128, 128], F32)
        swap_f = sbt("swap_f", [128, 128], F32)
        swap_bf = sbt("swap_bf", [128, 128], BF16)
        ones_bf = sbt("ones_bf", [128, 128], BF16)
        ones_f = sbt("ones_f", [128, 1], F32)
        wst_bf = sbt("wst_bf", [128, 8, 128], BF16)
        tb = sbt("tb", [128, 8, 128], F32)
        KT = sbt("KT", [128, nseq, 8, 256], BF16)
        VV = sbt("VV", [128, nseq, 2, 1024], BF16)
        EALL = [sbt("eall0", [128, 32, 65], BF16), sbt("eall1", [128, 32, 65], BF16)]
        stat = sbt("stat", [128, 64], F32)
        dcol = sbt("dcol", [128, 4], F32)
        half = sbt("half", [128, 4], F32)
        epsc = sbt("epsc", [128, 1], F32)
        mhalf = sbt("mhalf", [128, 1], F32)

        stat_ctr = [0]

        def stat_slot(n=1):
            s = stat_ctr[0]
            if s + n > 64:
                s = 0
            stat_ctr[0] = s + n
            return s

        def stk(s, n=1):
            return [("stat", i) for i in range(s, s + n)]

        def mm(out, lhsT, rhs, start, stop, reads, writes, tp=None):
            def f(h, out=out, lhsT=lhsT, rhs=rhs, start=start, stop=stop, tp=tp):
                if tp is None:
                    return h.matmul(out, lhsT=lhsT, rhs=rhs, start=start, stop=stop, skip_group_check=True)
                return h.matmul(out, lhsT=lhsT, rhs=rhs, start=start, stop=stop, tile_position=tp,
                                skip_group_check=True)
            return S.op("pe", f, reads, writes)

        def tr(out, in_, reads, writes, ident=None):
            ident = ident_bf[:] if ident is None else ident
            return S.op("pe", lambda h, out=out, in_=in_, ident=ident: h.transpose(out=out, in_=in_, identity=ident),
                        list(reads) + ["ident"], writes)

        def act(out, in_, func, reads, writes, scale=None, bias=None, accum=None):
            def f(h, out=out, in_=in_, func=func, scale=scale, bias=bias, accum=accum):
                kw = {}
                if scale is not None:
                    kw["scale"] = scale
                if bias is not None:
                    kw["bias"] = bias
                if accum is not None:
                    kw["accum_out"] = accum
                return h.activation(out=out, in_=in_, func=func, **kw)
            return S.op("act", f, reads, writes)

        def ts(eng, out, in0, s1, s2, op0, op1, reads, writes):
            def f(h, out=out, in0=in0, s1=s1, s2=s2, op0=op0, op1=op1):
                if op1 is None:
                    return h.tensor_scalar(out=out, in0=in0, scalar1=s1, scalar2=None, op0=op0)
                return h.tensor_scalar(out=out, in0=in0, scalar1=s1, scalar2=s2, op0=op0, op1=op1)
            return S.op(eng, f, reads, writes)

        def tt(eng, out, in0, in1, op, reads, writes):
            return S.op(eng, lambda h, out=out, in0=in0, in1=in1, op=op: h.tensor_tensor(out=out, in0=in0, in1=in1, op=op),
                        reads, writes)

        def stt(eng, out, in0, scalar, in1, op0, op1, reads, writes):
            return S.op(eng, lambda h, out=out, in0=in0, scalar=scalar, in1=in1, op0=op0, op1=op1:
                        h.scalar_tensor_tensor(out=out, in0=in0, scalar=scalar, in1=in1, op0=op0, op1=op1),
                        reads, writes)

        def cp(eng, out, in_, reads, writes):
            if eng == "act":
                return S.op("act", lambda h, out=out, in_=in_: h.activation(out=out, in_=in_, func=AF.Copy), reads, writes)
            return S.op(eng, lambda h, out=out, in_=in_: h.tensor_copy(out=out, in_=in_), reads, writes)

        def memset(eng, ap, val, writes):
            return S.op(eng, lambda h, ap=ap, val=val: h.memset(ap, val), (), writes)

        def dma(eng, out, in_, reads, writes, sem):
            return S.op(eng, lambda h, out=out, in_=in_: h.dma_start(out=out, in_=in_), reads, writes, dma_sem=sem)

        def wk(name):
            if CAT[name][0] < NF32:
                return [("W", name, i) for i in range((CAT[name][1] + 2047) // 2048)]
            return [("W", name)]

        def wb_view(name):
            off, n, _ = CAT[name]
            return wb_d[:, off:off + n]

        dma("sp", gpost[:].rearrange("p a b -> p (a b)"), sm_d["gpost"][:, :], (), ["gpost"], "c_gpost")
        dma("sp", gpre[:].rearrange("p a b -> p (a b)"), sm_d["gpre"][:, :], (), ["gpre"], "c_gpre")
        dma("sp", glng[:], sm_d["glng"][:, :], (), ["glng"], "c_glng")
        dma("sp", dcol[:], sm_d["dcol"][:, :], (), ["dcol"], "c_dcol")
        memset("pool", ident_f[:], 0.0, ["ident_f0"])
        S.op("pool", lambda h: h.affine_select(out=ident_f[:], in_=ident_f[:], pattern=[[-1, 128]],
                                               compare_op=ALU.not_equal, fill=1.0, base=0, channel_multiplier=1),
             ["ident_f0"], ["ident_f"])
        cp("dve", ident_bf[:], ident_f[:], ["ident_f"], ["ident"])
        cp("dve", swap_f[:, 0:64], ident_f[:, 64:128], ["ident_f"], ["swap_a"])
        cp("dve", swap_f[:, 64:128], ident_f[:, 0:64], ["ident_f"], ["swap_b"])
        cp("dve", swap_bf[:], swap_f[:], ["swap_a", "swap_b"], ["swap_bf"])
        memset("pool", ones_bf[:], 1.0, ["ones_bf"])
        memset("pool", ones_f[:], 1.0, ["ones_f"])
        memset("pool", epsc[:], EPS, ["epsc"])
        memset("pool", mhalf[:], -0.5, ["mhalf"])
        memset("pool", half[0:64, 0:1], 1.0, ["half_a"])
        memset("pool", half[64:128, 0:1], 0.0, ["half_b"])
        memset("pool", half[0:64, 1:2], 0.0, ["half_c"])
        memset("pool", half[64:128, 1:2], 1.0, ["half_d"])
        HK = ["half_a", "half_b", "half_c", "half_d"]
        tt("dve", half[:, 2:3], half[:, 0:1], half[:, 1:2], ALU.subtract, HK, ["half_s"])
        ts("dve", half[:, 3:4], half[:, 0:1], -1.0, None, ALU.mult, None, HK, ["half_n"])
        HK = HK + ["half_s", "half_n"]
        LO, HI, SGN, NLO = half[:, 0:1], half[:, 1:2], half[:, 2:3], half[:, 3:4]
        for e in EALL:
            memset("pool", e[:], 0.0, ["eall%d" % EALL.index(e)])

        NST = 4
        STG_F = [Buf(8 * i, 8 * KB, F32, [2048]) for i in range(NST)]
        STG_B = [Buf(32 + 4 * i, 4 * KB, BF16, [2048]) for i in range(NST)]
        conv_engs = ["act", "dve"]
        pieces = []
        for nm, (off, n, fold) in CAT.items():
            if off >= NF32:
                continue
            for p0 in range(0, n, 2048):
                pieces.append((nm, off + p0, min(2048, n - p0), fold, p0 // 512))
        ci = [0]

        def conv_load(i):
            nm, off, n, fold, kc0 = pieces[i]
            sf = STG_F[i % NST]
            dma("sp", sf.v[:, 0:n], wf_d[:, off:off + n], (), sf.k(), "stgf%d" % (i % NST))

        def conv_rest(i):
            nm, off, n, fold, kc0 = pieces[i]
            sf, sb_ = STG_F[i % NST], STG_B[i % NST]
            for c in range(n // 512):
                eng = conv_engs[ci[0] % 2]
                ci[0] += 1
                o_ = sb_.v[:, c * 512:(c + 1) * 512]
                i_ = sf.v[:, c * 512:(c + 1) * 512]
                if fold is None:
                    cp(eng, o_, i_, sf.k(), sb_.k())
                elif fold == "half":
                    if eng == "act":
                        act(o_, i_, AF.Copy, sf.k(), sb_.k(), scale=0.5)
                    else:
                        ts(eng, o_, i_, 0.5, None, ALU.mult, None, sf.k(), sb_.k())
                else:
                    sc = gpre[:, fold, kc0 + c:kc0 + c + 1]
                    if eng == "act":
                        act(o_, i_, AF.Copy, sf.k() + ["gpre"], sb_.k(), scale=sc)
                    else:
                        ts(eng, o_, i_, sc, None, ALU.mult, None, sf.k() + ["gpre"], sb_.k())
            dma("sp", wb_d[:, off:off + n], sb_.v[:, 0:n], sb_.k(), [("W", nm, kc0 // 4)], "stgb%d" % (i % NST))

        def gen_conv():
            conv_load(0)
            conv_load(1)
            conv_load(2)
            for i in range(len(pieces)):
                if i + 3 < len(pieces):
                    conv_load(i + 3)
                conv_rest(i)
                yield


        def gen_s5():
            WSF = Buf(56, 4 * KB, F32, [8, 128])
            dma("sp", WSF.v.rearrange("p a b -> p (a b)"), sm_d["wst"][:, :], (), WSF.k(), "c_wst")
            for g in range(8):
                S.op("pool", lambda h, g=g: h.affine_select(out=WSF.v[:, g, :], in_=WSF.v[:, g, :], pattern=[[1, 128]],
                                                            compare_op=ALU.is_ge, fill=0.0, base=0, channel_multiplier=-1),
                     WSF.k(), WSF.k())
            cp("dve", wst_bf[:], WSF.v, WSF.k(), ["wst_bf"])
            R2Lb = Buf(48, 4 * KB, F32, [1024])
            R2Rb = Buf(52, 4 * KB, F32, [1024])
            R2L, R2R = R2Lb.v, R2Rb.v
            memset("pool", R2L[0:2, :], 1.0, R2Lb.k())
            dma("sp", R2L[0:1, :], sm_d["rows"][0:1, :], R2Lb.k(), R2Lb.k(), "c_r2l")
            dma("sp", R2R[1:2, :], sm_d["rows"][1:2, :], (), R2Rb.k(), "c_r2r")
            b0 = bank(2)
            for g in range(8):
                mm(PS[0:1, 512 * b0 + 128 * g:512 * b0 + 128 * g + 128], ones_f[:, 0:1], WSF.v[:, g, :], True, True,
                   WSF.k() + ["ones_f"], psk(b0 + g // 4))
            cp("dve", R2R[0:1, :], PS[0:1, 512 * b0:512 * b0 + 1024], psk(b0, 2) + R2Rb.k(), R2Rb.k())
            b1 = bank(2)
            for g in range(8):
                mm(PS[:, 512 * b1 + 128 * g:512 * b1 + 128 * g + 128], R2L[0:2, 128 * g:128 * g + 128],
                   R2R[0:2, 128 * g:128 * g + 128], True, True, R2Lb.k() + R2Rb.k(), psk(b1 + g // 4))
            cp("dve", tb[:].rearrange("p a b -> p (a b)"), PS[:, 512 * b1:512 * b1 + 1024], psk(b1, 2), ["tb"])
            yield

            s5 = {}
            soff = [60]

            def salloc(name, nbytes, dt, tail):
                b = Buf(soff[0], nbytes, dt, tail)
                soff[0] += (nbytes + KB - 1) // KB
                s5[name] = b
                return b

            LAM = salloc("lam", 96 * 4, F32, [3, 32])
            B12 = salloc("b12", 4 * KB, F32, [2, 32, 16])
            C12 = salloc("c12", 4 * KB, F32, [2, 32, 16])
            KG = salloc("kg", 40 * 128, F32, [40, 32])
            PW = salloc("pw", 14 * 2 * 128, F32, [14, 2, 32])
            DER = salloc("der", 14 * 4 * 128, F32, [14, 4, 32])
            YB = [salloc("yb%d" % i, 512, F32, [128]) for i in range(2)]
            MASK = salloc("mask", 512, F32, [128])
            BBP = salloc("bbp", 16 * KB, F32, [8, 32, 16])
            WCF = salloc("wcf", 2 * KB, F32, [32, 16])
            WCP = salloc("wcp", 16 * KB, F32, [32, 128])
            YB += [salloc("yb%d" % i, 512, F32, [128]) for i in range(2, 8)]
            T1 = [salloc("t1a", 512, F32, [128]), salloc("t1b", 512, F32, [128])]
            ATST = [salloc("atst0", 8 * KB, BF16, [32, 128])]
            ATST.append(ATST[0])
            WAST = Buf(WCP.off // KB, 16 * KB, BF16, [4, 2, 8, 128])
            WC2ST = salloc("wc2st", 8 * KB, BF16, [32, 8, 16])
            KDST = salloc("kdst", 8 * KB, BF16, [4, 8, 128])
            TMP3 = salloc("tmp3", 2 * KB, F32, [32, 16])
            assert soff[0] <= ARENA_KB, soff[0]

            dma("sp", LAM.v.rearrange("p a b -> p (a b)"), sm_d["lam"][:, :], (), LAM.k(), "c_lam")
            dma("sp", B12.v.rearrange("p a b c -> p (a b c)"), sm_d["b12"][:, :], (), B12.k(), "c_b12")
            dma("sp", C12.v.rearrange("p a b c -> p (a b c)"), sm_d["c12"][:, :], (), C12.k(), "c_c12")

            kgi = [0]

            def kgt():
                i = kgi[0]
                kgi[0] += 1
                assert i < 40
                return KG.v[:, i, :]

            KK = KG.k() + LAM.k() + PW.k() + DER.k()
            lr_, li_, ls_ = LAM.v[:, 0, :], LAM.v[:, 1, :], LAM.v[:, 2, :]

            def k_tt(out, a, b, op):
                tt("dve", out, a, b, op, KK + HK, KK)

            def k_ts(out, a, s1, s2, op0, op1=None):
                ts("dve", out, a, s1, s2, op0, op1, KK + HK, KK)

            def k_act(out, a, func, scale=None):
                act(out, a, func, KK, KK, scale=scale)

            step = kgt(); k_act(step, ls_, AF.Exp)
            lrs = kgt(); k_tt(lrs, lr_, step, ALU.mult)
            lis = kgt(); k_tt(lis, li_, step, ALU.mult)
            mag = kgt(); k_act(mag, lrs, AF.Exp)

            def sincos(shift):
                t = kgt(); k_ts(t, lis, 1.0 / (2 * np.pi), shift, ALU.mult, ALU.add)
                ti = s5["tmp3"].v[:, 0:2, :].rearrange("p a b -> p (a b)").bitcast(I32)
                cp("dve", ti, t, KK, KK + TMP3.k())
                tf = kgt(); cp("dve", tf, ti, KK + TMP3.k(), KK)
                fr = kgt(); k_tt(fr, t, tf, ALU.subtract)
                g1 = kgt(); k_ts(g1, fr, 0.5, None, ALU.is_gt)
                fr2 = kgt(); k_tt(fr2, fr, g1, ALU.subtract)
                g2 = kgt(); k_ts(g2, fr2, -0.5, None, ALU.is_lt)
                fr3 = kgt(); k_tt(fr3, fr2, g2, ALU.add)
                fr4 = kgt(); k_ts(fr4, fr3, -0.49999994, 0.49999994, ALU.max, ALU.min)
                sv = kgt(); k_act(sv, fr4, AF.Sin, scale=float(2 * np.pi))
                return sv

            sinv = sincos(0.0)
            yield
            cosv = sincos(0.25)
            yield
            a_re, a_im = PW.v[:, 1, 0, :], PW.v[:, 1, 1, :]
            k_tt(a_re, mag, cosv, ALU.mult)
            k_tt(a_im, mag, sinv, ALU.mult)
            S.op("pool", lambda h: h.memset(PW.v[:, 0, 0, :], 1.0), (), KK)
            S.op("pool", lambda h: h.memset(PW.v[:, 0, 1, :], 0.0), (), KK)
            t1_, t2_ = kgt(), kgt()

            def cmul(dst, xr, xi, yr, yi):
                k_tt(t1_, xr, yr, ALU.mult)
                k_tt(t2_, xi, yi, ALU.mult)
                k_tt(dst[0], t1_, t2_, ALU.subtract)
                k_tt(t1_, xr, yi, ALU.mult)
                k_tt(t2_, xi, yr, ALU.mult)
                k_tt(dst[1], t1_, t2_, ALU.add)

            def pwv(n):
                return PW.v[:, n, 0, :], PW.v[:, n, 1, :]

            for n in range(2, 9):
                cmul(pwv(n), *pwv(n - 1), a_re, a_im)
                yield
            cmul(pwv(9), *pwv(8), *pwv(8))
            for n in range(10, 14):
                cmul(pwv(n), *pwv(n - 1), *pwv(n - 1))
                yield
            den = kgt(); k_tt(t1_, lr_, lr_, ALU.mult); k_tt(t2_, li_, li_, ALU.mult); k_tt(den, t1_, t2_, ALU.add)
            rden = kgt(); S.op("dve", lambda h: h.reciprocal(out=rden, in_=den), KK, KK)
            nr = kgt(); k_ts(nr, a_re, -1.0, None, ALU.add)
            co_re, co_im = kgt(), kgt()
            k_tt(t1_, nr, lr_, ALU.mult); k_tt(t2_, a_im, li_, ALU.mult); k_tt(co_re, t1_, t2_, ALU.add)
            k_tt(co_re, co_re, rden, ALU.mult)
            k_tt(t1_, a_im, lr_, ALU.mult); k_tt(t2_, nr, li_, ALU.mult); k_tt(co_im, t1_, t2_, ALU.subtract)
            k_tt(co_im, co_im, rden, ALU.mult)
            Z1, Z2 = kgt(), kgt()
            k_ts(t1_, co_im, HI, None, ALU.mult)
            stt("dve", Z1, co_re, LO, t1_, ALU.mult, ALU.add, KK + HK, KK)
            k_ts(t1_, co_re, HI, None, ALU.mult)
            stt("dve", Z2, co_im, NLO, t1_, ALU.mult, ALU.add, KK + HK, KK)
            for n in range(14):
                pr, pi_ = pwv(n)
                k_ts(DER.v[:, n, 0, :], pi_, SGN, None, ALU.mult)
                k_ts(t1_, pi_, HI, None, ALU.mult)
                stt("dve", DER.v[:, n, 1, :], pr, LO, t1_, ALU.mult, ALU.subtract, KK + HK, KK)
                k_ts(t1_, pr, HI, None, ALU.mult)
                stt("dve", DER.v[:, n, 2, :], pi_, NLO, t1_, ALU.mult, ALU.subtract, KK + HK, KK)
                yield

            def bc16(v):
                return v.unsqueeze(2).to_broadcast([128, 32, 16])

            XN = BBP
            cr_, ci_, z1n, z2n = kgt(), kgt(), kgt(), kgt()
            for n in range(8):
                if n == 0:
                    za, zb = Z1, Z2
                else:
                    cmul((cr_, ci_), *pwv(n), co_re, co_im)
                    k_ts(t1_, ci_, HI, None, ALU.mult)
                    stt("dve", z1n, cr_, LO, t1_, ALU.mult, ALU.add, KK + HK, KK)
                    k_ts(t1_, cr_, HI, None, ALU.mult)
                    stt("dve", z2n, ci_, NLO, t1_, ALU.mult, ALU.add, KK + HK, KK)
                    za, zb = z1n, z2n
                tt("dve", TMP3.v, B12.v[:, 0, :, :], bc16(za), ALU.mult, KK + B12.k() + TMP3.k(), TMP3.k())
                tt("dve", XN.v[:, n, :, :], B12.v[:, 1, :, :], bc16(zb), ALU.mult, KK + B12.k(), XN.k())
                tt("dve", XN.v[:, n, :, :], XN.v[:, n, :, :], TMP3.v, ALU.add, XN.k() + TMP3.k(), XN.k())
                yield
            MK3 = MASK.v.rearrange("p (a b) -> p a b", b=16)
            memset("pool", MASK.v, 1.0, MASK.k())
            S.op("pool", lambda h: h.affine_select(out=MK3, in_=MK3, pattern=[[-16, 8], [0, 16]], compare_op=ALU.is_ge,
                                                   fill=0.0, base=0, channel_multiplier=1), MASK.k(), MASK.k())
            S.op("pool", lambda h: h.affine_select(out=MK3, in_=MK3, pattern=[[16, 8], [0, 16]], compare_op=ALU.is_ge,
                                                   fill=0.0, base=15, channel_multiplier=-1), MASK.k(), MASK.k())
            for n in range(9):
                tt("dve", TMP3.v, C12.v[:, 0, :, :], bc16(DER.v[:, n, 1, :]), ALU.mult, KK + C12.k() + TMP3.k(), TMP3.k())
                tt("dve", WCF.v, C12.v[:, 1, :, :], bc16(DER.v[:, n, 2, :]), ALU.mult, KK + C12.k() + WCF.k(), WCF.k())
                tt("dve", WCF.v, WCF.v, TMP3.v, ALU.add, WCF.k() + TMP3.k(), WCF.k())
                yield
                if n >= 1:
                    for q in range(4):
                        for gq in range(2):
                            s0 = q * 8 + gq
                            g0 = 2 * q + gq
                            cp("dve",
                               WC2ST.v[:, s0:s0 + 7:2, n - 1, :], WCF.v[:, g0::8, :],
                               WCF.k() + WC2ST.k(), WC2ST.k())
                if n <= 7:
                    for ct in range(4):
                        bq = bank()
                        mm(PS[:, 512 * bq:512 * bq + 128], XN.v[:, 0, 8 * ct:8 * ct + 8, :].rearrange("p a b -> p (a b)"),
                           WCF.v[:, 8 * ct:8 * ct + 8, :].rearrange("p a b -> p (a b)"), True, True, XN.k() + WCF.k(), psk(bq))
                        if n == 0:
                            tk = T1[ct % 2]
                            tt("dve", tk.v, PS[:, 512 * bq:512 * bq + 128], MASK.v, ALU.mult, psk(bq) + MASK.k() + tk.k(), tk.k())
                            stt("dve", KDST.v[:, ct, n, :], ident_f[:], dcol[:, ct:ct + 1], tk.v,
                                ALU.mult, ALU.add, tk.k() + ["ident_f", "dcol"], KDST.k())
                        else:
                            tt("dve", KDST.v[:, ct, n, :], PS[:, 512 * bq:512 * bq + 128], MASK.v, ALU.mult,
                               psk(bq) + MASK.k(), KDST.k())
                        yield
            dma("sp", wb_view("kd"), KDST.v.rearrange("p a b c -> p (a b c)"), KDST.k(), wk("kd"), "s_kd")
            dma("sp", wb_view("wc"), WC2ST.v.rearrange("p a b c -> p (a b c)"), WC2ST.k(), wk("wc"), "s_wc0")
            for yb in YB:
                memset("pool", yb.v, 0.0, yb.k())
            for ct in range(4):
                for gq in range(2):
                    for jb in range(2):
                        bq = bank()
                        for jj in range(4):
                            j = 4 * jb + jj
                            yb = YB[4 * gq + jj]
                            cp("dve", yb.v.rearrange("p (q c) -> p q c", c=32)[:, :, 16 * gq:16 * gq + 16],
                               XN.v[:, 7 - j, ct * 8 + gq:ct * 8 + gq + 7:2, :], XN.k() + yb.k(), yb.k())
                            tr(PS[:, 512 * bq + 128 * jj:512 * bq + 128 * jj + 128], yb.v, yb.k() + ["ident_f"], psk(bq),
                               ident=ident_f[:])
                        cp("act", WAST.v[:, ct, gq, 4 * jb:4 * jb + 4, :].rearrange("p a b -> p (a b)"), psv(bq), psk(bq), WAST.k())
                        yield
            dma("sp", wb_view("wa0"), WAST.v[:, 0:2].rearrange("p a b c d -> p (a b c d)"), WAST.k(), wk("wa0"), "s_wa0")
            dma("sp", wb_view("wa1"), WAST.v[:, 2:4].rearrange("p a b c d -> p (a b c d)"), WAST.k(), wk("wa1"), "s_wa1")

            lvl_pw = [8, 9, 10, 11, 12, 13]
            T1B = Buf(WCP.off // KB, 8 * KB, BF16, [32, 128])
            T2B = Buf(WCP.off // KB + 8, 8 * KB, BF16, [32, 128])
            rs_, ss_ = kgt(), kgt()
            idb = ident_bf[:].unsqueeze(1).to_broadcast([128, 32, 128])
            swb = swap_bf[:].unsqueeze(1).to_broadcast([128, 32, 128])
            for m in range(6):
                n = lvl_pw[m]
                st = ATST[0]
                cp("dve", rs_.rearrange("p (q c g) -> p q c g", q=4, c=4, g=2),
                   PW.v[:, n, 0, :].rearrange("p (c q g) -> p q c g", c=4, q=4, g=2), KK, KK)
                cp("dve", ss_.rearrange("p (q c g) -> p q c g", q=4, c=4, g=2),
                   DER.v[:, n, 0, :].rearrange("p (c q g) -> p q c g", c=4, q=4, g=2), KK, KK)
                tt("dve", T1B.v, idb, rs_.unsqueeze(2).to_broadcast([128, 32, 128]), ALU.mult,
                   KK + ["ident"] + T1B.k(), T1B.k())
                tt("dve", T2B.v, swb, ss_.unsqueeze(2).to_broadcast([128, 32, 128]), ALU.mult,
                   KK + ["swap_bf"] + T2B.k(), T2B.k())
                tt("dve", st.v, T1B.v, T2B.v, ALU.add, T1B.k() + T2B.k() + st.k(), st.k())
                dma("sp", wb_view(f"at{m}"), st.v.rearrange("p a b -> p (a b)"), st.k(), wk(f"at{m}"), "s_at%d" % (m % 2))
                yield

        g1, g2 = gen_conv(), gen_s5()
        live = [g1, g2, g2]
        while live:
            for g in list(live):
                try:
                    next(g)
                except StopIteration:
                    live = [x for x in live if x is not g]

        tile_order = (["in0", "in1", "in2", "in3", "in4", "in5", "in6", "wa0", "wa1", "at0", "gm0", "at1", "at2", "gm1", "at3", "at4", "in7", "at5",
                       "kd", "wc", "glu", "bs0", "in8", "bs1",
                       "mo0", "mo1", "q0", "q1", "o0", "o1"] + [f"gu{s}" for s in range(11)]
                      + ["dn00", "dn01", "dn02", "dn10", "dn11", "dn12"])
        if stop_after == "mixer":
            tile_order = tile_order[:tile_order.index("q0")]
        elif stop_after == "attn":
            tile_order = tile_order[:tile_order.index("gu0")]
        seq_order = ["kv0", "kv1", "kv2", "kv3"] + tile_order * (nseq * tps)
        ring_pos = [0]
        ring_loaded = [0]

        def ring_load(k):
            nm = seq_order[k]
            off, n, _ = CAT[nm]
            slot = RING[k % 4]
            dma("sp", slot.v[:, 0:n], wb_d[:, off:off + n], wk(nm), slot.k(), "ring%d" % (k % 4))

        ring_held = {}

        def ring_get(expect, hold=False):
            k = ring_pos[0]
            assert seq_order[k] == expect, (seq_order[k], expect)
            assert k < (min(ring_held.values()) if ring_held else k) + 4, (expect, ring_held)
            while ring_loaded[0] <= k:
                ring_load(ring_loaded[0])
                ring_loaded[0] += 1
            ring_pos[0] += 1
            if hold:
                ring_held[expect] = k
            return RING[k % 4].v, RING[k % 4].k()

        def ring_unhold(name):
            ring_held.pop(name)

        def ring_prefetch():
            lim = (min(ring_held.values()) if ring_held else ring_pos[0]) + 4
            while ring_loaded[0] < min(len(seq_order), lim):
                ring_load(ring_loaded[0])
                ring_loaded[0] += 1

        def rms_rstd(src_ap, src_keys, ncols, junk):
            s = stat_slot(2)
            act(junk.v[:, 0:ncols], src_ap, AF.Square, src_keys + junk.k(), junk.k() + stk(s), accum=stat[:, s:s + 1])
            ts("pool", stat[:, s:s + 1], stat[:, s:s + 1], 1.0 / ncols, EPS, ALU.mult, ALU.add, stk(s), stk(s))
            tt("pool", stat[:, s + 1:s + 2], stat[:, s:s + 1], mhalf[:, 0:1], ALU.pow, stk(s) + ["mhalf"], stk(s + 1))
            return s + 1

        def norm_sub_a(xs, sub, junk, ht):
            r = rms_rstd(xs.v[:, sub, :], xs.k(sub), 1024, junk)
            ts("dve", ht.v, xs.v[:, sub, :], stat[:, r:r + 1], None, ALU.mult, None, xs.k(sub) + stk(r), ht.k())

        def norm_sub_b(sub, ht):
            b = bank()
            for kc in range(8):
                tr(psbf(b)[:, 128 * kc:128 * kc + 128], ht.v[:, 128 * kc:128 * kc + 128], ht.k(), psk(b))
            cp("act" if sub == 1 else "dve", HT.v[:, :, 128 * sub:128 * sub + 128],
               psbf(b).rearrange("p (a b) -> p a b", b=128), psk(b), HT.k())

        def norm_to_hT(xs, hT_cols_fn, nsub, src_fn, junk):
            for sub in range(nsub):
                src, skeys = src_fn(sub)
                r = rms_rstd(src, skeys, 1024, junk)
                ht = HTOK[sub % 2]
                ts("dve", ht.v, src, stat[:, r:r + 1], None, ALU.mult, None, skeys + stk(r), ht.k())
                b = bank()
                for kc in range(8):
                    tr(psbf(b)[:, 128 * kc:128 * kc + 128], ht.v[:, 128 * kc:128 * kc + 128], ht.k(), psk(b))
                dst, dkeys = hT_cols_fn(sub)
                cp("act" if sub % 2 else "dve", dst, psbf(b).rearrange("p (a b) -> p a b", b=128), psk(b), dkeys)

        def post_norm_residual(xs, sub, ps_ap, ps_keys, gi):
            r = rms_rstd(ps_ap, ps_keys, 1024, JUNK)
            tmp = TMPA[sub % 2]
            stt("dve", tmp.v, ps_ap, stat[:, r:r + 1], gpost[:, gi, :], ALU.mult, ALU.mult, ps_keys + stk(r) + ["gpost"], tmp.k())
            tt("dve", xs.v[:, sub, :], xs.v[:, sub, :], tmp.v, ALU.add, xs.k(sub) + tmp.k(), xs.k(sub))

        def fm_proj(slab, skeys, ncol_chunks, col0, rhsbuf, nk, consume):
            sv = slab.rearrange("p (a b) -> p a b", b=512)
            for c in range(ncol_chunks):
                b = bank()
                for kc in range(nk):
                    mm(psv(b), sv[:, kc, col0 + 128 * c:col0 + 128 * c + 128], rhsbuf.v[:, kc, :], kc == 0, kc == nk - 1,
                       skeys + rhsbuf.k(kc), psk(b))
                consume(c, b)

        MEMT = Buf(U0 + 0, nseq * 4 * KB, BF16, [nseq, 8, 256], 4 * KB)
        MTOK = [Buf(U0 + 24, 4 * KB, F32, [1024]), Buf(U0 + 28, 4 * KB, F32, [1024])]
        for sq in range(nseq):
            for mc in range(2):
                mt = MTOK[mc]
                dma("pool", mt.v, mem_d[sq * 256 + mc * 128:sq * 256 + mc * 128 + 128, :], (), mt.k(), "mtok%d" % mc)
                r = rms_rstd(mt.v, mt.k(), 1024, JUNK)
                ht = HTOK[mc]
                ts("dve", ht.v, mt.v, stat[:, r:r + 1], None, ALU.mult, None, mt.k() + stk(r), ht.k())
                b = bank()
                for kc in range(8):
                    tr(psbf(b)[:, 128 * kc:128 * kc + 128], ht.v[:, 128 * kc:128 * kc + 128], ht.k(), psk(b))
                cp("dve", MEMT.v[:, sq, :, 128 * mc:128 * mc + 128], psbf(b).rearrange("p (a b) -> p a b", b=128),
                   psk(b), MEMT.k(sq))
        for s_ in range(4):
            slab, skeys = ring_get(f"kv{s_}")
            sv = slab.rearrange("p (a b) -> p a b", b=512)
            for sq in range(nseq):
                if s_ < 2:
                    for c in range(4):
                        b = bank()
                        for kc in range(8):
                            mm(PS[:, 512 * b:512 * b + 256], sv[:, kc, 128 * c:128 * c + 128], MEMT.v[:, sq, kc, :],
                               kc == 0, kc == 7, skeys + MEMT.k(sq), psk(b))
                        cp("act" if c % 2 else "dve", KT[:, sq, 4 * s_ + c, :], PS[:, 512 * b:512 * b + 256], psk(b),
                           ["KT%d_%d" % (sq, 4 * s_ + c)])
                else:
                    hf = s_ - 2
                    for mc in range(2):
                        b = bank()
                        for kc in range(8):
                            mm(psv(b), MEMT.v[:, sq, kc, 128 * mc:128 * mc + 128], sv[:, kc, :], kc == 0, kc == 7,
                               skeys + MEMT.k(sq), psk(b))
                        cp("act" if mc % 2 else "dve", VV[:, sq, mc, 512 * hf:512 * hf + 512], psv(b), psk(b),
                           ["VV%d_%d_%d" % (sq, mc, hf)])
            ring_prefetch()
        KTK = lambda sq: ["KT%d_%d" % (sq, c) for c in range(8)]
        VVK = lambda sq: ["VV%d_%d_%d" % (sq, mc, hf) for mc in range(2) for hf in range(2)]

        store_ops = []
        ntiles = nseq * tps

        def x_rows(ti):
            sq, i = divmod(ti, tps)
            r0 = sq * SEQ + i * TT
            return r0

        def load_x(ti):
            xs = XS[ti % 2]
            r0 = x_rows(ti)
            dma("pool", xs.v, x_d[r0:r0 + TT, :].rearrange("(s p) d -> p s d", p=128), (), xs.k(), "xs%d" % (ti % 2))

        mixer_norm_done = [False]

        def mixer_norm(tj):
            xj = XS[tj % 2]
            for sub in range(4):
                norm_sub_a(xj, sub, JUNK2, HTOK[sub % 2])
                norm_sub_b(sub, HTOK[sub % 2])

        load_x(0)
        for ti in range(ntiles):
            sq, itile = divmod(ti, tps)
            xs = XS[ti % 2]
            if ti + 1 < ntiles:
                load_x(ti + 1)
            e_prev, e_cur = EALL[ti % 2], EALL[(ti + 1) % 2]
            ek_prev, ek_cur = ["eall%d" % (ti % 2)], ["eall%d" % ((ti + 1) % 2)]
            if itile == 0:
                memset("pool", e_prev[:, :, 64:65], 0.0, ek_prev)

            if not mixer_norm_done[0]:
                mixer_norm(ti)
            mixer_norm_done[0] = False
            for s_ in range(2):
                slab, skeys = ring_get(f"in{s_}")
                fm_proj(slab, skeys, 4, 0, HT, 8,
                        lambda c, b, s_=s_: act(UT.v[:, 4 * s_ + c, :], psv(b), AF.Gelu, psk(b), UT.k(4 * s_ + c)))
                ring_prefetch()
            slab2, sk2 = ring_get("in2")
            slab3, sk3 = ring_get("in3")
            vstat = {}

            def v_mm(sub):
                vt = VT[sub % 2]
                for hf, (slab, skeys) in enumerate(((slab2, sk2), (slab3, sk3))):
                    sv = slab.rearrange("p (a b) -> p a b", b=512)
                    b = bank()
                    for kc in range(8):
                        mm(psv(b), HT.v[:, kc, 128 * sub:128 * sub + 128], sv[:, kc, :], kc == 0, kc == 7,
                           skeys + HT.k(kc), psk(b))
                    act(vt.v[:, 512 * hf:512 * hf + 512], psv(b), AF.Gelu, psk(b), vt.k())
                s = stat_slot(16)
                vstat[sub] = s
                bsv = stat[:, s:s + 12].rearrange("p (a b) -> p a b", b=6)
                for hf in range(2):
                    S.op("dve", lambda h, hf=hf, vt=vt, bsv=bsv: h.bn_stats(out=bsv[:, hf, :], in_=vt.v[:, 512 * hf:512 * hf + 512]),
                         vt.k(), stk(s + 6 * hf, 6))
                S.op("dve", lambda h, s=s, bsv=bsv: h.bn_aggr(out=stat[:, s + 12:s + 14], in_=bsv), stk(s, 12), stk(s + 12, 2))

            def v_ln(sub):
                vt = VT[sub % 2]
                s = vstat[sub]
                ts("pool", stat[:, s + 14:s + 15], stat[:, s + 13:s + 14], EPS, None, ALU.add, None, stk(s + 13), stk(s + 14))
                tt("pool", stat[:, s + 15:s + 16], stat[:, s + 14:s + 15], mhalf[:, 0:1], ALU.pow, stk(s + 14) + ["mhalf"], stk(s + 15))
                vh = VHAT[sub % 2]
                ts("dve", vh.v, vt.v, stat[:, s + 12:s + 13], stat[:, s + 15:s + 16], ALU.subtract, ALU.mult,
                   vt.k() + stk(s + 12) + stk(s + 15), vh.k())

            def gmlp(sub):
                vh = VHAT[sub % 2]
                b = bank(2)
                for g in range(8):
                    mm(PS[:, 512 * b + 128 * g:512 * b + 128 * g + 128], vh.v[:, 128 * g:128 * g + 128], wst_bf[:, g, :],
                       True, True, vh.k() + ["wst_bf"], psk(b + g // 4))
                tmp = TMPA[sub % 2]
                for g in range(8):
                    stt("dve", tmp.v[:, 128 * g:128 * g + 128], PS[:, 512 * b + 128 * g:512 * b + 128 * g + 128],
                        glng[:, g:g + 1], tb[:, g, :], ALU.mult, ALU.add, psk(b + g // 4) + ["glng", "tb"], tmp.k())
                tt("dve", YGM.v[:, :, 128 * sub:128 * sub + 128], tmp.v.rearrange("p (a b) -> p a b", b=128),
                   UT.v[:, :, 128 * sub:128 * sub + 128], ALU.mult, tmp.k() + UT.k(), YGM.k())

            def fm_chunk(slab, skeys, c, rhsbuf, nk, consume, t0=0, tn=512, b=None):
                sv = slab.rearrange("p (a b) -> p a b", b=512)
                b = bank() if b is None else b
                for kc in range(nk):
                    mm(PS[:, 512 * b:512 * b + tn], sv[:, kc, 128 * c:128 * c + 128], rhsbuf.v[:, kc, t0:t0 + tn],
                       kc == 0, kc == nk - 1, skeys + rhsbuf.k(kc), psk(b))
                consume(c, b)

            def z_cons(c, b):
                cp("act", ZS5.v[:, c, :], psv(b), psk(b), ZS5.k(c))

            def ga_cons(s_):
                return lambda c, b: act(SGA.v[:, 4 * s_ + c, :], psv(b), AF.Tanh, psk(b), SGA.k(4 * s_ + c), scale=0.5)

            v_mm(0)
            v_mm(1)
            slab4, sk4 = ring_get("in4")
            fm_chunk(slab4, sk4, 0, HT, 8, z_cons)
            fm_chunk(slab4, sk4, 1, HT, 8, z_cons)
            v_ln(0)
            gmlp(0)
            v_mm(2)
            fm_chunk(slab4, sk4, 2, HT, 8, z_cons)
            fm_chunk(slab4, sk4, 3, HT, 8, z_cons)
            slab5, sk5 = ring_get("in5")
            v_ln(1)
            gmlp(1)
            v_mm(3)
            for c in range(4):
                fm_chunk(slab5, sk5, c, HT, 8, ga_cons(0))
            v_ln(2)
            gmlp(2)
            slab6, sk6 = ring_get("in6")
            fm_chunk(slab6, sk6, 0, HT, 8, ga_cons(1))
            fm_chunk(slab6, sk6, 1, HT, 8, ga_cons(1))
            v_ln(3)
            gmlp(3)
            fm_chunk(slab6, sk6, 2, HT, 8, ga_cons(1))
            fm_chunk(slab6, sk6, 3, HT, 8, ga_cons(1))
            ring_prefetch()

            for c in range(4):
                cp("dve", ZS5J.v[:, c, :].rearrange("p (j t) -> p j t", j=8), ZS5.v[:, c, :].rearrange("p (t j) -> p j t", j=8),
                   ZS5.k(c), ZS5J.k(c))
            wa = []
            for s_ in range(2):
                slab, skeys = ring_get(f"wa{s_}")
                wa.append((slab.rearrange("p (a b c d) -> p a b c d", b=2, c=8, d=128), skeys))
            bx = bank(4)
            assert bx % 4 == 0
            for r_ in range(8):
                ct, gq = divmod(r_, 2)
                wv, wkeys = wa[ct // 2]
                for j in range(8):
                    for q in range(4):
                        o_ = PS[:, 512 * (bx + q) + 64 * r_:512 * (bx + q) + 64 * r_ + 64]
                        mm(o_, wv[32 * q:32 * q + 32, ct % 2, gq, j, :], ZS5J.v[32 * q:32 * q + 32, ct, 64 * j:64 * j + 64],
                           j == 0, j == 7, wkeys + ZS5J.k(ct), psk(bx + q), tp=(32 * q, 0))
            for q in range(4):
                cp("act" if q % 2 else "dve", XA.v[:, 8 * q:8 * q + 8, :], psv(bx + q).rearrange("p (a b) -> p a b", b=64),
                   psk(bx + q), XA.k(8 * q, 8))
            ring_prefetch()
            at0, at0k = ring_get("at0")
            at0v = at0.rearrange("p (a b) -> p a b", b=128)
            bc_ = bank()
            for sg in range(32):
                mm(PS[:, 512 * bc_ + sg:512 * bc_ + sg + 1], at0v[:, sg, :], e_prev[:, sg, 64:65], True, True,
                   at0k + ek_prev, psk(bc_))
            tt("dve", XA.v[:, :, 0], XA.v[:, :, 0], PS[:, 512 * bc_:512 * bc_ + 32], ALU.add, XA.k() + psk(bc_), XA.k())
            cp("pool", e_cur[:, :, 0:1], e_prev[:, :, 64:65], ek_prev, ek_cur)
            fb = (bx + 4) % 8
            fctr = [0]
            pa_slab = {}

            def pa_chunk(c8):
                s_, c = divmod(c8, 4)
                slab, skeys = pa_slab[s_]
                bsel = fb + fctr[0] % 4
                fctr[0] += 1
                fm_chunk(slab, skeys, c, YGM, 8,
                         lambda c, b, s_=s_: stt("dve", MRG.v[:, 4 * s_ + c, :], SGA.v[:, 4 * s_ + c, :], 1.0, psv(b), ALU.add, ALU.mult,
                                                 psk(b) + SGA.k(4 * s_ + c), MRG.k(4 * s_ + c)), b=bsel)

            for m in range(6):
                if m == 0:
                    atv, atk = at0v, at0k
                    pa_slab[0] = ring_get("gm0", hold=True)
                else:
                    a_, atk = ring_get(f"at{m}")
                    atv = a_.rearrange("p (a b) -> p a b", b=128)
                    if m == 2:
                        pa_slab[1] = ring_get("gm1", hold=True)
                    if m == 4:
                        gb_slab = ring_get("in7", hold=True)
                sh = 1 << m
                for q in range(4):
                    b = bx + q
                    for r_ in range(8):
                        sg = 8 * q + r_
                        base = 512 * b + 64 * r_
                        mm(PS[:, base + sh:base + 64], atv[:, sg, :], XA.v[:, sg, 0:64 - sh], True, True,
                           atk + XA.k(sg), psk(b))
                    tt("dve", XA.v[:, 8 * q:8 * q + 8, sh:64], XA.v[:, 8 * q:8 * q + 8, sh:64],
                       psv(b).rearrange("p (a b) -> p a b", b=64)[:, :, sh:64], ALU.add, psk(b) + XA.k(8 * q, 8), XA.k(8 * q, 8))
                if m < 4:
                    pa_chunk(2 * m)
                    pa_chunk(2 * m + 1)
                if m >= 4:
                    for c in (2 * (m - 4), 2 * (m - 4) + 1):
                        bsel = fb + fctr[0] % 4
                        fctr[0] += 1
                        fm_chunk(gb_slab[0], gb_slab[1], c, HT, 8,
                                 lambda c, b: act(SGA.v[:, c, :], psv(b), AF.Tanh, psk(b), SGA.k(c), scale=0.5), b=bsel)
                    if m == 5:
                        ring_unhold("in7")
                if m == 1:
                    ring_unhold("gm0")
                if m == 3:
                    ring_unhold("gm1")
                ring_prefetch()
            cp("act", e_cur[:, 0:16, 1:65], XA.v[:, 0:16, :], XA.k(0, 16), ek_cur)
            cp("dve", e_cur[:, 16:32, 1:65], XA.v[:, 16:32, :], XA.k(16, 16), ek_cur)
            kd, kdk = ring_get("kd")
            kdv = kd.rearrange("p (a b c) -> p a b c", b=8, c=128)
            wcs, wck = ring_get("wc")
            wcv = wcs.rearrange("p (a b) -> p a b", b=128)
            for ct in range(4):
                b = bx + ct
                for d_ in range(8):
                    mm(PS[:, 512 * b + 64 * d_:512 * b + 512], kdv[:, ct, d_, :],
                       ZS5J.v[:, ct, 0:64 * (8 - d_)], d_ == 0, False, kdk + ZS5J.k(ct), psk(b))
            def c_state(ct):
                by = fb + 2 * (ct % 2)
                for g8 in range(8):
                    q, gq = divmod(g8, 2)
                    sg = q * 8 + ct * 2 + gq
                    mm(PS[0:64, 512 * by + 128 * g8:512 * by + 128 * g8 + 128], e_cur[:, sg, 0:64], wcv[:, sg, :], True, True,
                       wck + ek_cur, psk(by + g8 // 4))
                yt = YTOK[ct % 2]
                cp("act" if ct % 2 else "dve", yt.v[0:64, :].rearrange("p (j g h) -> p g j h", j=8, g=8, h=16),
                   PS[0:64, 512 * by:512 * by + 1024].rearrange("p (g j h) -> p g j h", g=8, j=8, h=16),
                   psk(by, 2), yt.k())

            def c_back(ct):
                b = bx + ct
                yt = YTOK[ct % 2]
                for jp in range(8):
                    mm(PS[:, 512 * b + 64 * jp:512 * b + 64 * jp + 64], yt.v[0:64, 128 * jp:128 * jp + 128],
                       ident_bf[0:64, 0:64], False, jp == 7, yt.k() + ["ident"], psk(b))
                act(YGT.v[:, ct, :], psv(b), AF.Gelu, psk(b), YGT.k(ct))

            c_state(0)
            c_state(1)
            c_back(0)
            c_state(2)
            c_back(1)
            c_state(3)
            c_back(2)
            c_back(3)
            ring_prefetch()
            slab, skeys = ring_get("glu")
            fm_proj(slab, skeys, 4, 0, YGT, 4,
                    lambda c, b: act(SGL.v[:, c, :], psv(b), AF.Tanh, psk(b), SGL.k(c), scale=0.5))
            for c in range(4):
                stt("dve", YS5.v[:, c, :], SGL.v[:, c, :], 1.0, YGT.v[:, c, :], ALU.add, ALU.mult, YGT.k(c) + SGL.k(c), YS5.k(c))
            ring_prefetch()
            for s_ in range(2):
                if s_ == 1:
                    gslab, gkeys = ring_get("in8")
                bslab, bkeys = ring_get(f"bs{s_}")
                for c in range(4):
                    if s_ == 1:
                        fm_chunk(gslab, gkeys, c, HT, 8,
                                 lambda c, b: act(SGA.v[:, 4 + c, :], psv(b), AF.Tanh, psk(b), SGA.k(4 + c), scale=0.5))

                    def cons(c, b, s_=s_):
                        m2 = M2[c % 2]
                        stt("dve", m2.v.rearrange("p (t j) -> p t j", j=8), SGA.v[:, 4 * s_ + c, :].rearrange("p (t j) -> p t j", j=8), 1.0,
                            psv(b).rearrange("p (j t) -> p t j", j=8), ALU.add, ALU.mult, psk(b) + SGA.k(4 * s_ + c), m2.k())
                        tt("dve", MRG.v[:, 4 * s_ + c, :], MRG.v[:, 4 * s_ + c, :], m2.v, ALU.add,
                           MRG.k(4 * s_ + c) + m2.k(), MRG.k(4 * s_ + c))
                    fm_chunk(bslab, bkeys, c, YS5, 4, cons)
                ring_prefetch()

            def stage_b(sub):
                ht = HTOK[sub % 2]
                norm_sub_a(xs, sub, JUNK, ht)
                norm_sub_b(sub, ht)

            def tm_out(names, srcbuf, gi, follow=False, nxt=None):
                s0, k0 = ring_get(names[0])
                s1, k1 = ring_get(names[1])
                def m_a(sub):
                    b = bank(2)
                    for hf, (slab, skeys) in enumerate(((s0, k0), (s1, k1))):
                        sv = slab.rearrange("p (a b) -> p a b", b=512)
                        for kc in range(8):
                            mm(psv(b + hf), srcbuf.v[:, kc, 128 * sub:128 * sub + 128], sv[:, kc, :], kc == 0, kc == 7,
                               skeys + srcbuf.k(kc), psk(b + hf))
                    return b

                def a_(sub, b):
                    post_norm_residual(xs, sub, psv(b, 2), psk(b, 2), gi)

                if not follow:
                    for sub in range(4):
                        a_(sub, m_a(sub))
                    ring_prefetch()
                    return
                a_(0, m_a(0))
                a_(1, m_a(1))
                b2 = m_a(2)
                norm_sub_a(xs, 0, JUNK, HTOK4[0])
                a_(2, b2)
                b3 = m_a(3)
                norm_sub_a(xs, 1, JUNK, HTOK4[1])
                a_(3, b3)
                norm_sub_a(xs, 2, JUNK, HTOK4[2])
                norm_sub_a(xs, 3, JUNK, HTOK4[3])
                for sub in range(4):
                    norm_sub_b(sub, HTOK4[sub])
                if nxt is not None:
                    nxt()
                ring_prefetch()

            def q_first():
                slab, skeys = ring_get("q0")
                fm_proj(slab, skeys, 4, 0, HT, 8, lambda c, b: cp("act", QT.v[:, c, :], psv(b), psk(b), QT.k(c)))

            def gu_chunk(slab, skeys, s_, c, t0=0, tn=512):
                sv = slab.rearrange("p (a b) -> p a b", b=512)
                j = 2 * s_ + c
                bg = bank()
                for kc in range(8):
                    mm(PS[:, 512 * bg:512 * bg + tn], sv[:, kc, 128 * c:128 * c + 128], HT.v[:, kc, t0:t0 + tn], kc == 0, kc == 7,
                       skeys + HT.k(kc), psk(bg))
                si = SILU[j % 2]
                act(si.v[:, 0:tn], PS[:, 512 * bg:512 * bg + tn], AF.Silu, psk(bg), si.k())
                bu = bank()
                for kc in range(8):
                    mm(PS[:, 512 * bu:512 * bu + tn], sv[:, kc, 256 + 128 * c:256 + 128 * c + 128], HT.v[:, kc, t0:t0 + tn],
                       kc == 0, kc == 7, skeys + HT.k(kc), psk(bu))
                tt("dve", ACTB.v[:, j, t0:t0 + tn], PS[:, 512 * bu:512 * bu + tn], si.v[:, 0:tn], ALU.mult,
                   psk(bu) + si.k(), ACTB.k(j))

            def gu_first():
                slab, skeys = ring_get("gu0")
                for c in range(2):
                    gu_chunk(slab, skeys, 0, c)

            q_slab, gu_slab = [], []

            tm_out(("mo0", "mo1"), MRG, 0, follow=(stop_after != "mixer"), nxt=(q_first if stop_after != "mixer" else None))

            if stop_after != "mixer":
                slab, skeys = ring_get("q1")
                fm_proj(slab, skeys, 4, 0, HT, 8,
                        lambda c, b: cp("act", QT.v[:, 4 + c, :], psv(b), psk(b), QT.k(4 + c)))
                ring_prefetch()
                for hh in range(4):
                    for mc in range(2):
                        b = bank()
                        for dc in range(2):
                            mm(psv(b), KT[:, sq, 2 * hh + dc, 128 * mc:128 * mc + 128], QT.v[:, 2 * hh + dc, :], dc == 0, dc == 1,
                               KTK(sq) + QT.k(2 * hh + dc), psk(b))
                        act(PT.v[:, 2 * hh + mc, :], psv(b), AF.Exp, psk(b), PT.k(2 * hh + mc), scale=1.0 / 16.0)
                for hh in range(4):
                    b = bank()
                    for mc in range(2):
                        mm(psv(b), ones_bf[:], PT.v[:, 2 * hh + mc, :], mc == 0, mc == 1, ["ones_bf"] + PT.k(2 * hh + mc), psk(b))
                    rs = RS[hh % 2]
                    act(rs.v, psv(b), AF.Ln, psk(b), rs.k())
                    act(rs.v, rs.v, AF.Exp, rs.k(), rs.k(), scale=-1.0)
                    for dc in range(2):
                        b = bank()
                        for mc in range(2):
                            mm(psv(b), VV[:, sq, mc, 256 * hh + 128 * dc:256 * hh + 128 * dc + 128], PT.v[:, 2 * hh + mc, :],
                               mc == 0, mc == 1, VVK(sq) + PT.k(2 * hh + mc), psk(b))
                        tt("dve", OT.v[:, 2 * hh + dc, :], psv(b), rs.v, ALU.mult, psk(b) + rs.k(), OT.k(2 * hh + dc))
                tm_out(("o0", "o1"), OT, 1, follow=(stop_after == "ffn"), nxt=(gu_first if stop_after == "ffn" else None))

            if stop_after == "ffn":
                if ti + 1 < ntiles:
                    xn = XS[(ti + 1) % 2]
                    for sub in range(4):
                        norm_sub_a(xn, sub, JUNK2, HTOK4[sub])
                for s_ in range(1, 11):
                    slab, skeys = ring_get(f"gu{s_}")
                    for c in range(2):
                        gu_chunk(slab, skeys, s_, c)
                    ring_prefetch()
                for hf in range(2):
                    slabs = [ring_get(f"dn{hf}{kg}") for kg in range(3)]
                    for sub in range(4):
                        b = bank()
                        for j in range(22):
                            slab, skeys = slabs[j // 8]
                            sv = slab.rearrange("p (a b) -> p a b", b=512)
                            mm(psv(b), ACTB.v[:, j, 128 * sub:128 * sub + 128], sv[:, j % 8, :], j == 0, j == 21,
                               skeys + ACTB.k(j), psk(b))
                        cp("act" if sub % 2 else "dve", YBUF.v[:, sub, 512 * hf:512 * hf + 512], psv(b), psk(b), YBUF.k(sub))
                    ring_prefetch()
                if ti + 1 < ntiles:
                    for sub in range(4):
                        norm_sub_b(sub, HTOK4[sub])
                    mixer_norm_done[0] = True
                for sub in range(4):
                    post_norm_residual(xs, sub, YBUF.v[:, sub, :], YBUF.k(sub), 2)

            r0 = x_rows(ti)
            store_ops.append(dma("pool", y_d[r0:r0 + TT, :].rearrange("(s p) d -> p s d", p=128), xs.v, xs.k(), (),
                                 "ys%d" % (ti % 2)))

        S.emit(finals=[("pool", i) for i in store_ops])
    return nc


_NC_CACHE = {}


def kernel(**inputs):
    inp = {k: np.asarray(v) for k, v in inputs.items()}
    x = inp["x"]
    mem = inp["mem"]
    B = x.shape[0]
    nseq = B // NCORES
    wf = _host_weights(inp)
    sm = _host_small(inp)
    key = ("full", nseq)
    if key not in _NC_CACHE:
        _NC_CACHE[key] = build(nseq=nseq, tps=SEQ // TT)
    nc = _NC_CACHE[key]
    in_maps = []
    for c in range(NCORES):
        m = {"x": np.ascontiguousarray(x[c * nseq:(c + 1) * nseq].reshape(nseq * SEQ, D)),
             "mem": np.ascontiguousarray(mem[c * nseq:(c + 1) * nseq].reshape(nseq * 256, D)),
             "wf": wf}
        m.update(sm)
        in_maps.append(m)
    res = run_bass_kernel_spmd(nc, in_maps, core_ids=list(range(NCORES)))
    out = np.concatenate([np.asarray(r["y"]).reshape(nseq, SEQ, D) for r in res.results], axis=0)
    return out.astype(np.float32)
```

```python
import contextlib
import numpy as np
import concourse.bass as bass
import concourse.mybir as mybir
from concourse.bass_utils import run_bass_kernel_spmd

F32 = mybir.dt.float32
BF16 = mybir.dt.bfloat16
I32 = mybir.dt.int32
AF = mybir.ActivationFunctionType
ALU = mybir.AluOpType

NCORES = 8
SEQ = 4096
D = 1024
TT = 512
EPS = 1e-6
ENGS = ("pe", "act", "dve", "pool", "sp")


class Sched:
    def __init__(self, nc):
        self.nc = nc
        self.ops = []
        self.last_w = {}
        self.readers = {}
        self.dma_cnt = {}

    def op(self, eng, fn, reads=(), writes=(), dma_sem=None):
        idx = len(self.ops)
        deps = set()
        for k in reads:
            w = self.last_w.get(k)
            if w is not None:
                deps.add(w)
        for k in writes:
            w = self.last_w.get(k)
            if w is not None:
                deps.add(w)
            for r in self.readers.get(k, ()):
                deps.add(r)
        if eng == "pe":
            deps = {d for d in deps if self.ops[d]["eng"] != "pe"}
        o = dict(eng=eng, fn=fn, deps=sorted(deps), pub=False, dma=dma_sem, val=None)
        if dma_sem is not None:
            c = self.dma_cnt.get(dma_sem, 0) + 16
            self.dma_cnt[dma_sem] = c
            o["val"] = c
            o["pub"] = True
        self.ops.append(o)
        for d in deps:
            self.ops[d]["pub"] = True
        for k in reads:
            self.readers.setdefault(k, []).append(idx)
        for k in writes:
            self.last_w[k] = idx
            self.readers[k] = []
        return idx

    def emit(self, finals=()):
        nc = self.nc
        fin = {}
        for (ename, d) in finals:
            fin.setdefault(ename, []).append(d)
            self.ops[d]["pub"] = True
        cnt = {e: 0 for e in ENGS}
        for o in self.ops:
            if o["dma"] is None and o["pub"]:
                cnt[o["eng"]] += 1
                o["val"] = cnt[o["eng"]]
        with contextlib.ExitStack() as es:
            esem = {e: es.enter_context(nc.semaphore("s_" + e)) for e in ENGS}
            dsem = {n: es.enter_context(nc.semaphore("d_" + n)) for n in self.dma_cnt}
            block = es.enter_context(nc.Block())

            def semof(o):
                if o["dma"] is not None:
                    return ("d", o["dma"]), dsem[o["dma"]]
                return ("e", o["eng"]), esem[o["eng"]]

            def run(ename, h):
                known = {}

                def wait(d):
                    p = self.ops[d]
                    key, sh = semof(p)
                    if known.get(key, 0) < p["val"]:
                        h.wait_ge(sh, p["val"])
                        known[key] = p["val"]

                for o in self.ops:
                    if o["eng"] != ename:
                        continue
                    for d in o["deps"]:
                        wait(d)
                    inst = o["fn"](h)
                    if o["pub"]:
                        _, sh = semof(o)
                        inst.then_inc(sh, 16 if o["dma"] is not None else 1)
                for d in fin.get(ename, ()):
                    wait(d)

            @block.tensor
            def _(h):
                run("pe", h)

            @block.scalar
            def _(h):
                run("act", h)

            @block.vector
            def _(h):
                run("dve", h)

            @block.gpsimd
            def _(h):
                run("pool", h)

            @block.sync
            def _(h):
                run("sp", h)


def _slab_catalog():
    cat = {}
    off = 0

    def add(name, n, fold=None):
        nonlocal off
        cat[name] = (off, n, fold)
        off += n

    for s in range(9):
        add(f"in{s}", 4096, 0)
    for s in range(2):
        add(f"gm{s}", 4096)
    for s in range(2):
        add(f"bs{s}", 2048, "half")
    for s in range(2):
        add(f"mo{s}", 4096, "half")
    add("glu", 2048)
    for s in range(2):
        add(f"q{s}", 4096, 1)
    for s in range(4):
        add(f"kv{s}", 4096, 3)
    for s in range(2):
        add(f"o{s}", 4096)
    for s in range(11):
        add(f"gu{s}", 4096, 2)
    for h in range(2):
        for kg in range(3):
            add(f"dn{h}{kg}", 4096 if kg < 2 else 3072)
    nf32 = off
    for m in range(6):
        add(f"at{m}", 4096)
    add("wa0", 4096)
    add("wa1", 4096)
    add("kd", 4096)
    add("wc", 4096)
    return cat, nf32, off


CAT, NF32, NTOT = _slab_catalog()


def _slabify(W, c0, c1):
    K = W.shape[0]
    kc = K // 128
    return np.ascontiguousarray(
        W[:, c0:c1].reshape(kc, 128, c1 - c0).transpose(1, 0, 2)).reshape(128, kc * (c1 - c0))


def _host_weights(inp):
    parts = {}
    w_in = inp["w_in"][0]
    for s in range(9):
        parts[f"in{s}"] = _slabify(w_in, 512 * s, 512 * s + 512)
    for nm, key in (("gm", "w_br_gm"), ("bs", "w_br_s5"), ("mo", "w_mix_out"), ("q", "ca_w_q"),
                    ("o", "ca_w_o")):
        W = inp[key][0]
        for s in range(2):
            parts[f"{nm}{s}"] = _slabify(W, 512 * s, 512 * s + 512)
    parts["glu"] = _slabify(inp["s5_w_glu"][0], 0, 512)
    W = inp["ca_w_kv"][0]
    for s in range(4):
        parts[f"kv{s}"] = _slabify(W, 512 * s, 512 * s + 512)
    W = inp["ffn_w_gu"][0]
    for s in range(11):
        Wc = np.concatenate([W[:, 256 * s:256 * s + 256], W[:, 2816 + 256 * s:2816 + 256 * s + 256]], axis=1)
        parts[f"gu{s}"] = _slabify(Wc, 0, 512)
    W = inp["ffn_w_down"][0]
    for h in range(2):
        for kg in range(3):
            r0 = kg * 1024
            r1 = min(r0 + 1024, 2816)
            parts[f"dn{h}{kg}"] = _slabify(W[r0:r1], 512 * h, 512 * h + 512)
    out = np.empty((128, NF32), np.float32)
    for nm, (off, n, _) in CAT.items():
        if off >= NF32:
            continue
        assert parts[nm].shape == (128, n), (nm, parts[nm].shape, n)
        out[:, off:off + n] = parts[nm]
    return out


def _host_small(inp):
    f = np.float32
    sm = {}
    sm["gpost"] = np.ascontiguousarray(np.broadcast_to(
        np.concatenate([inp["g_mix_post"][0], inp["g_ca_post"][0], inp["g_ffn_post"][0]])[None, :], (128, 3072))).astype(f)
    gp = np.stack([inp["g_mix_pre"][0], inp["g_ca_pre"][0], inp["g_ffn_pre"][0], inp["g_mem"][0]], 0)
    sm["gpre"] = np.ascontiguousarray(gp.reshape(4, 8, 128).transpose(2, 0, 1)).reshape(128, 32).astype(f)
    sm["glng"] = np.ascontiguousarray(inp["gm_ln_g"][0].reshape(8, 128).T).astype(f)
    sm["rows"] = np.ascontiguousarray(np.stack([inp["gm_ln_b"][0], inp["gm_b_s"][0].reshape(1024)], 0)).astype(f)
    sm["wst"] = np.ascontiguousarray(inp["gm_w_s"][0].transpose(2, 0, 1)).reshape(128, 1024).astype(f)
    lr = inp["s5_lam_re"][0].T
    li = inp["s5_lam_im"][0].T
    ls = np.broadcast_to(inp["s5_log_step"][0][None, :], (128, 32))
    sm["lam"] = np.ascontiguousarray(np.concatenate(
        [np.concatenate([lr, lr], 0), np.concatenate([li, li], 0), ls], 1)).astype(f)
    b1 = inp["s5_b_re"][0].transpose(1, 0, 2).reshape(64, 512)
    b2 = inp["s5_b_im"][0].transpose(1, 0, 2).reshape(64, 512)
    sm["b12"] = np.ascontiguousarray(np.concatenate(
        [np.concatenate([b1, b1], 0), np.concatenate([b2, b2], 0)], 1)).astype(f)
    c1 = inp["s5_c_re"][0].transpose(2, 0, 1).reshape(64, 512)
    c2 = inp["s5_c_im"][0].transpose(2, 0, 1).reshape(64, 512)
    sm["c12"] = np.ascontiguousarray(np.concatenate(
        [np.concatenate([c1, c1], 0), np.concatenate([c2, c2], 0)], 1)).astype(f)
    sm["dcol"] = np.ascontiguousarray(inp["s5_d"][0].reshape(4, 128).T).astype(f)
    return sm


SMALL_SHAPES = dict(gpost=[128, 3072], gpre=[128, 32], glng=[128, 8], rows=[2, 1024], wst=[128, 1024],
                    lam=[128, 96], b12=[128, 1024], c12=[128, 1024], dcol=[128, 4])


def sig_g(sig):
    q, r = divmod(sig, 8)
    ct, gq = divmod(r, 2)
    return ct * 8 + 2 * q + gq


def build(nseq=2, tps=8, stop_after="ffn"):
    nc = bass.Bass("TRN2", target_bir_lowering=False)
    ntok = nseq * SEQ
    x_d = nc.dram_tensor("x", [ntok, D], F32, kind="ExternalInput").ap()
    mem_d = nc.dram_tensor("mem", [nseq * 256, D], F32, kind="ExternalInput").ap()
    wf_d = nc.dram_tensor("wf", [128, NF32], F32, kind="ExternalInput").ap()
    sm_d = {k: nc.dram_tensor(k, shp, F32, kind="ExternalInput").ap() for k, shp in SMALL_SHAPES.items()}
    y_d = nc.dram_tensor("y", [ntok, D], F32, kind="ExternalOutput").ap()
    wb_d = nc.dram_tensor("wb", [128, NTOT], BF16, kind="Internal").ap()

    S = Sched(nc)
    es = contextlib.ExitStack()
    with es:
        def sbt(name, shape, dt):
            return es.enter_context(nc.sbuf_tensor("sb_" + name, shape, dt))

        KB = 1024
        ARENA_KB = 156
        A = sbt("arena", [128, ARENA_KB * KB // 2], BF16)
        PS = es.enter_context(nc.psum_tensor("ps", [128, 4096], F32))

        class Buf:
            def __init__(self, off_kb, size_b, dt, shape_tail, chunk_b=None):
                self.off = int(off_kb * KB)
                self.size = size_b
                self.dt = dt
                self.chunk_b = chunk_b or size_b
                v = A[:, self.off // 2:(self.off + size_b) // 2]
                if dt == F32:
                    v = v.bitcast(F32)
                elif dt == I32:
                    v = v.bitcast(I32)
                if len(shape_tail) == 2:
                    v = v.rearrange("p (a b) -> p a b", b=shape_tail[1])
                elif len(shape_tail) == 3:
                    v = v.rearrange("p (a b c) -> p a b c", b=shape_tail[1], c=shape_tail[2])
                elif len(shape_tail) == 4:
                    v = v.rearrange("p (a b c d) -> p a b c d", b=shape_tail[1], c=shape_tail[2], d=shape_tail[3])
                self.v = v

            def k(self, c=None, n=1):
                if c is None:
                    lo, hi = self.off, self.off + self.size
                else:
                    lo = self.off + c * self.chunk_b
                    hi = lo + n * self.chunk_b
                return [("A", p) for p in range(lo // KB, (hi - 1) // KB + 1)]

        def psk(b, n=1):
            return [("ps", i) for i in range(b, b + n)]

        def psv(b, n=1):
            return PS[:, 512 * b:512 * (b + n)]

        def psbf(b):
            return PS[:, 512 * b:512 * (b + 1)].bitcast(BF16)

        bank_ctr = [0]

        def bank(n=1):
            b = bank_ctr[0]
            if n > 1 and b % n:
                b += n - b % n
            if b + n > 8:
                b = 0
            bank_ctr[0] = (b + n) % 8
            return b

        XS = [Buf(0, 16 * KB, F32, [4, 1024], 4 * KB), Buf(16, 16 * KB, F32, [4, 1024], 4 * KB)]
        RING = [Buf(32 + 8 * i, 8 * KB, BF16, [4096]) for i in range(4)]
        U0 = 64
        HT = Buf(144, 8 * KB, BF16, [8, 512], KB)
        HTOK = [Buf(152, 2 * KB, BF16, [1024]), Buf(154, 2 * KB, BF16, [1024])]
        UT = Buf(U0 + 0, 8 * KB, BF16, [8, 512], KB)
        ZS5 = Buf(U0 + 8, 4 * KB, BF16, [4, 512], KB)
        VT = [Buf(U0 + 12, 4 * KB, F32, [1024]), Buf(U0 + 16, 4 * KB, F32, [1024])]
        ZS5J = Buf(U0 + 12, 4 * KB, BF16, [4, 512], KB)
        VHAT = [Buf(U0 + 20, 2 * KB, BF16, [1024]), Buf(U0 + 22, 2 * KB, BF16, [1024])]
        YGM = Buf(U0 + 24, 8 * KB, BF16, [8, 512], KB)
        XA = Buf(U0 + 32, 4 * KB, BF16, [32, 64], 128)
        YTOK = [Buf(U0 + 36, 2 * KB, BF16, [1024]), Buf(U0 + 38, 2 * KB, BF16, [1024])]
        YGT = Buf(U0 + 40, 4 * KB, BF16, [4, 512], KB)
        SGL = Buf(U0 + 44, 4 * KB, BF16, [4, 512], KB)
        YS5 = Buf(U0 + 48, 4 * KB, BF16, [4, 512], KB)
        SGA = Buf(U0 + 52, 8 * KB, BF16, [8, 512], KB)
        MRG = Buf(U0 + 60, 8 * KB, BF16, [8, 512], KB)
        TMPA = [Buf(U0 + 68, 4 * KB, F32, [1024]), Buf(U0 + 72, 4 * KB, F32, [1024])]
        M2 = [Buf(U0 + 76, 2 * KB, F32, [512]), Buf(U0 + 78, 2 * KB, F32, [512])]
        QT = Buf(U0 + 0, 8 * KB, BF16, [8, 512], KB)
        PT = Buf(U0 + 8, 8 * KB, BF16, [8, 512], KB)
        RS = [Buf(U0 + 16, 2 * KB, F32, [512]), Buf(U0 + 18, 2 * KB, F32, [512])]
        OT = Buf(U0 + 24, 8 * KB, BF16, [8, 512], KB)
        ACTB = Buf(U0 + 0, 22 * KB, BF16, [22, 512], KB)
        SILU = [Buf(U0 + 22, KB, BF16, [512]), Buf(U0 + 23, KB, BF16, [512])]
        YBUF = Buf(U0 + 24, 16 * KB, F32, [4, 1024], 4 * KB)
        JUNK = Buf(U0 + 40, 2 * KB, BF16, [1024])
        JUNK2 = Buf(U0 + 42, 2 * KB, BF16, [1024])
        HTOK4 = [Buf(U0 + 44 + 2 * i, 2 * KB, BF16, [1024]) for i in range(4)]

        gpost = sbt("gpost", [128, 3, 1024], F32)
        gpre = sbt("gpre", [128, 4, 8], F32)
        glng = sbt("glng", [128, 8], F32)
        ident_bf = sbt("ident_bf", [128, 128], BF16)
        ident_f = sbt("ident_f", [128, 128], F32)
        swap_f = sbt("swap_f", [128, 128], F32)
        swap_bf = sbt("swap_bf", [128, 128], BF16)
        ones_bf = sbt("ones_bf", [128, 128], BF16)
        ones_f = sbt("ones_f", [128, 1], F32)
        wst_bf = sbt("wst_bf", [128, 8, 128], BF16)
        tb = sbt("tb", [128, 8, 128], F32)
        KT = sbt("KT", [128, nseq, 8, 256], BF16)
        VV = sbt("VV", [128, nseq, 2, 1024], BF16)
        EALL = [sbt("eall0", [128, 32, 65], BF16), sbt("eall1", [128, 32, 65], BF16)]
        stat = sbt("stat", [128, 64], F32)
        dcol = sbt("dcol", [128, 4], F32)
        half = sbt("half", [128, 4], F32)
        epsc = sbt("epsc", [128, 1], F32)
        mhalf = sbt("mhalf", [128, 1], F32)

        stat_ctr = [0]

        def stat_slot(n=1):
            s = stat_ctr[0]
            if s + n > 64:
                s = 0
            stat_ctr[0] = s + n
            return s

        def stk(s, n=1):
            return [("stat", i) for i in range(s, s + n)]

        def mm(out, lhsT, rhs, start, stop, reads, writes, tp=None):
            def f(h, out=out, lhsT=lhsT, rhs=rhs, start=start, stop=stop, tp=tp):
                if tp is None:
                    return h.matmul(out, lhsT=lhsT, rhs=rhs, start=start, stop=stop, skip_group_check=True)
                return h.matmul(out, lhsT=lhsT, rhs=rhs, start=start, stop=stop, tile_position=tp,
                                skip_group_check=True)
            return S.op("pe", f, reads, writes)

        def tr(out, in_, reads, writes, ident=None):
            ident = ident_bf[:] if ident is None else ident
            return S.op("pe", lambda h, out=out, in_=in_, ident=ident: h.transpose(out=out, in_=in_, identity=ident),
                        list(reads) + ["ident"], writes)

        def act(out, in_, func, reads, writes, scale=None, bias=None, accum=None):
            def f(h, out=out, in_=in_, func=func, scale=scale, bias=bias, accum=accum):
                kw = {}
                if scale is not None:
                    kw["scale"] = scale
                if bias is not None:
                    kw["bias"] = bias
                if accum is not None:
                    kw["accum_out"] = accum
                return h.activation(out=out, in_=in_, func=func, **kw)
            return S.op("act", f, reads, writes)

        def ts(eng, out, in0, s1, s2, op0, op1, reads, writes):
            def f(h, out=out, in0=in0, s1=s1, s2=s2, op0=op0, op1=op1):
                if op1 is None:
                    return h.tensor_scalar(out=out, in0=in0, scalar1=s1, scalar2=None, op0=op0)
                return h.tensor_scalar(out=out, in0=in0, scalar1=s1, scalar2=s2, op0=op0, op1=op1)
            return S.op(eng, f, reads, writes)

        def tt(eng, out, in0, in1, op, reads, writes):
            return S.op(eng, lambda h, out=out, in0=in0, in1=in1, op=op: h.tensor_tensor(out=out, in0=in0, in1=in1, op=op),
                        reads, writes)

        def stt(eng, out, in0, scalar, in1, op0, op1, reads, writes):
            return S.op(eng, lambda h, out=out, in0=in0, scalar=scalar, in1=in1, op0=op0, op1=op1:
                        h.scalar_tensor_tensor(out=out, in0=in0, scalar=scalar, in1=in1, op0=op0, op1=op1),
                        reads, writes)

        def cp(eng, out, in_, reads, writes):
            if eng == "act":
                return S.op("act", lambda h, out=out, in_=in_: h.activation(out=out, in_=in_, func=AF.Copy), reads, writes)
            return S.op(eng, lambda h, out=out, in_=in_: h.tensor_copy(out=out, in_=in_), reads, writes)

        def memset(eng, ap, val, writes):
            return S.op(eng, lambda h, ap=ap, val=val: h.memset(ap, val), (), writes)

        def dma(eng, out, in_, reads, writes, sem):
            return S.op(eng, lambda h, out=out, in_=in_: h.dma_start(out=out, in_=in_), reads, writes, dma_sem=sem)

        def wk(name):
            if CAT[name][0] < NF32:
                return [("W", name, i) for i in range((CAT[name][1] + 2047) // 2048)]
            return [("W", name)]

        def wb_view(name):
            off, n, _ = CAT[name]
            return wb_d[:, off:off + n]

        dma("sp", gpost[:].rearrange("p a b -> p (a b)"), sm_d["gpost"][:, :], (), ["gpost"], "c_gpost")
        dma("sp", gpre[:].rearrange("p a b -> p (a b)"), sm_d["gpre"][:, :], (), ["gpre"], "c_gpre")
        dma("sp", glng[:], sm_d["glng"][:, :], (), ["glng"], "c_glng")
        dma("sp", dcol[:], sm_d["dcol"][:, :], (), ["dcol"], "c_dcol")
        memset("pool", ident_f[:], 0.0, ["ident_f0"])
        S.op("pool", lambda h: h.affine_select(out=ident_f[:], in_=ident_f[:], pattern=[[-1, 128]],
                                               compare_op=ALU.not_equal, fill=1.0, base=0, channel_multiplier=1),
             ["ident_f0"], ["ident_f"])
        cp("dve", ident_bf[:], ident_f[:], ["ident_f"], ["ident"])
        cp("dve", swap_f[:, 0:64], ident_f[:, 64:128], ["ident_f"], ["swap_a"])
        cp("dve", swap_f[:, 64:128], ident_f[:, 0:64], ["ident_f"], ["swap_b"])
        cp("dve", swap_bf[:], swap_f[:], ["swap_a", "swap_b"], ["swap_bf"])
        memset("pool", ones_bf[:], 1.0, ["ones_bf"])
        memset("pool", ones_f[:], 1.0, ["ones_f"])
        memset("pool", epsc[:], EPS, ["epsc"])
        memset("pool", mhalf[:], -0.5, ["mhalf"])
        memset("pool", half[0:64, 0:1], 1.0, ["half_a"])
        memset("pool", half[64:128, 0:1], 0.0, ["half_b"])
        memset("pool", half[0:64, 1:2], 0.0, ["half_c"])
        memset("pool", half[64:128, 1:2], 1.0, ["half_d"])
        HK = ["half_a", "half_b", "half_c", "half_d"]
        tt("dve", half[:, 2:3], half[:, 0:1], half[:, 1:2], ALU.subtract, HK, ["half_s"])
        ts("dve", half[:, 3:4], half[:, 0:1], -1.0, None, ALU.mult, None, HK, ["half_n"])
        HK = HK + ["half_s", "half_n"]
        LO, HI, SGN, NLO = half[:, 0:1], half[:, 1:2], half[:, 2:3], half[:, 3:4]
        for e in EALL:
            memset("pool", e[:], 0.0, ["eall%d" % EALL.index(e)])

        NST = 4
        STG_F = [Buf(8 * i, 8 * KB, F32, [2048]) for i in range(NST)]
        STG_B = [Buf(32 + 4 * i, 4 * KB, BF16, [2048]) for i in range(NST)]
        conv_engs = ["act", "dve"]
        pieces = []
        for nm, (off, n, fold) in CAT.items():
            if off >= NF32:
                continue
            for p0 in range(0, n, 2048):
                pieces.append((nm, off + p0, min(2048, n - p0), fold, p0 // 512))
        ci = [0]

        def conv_load(i):
            nm, off, n, fold, kc0 = pieces[i]
            sf = STG_F[i % NST]
            dma("sp", sf.v[:, 0:n], wf_d[:, off:off + n], (), sf.k(), "stgf%d" % (i % NST))

        def conv_rest(i):
            nm, off, n, fold, kc0 = pieces[i]
            sf, sb_ = STG_F[i % NST], STG_B[i % NST]
            for c in range(n // 512):
                eng = conv_engs[ci[0] % 2]
                ci[0] += 1
                o_ = sb_.v[:, c * 512:(c + 1) * 512]
                i_ = sf.v[:, c * 512:(c + 1) * 512]
                if fold is None:
                    cp(eng, o_, i_, sf.k(), sb_.k())
                elif fold == "half":
                    if eng == "act":
                        act(o_, i_, AF.Copy, sf.k(), sb_.k(), scale=0.5)
                    else:
                        ts(eng, o_, i_, 0.5, None, ALU.mult, None, sf.k(), sb_.k())
                else:
                    sc = gpre[:, fold, kc0 + c:kc0 + c + 1]
                    if eng == "act":
                        act(o_, i_, AF.Copy, sf.k() + ["gpre"], sb_.k(), scale=sc)
                    else:
                        ts(eng, o_, i_, sc, None, ALU.mult, None, sf.k() + ["gpre"], sb_.k())
            dma("sp", wb_d[:, off:off + n], sb_.v[:, 0:n], sb_.k(), [("W", nm, kc0 // 4)], "stgb%d" % (i % NST))

        def gen_conv():
            conv_load(0)
            conv_load(1)
            conv_load(2)
            for i in range(len(pieces)):
                if i + 3 < len(pieces):
                    conv_load(i + 3)
                conv_rest(i)
                yield


        def gen_s5():
            WSF = Buf(56, 4 * KB, F32, [8, 128])
            dma("sp", WSF.v.rearrange("p a b -> p (a b)"), sm_d["wst"][:, :], (), WSF.k(), "c_wst")
            for g in range(8):
                S.op("pool", lambda h, g=g: h.affine_select(out=WSF.v[:, g, :], in_=WSF.v[:, g, :], pattern=[[1, 128]],
                                                            compare_op=ALU.is_ge, fill=0.0, base=0, channel_multiplier=-1),
                     WSF.k(), WSF.k())
            cp("dve", wst_bf[:], WSF.v, WSF.k(), ["wst_bf"])
            R2Lb = Buf(48, 4 * KB, F32, [1024])
            R2Rb = Buf(52, 4 * KB, F32, [1024])
            R2L, R2R = R2Lb.v, R2Rb.v
            memset("pool", R2L[0:2, :], 1.0, R2Lb.k())
            dma("sp", R2L[0:1, :], sm_d["rows"][0:1, :], R2Lb.k(), R2Lb.k(), "c_r2l")
            dma("sp", R2R[1:2, :], sm_d["rows"][1:2, :], (), R2Rb.k(), "c_r2r")
            b0 = bank(2)
            for g in range(8):
                mm(PS[0:1, 512 * b0 + 128 * g:512 * b0 + 128 * g + 128], ones_f[:, 0:1], WSF.v[:, g, :], True, True,
                   WSF.k() + ["ones_f"], psk(b0 + g // 4))
            cp("dve", R2R[0:1, :], PS[0:1, 512 * b0:512 * b0 + 1024], psk(b0, 2) + R2Rb.k(), R2Rb.k())
            b1 = bank(2)
            for g in range(8):
                mm(PS[:, 512 * b1 + 128 * g:512 * b1 + 128 * g + 128], R2L[0:2, 128 * g:128 * g + 128],
                   R2R[0:2, 128 * g:128 * g + 128], True, True, R2Lb.k() + R2Rb.k(), psk(b1 + g // 4))
            cp("dve", tb[:].rearrange("p a b -> p (a b)"), PS[:, 512 * b1:512 * b1 + 1024], psk(b1, 2), ["tb"])
            yield

            s5 = {}
            soff = [60]

            def salloc(name, nbytes, dt, tail):
                b = Buf(soff[0], nbytes, dt, tail)
                soff[0] += (nbytes + KB - 1) // KB
                s5[name] = b
                return b

            LAM = salloc("lam", 96 * 4, F32, [3, 32])
            B12 = salloc("b12", 4 * KB, F32, [2, 32, 16])
            C12 = salloc("c12", 4 * KB, F32, [2, 32, 16])
            KG = salloc("kg", 40 * 128, F32, [40, 32])
            PW = salloc("pw", 14 * 2 * 128, F32, [14, 2, 32])
            DER = salloc("der", 14 * 4 * 128, F32, [14, 4, 32])
            YB = [salloc("yb%d" % i, 512, F32, [128]) for i in range(2)]
            MASK = salloc("mask", 512, F32, [128])
            BBP = salloc("bbp", 16 * KB, F32, [8, 32, 16])
            WCF = salloc("wcf", 2 * KB, F32, [32, 16])
            WCP = salloc("wcp", 16 * KB, F32, [32, 128])
            YB += [salloc("yb%d" % i, 512, F32, [128]) for i in range(2, 8)]
            T1 = [salloc("t1a", 512, F32, [128]), salloc("t1b", 512, F32, [128])]
            ATST = [salloc("atst0", 8 * KB, BF16, [32, 128])]
            ATST.append(ATST[0])
            WAST = Buf(WCP.off // KB, 16 * KB, BF16, [4, 2, 8, 128])
            WC2ST = salloc("wc2st", 8 * KB, BF16, [32, 8, 16])
            KDST = salloc("kdst", 8 * KB, BF16, [4, 8, 128])
            TMP3 = salloc("tmp3", 2 * KB, F32, [32, 16])
            assert soff[0] <= ARENA_KB, soff[0]

            dma("sp", LAM.v.rearrange("p a b -> p (a b)"), sm_d["lam"][:, :], (), LAM.k(), "c_lam")
            dma("sp", B12.v.rearrange("p a b c -> p (a b c)"), sm_d["b12"][:, :], (), B12.k(), "c_b12")
            dma("sp", C12.v.rearrange("p a b c -> p (a b c)"), sm_d["c12"][:, :], (), C12.k(), "c_c12")

            kgi = [0]

            def kgt():
                i = kgi[0]
                kgi[0] += 1
                assert i < 40
                return KG.v[:, i, :]

            KK = KG.k() + LAM.k() + PW.k() + DER.k()
            lr_, li_, ls_ = LAM.v[:, 0, :], LAM.v[:, 1, :], LAM.v[:, 2, :]

            def k_tt(out, a, b, op):
                tt("dve", out, a, b, op, KK + HK, KK)

            def k_ts(out, a, s1, s2, op0, op1=None):
                ts("dve", out, a, s1, s2, op0, op1, KK + HK, KK)

            def k_act(out, a, func, scale=None):
                act(out, a, func, KK, KK, scale=scale)

            step = kgt(); k_act(step, ls_, AF.Exp)
            lrs = kgt(); k_tt(lrs, lr_, step, ALU.mult)
            lis = kgt(); k_tt(lis, li_, step, ALU.mult)
            mag = kgt(); k_act(mag, lrs, AF.Exp)

            def sincos(shift):
                t = kgt(); k_ts(t, lis, 1.0 / (2 * np.pi), shift, ALU.mult, ALU.add)
                ti = s5["tmp3"].v[:, 0:2, :].rearrange("p a b -> p (a b)").bitcast(I32)
                cp("dve", ti, t, KK, KK + TMP3.k())
                tf = kgt(); cp("dve", tf, ti, KK + TMP3.k(), KK)
                fr = kgt(); k_tt(fr, t, tf, ALU.subtract)
                g1 = kgt(); k_ts(g1, fr, 0.5, None, ALU.is_gt)
                fr2 = kgt(); k_tt(fr2, fr, g1, ALU.subtract)
                g2 = kgt(); k_ts(g2, fr2, -0.5, None, ALU.is_lt)
                fr3 = kgt(); k_tt(fr3, fr2, g2, ALU.add)
                fr4 = kgt(); k_ts(fr4, fr3, -0.49999994, 0.49999994, ALU.max, ALU.min)
                sv = kgt(); k_act(sv, fr4, AF.Sin, scale=float(2 * np.pi))
                return sv

            sinv = sincos(0.0)
            yield
            cosv = sincos(0.25)
            yield
            a_re, a_im = PW.v[:, 1, 0, :], PW.v[:, 1, 1, :]
            k_tt(a_re, mag, cosv, ALU.mult)
            k_tt(a_im, mag, sinv, ALU.mult)
            S.op("pool", lambda h: h.memset(PW.v[:, 0, 0, :], 1.0), (), KK)
            S.op("pool", lambda h: h.memset(PW.v[:, 0, 1, :], 0.0), (), KK)
            t1_, t2_ = kgt(), kgt()

            def cmul(dst, xr, xi, yr, yi):
                k_tt(t1_, xr, yr, ALU.mult)
                k_tt(t2_, xi, yi, ALU.mult)
                k_tt(dst[0], t1_, t2_, ALU.subtract)
                k_tt(t1_, xr, yi, ALU.mult)
                k_tt(t2_, xi, yr, ALU.mult)
                k_tt(dst[1], t1_, t2_, ALU.add)

            def pwv(n):
                return PW.v[:, n, 0, :], PW.v[:, n, 1, :]

            for n in range(2, 9):
                cmul(pwv(n), *pwv(n - 1), a_re, a_im)
                yield
            cmul(pwv(9), *pwv(8), *pwv(8))
            for n in range(10, 14):
                cmul(pwv(n), *pwv(n - 1), *pwv(n - 1))
                yield
            den = kgt(); k_tt(t1_, lr_, lr_, ALU.mult); k_tt(t2_, li_, li_, ALU.mult); k_tt(den, t1_, t2_, ALU.add)
            rden = kgt(); S.op("dve", lambda h: h.reciprocal(out=rden, in_=den), KK, KK)
            nr = kgt(); k_ts(nr, a_re, -1.0, None, ALU.add)
            co_re, co_im = kgt(), kgt()
            k_tt(t1_, nr, lr_, ALU.mult); k_tt(t2_, a_im, li_, ALU.mult); k_tt(co_re, t1_, t2_, ALU.add)
            k_tt(co_re, co_re, rden, ALU.mult)
            k_tt(t1_, a_im, lr_, ALU.mult); k_tt(t2_, nr, li_, ALU.mult); k_tt(co_im, t1_, t2_, ALU.subtract)
            k_tt(co_im, co_im, rden, ALU.mult)
            Z1, Z2 = kgt(), kgt()
            k_ts(t1_, co_im, HI, None, ALU.mult)
            stt("dve", Z1, co_re, LO, t1_, ALU.mult, ALU.add, KK + HK, KK)
            k_ts(t1_, co_re, HI, None, ALU.mult)
            stt("dve", Z2, co_im, NLO, t1_, ALU.mult, ALU.add, KK + HK, KK)
            for n in range(14):
                pr, pi_ = pwv(n)
                k_ts(DER.v[:, n, 0, :], pi_, SGN, None, ALU.mult)
                k_ts(t1_, pi_, HI, None, ALU.mult)
                stt("dve", DER.v[:, n, 1, :], pr, LO, t1_, ALU.mult, ALU.subtract, KK + HK, KK)
                k_ts(t1_, pr, HI, None, ALU.mult)
                stt("dve", DER.v[:, n, 2, :], pi_, NLO, t1_, ALU.mult, ALU.subtract, KK + HK, KK)
                yield

            def bc16(v):
                return v.unsqueeze(2).to_broadcast([128, 32, 16])

            XN = BBP
            cr_, ci_, z1n, z2n = kgt(), kgt(), kgt(), kgt()
            for n in range(8):
                if n == 0:
                    za, zb = Z1, Z2
                else:
                    cmul((cr_, ci_), *pwv(n), co_re, co_im)
                    k_ts(t1_, ci_, HI, None, ALU.mult)
                    stt("dve", z1n, cr_, LO, t1_, ALU.mult, ALU.add, KK + HK, KK)
                    k_ts(t1_, cr_, HI, None, ALU.mult)
                    stt("dve", z2n, ci_, NLO, t1_, ALU.mult, ALU.add, KK + HK, KK)
                    za, zb = z1n, z2n
                tt("dve", TMP3.v, B12.v[:, 0, :, :], bc16(za), ALU.mult, KK + B12.k() + TMP3.k(), TMP3.k())
                tt("dve", XN.v[:, n, :, :], B12.v[:, 1, :, :], bc16(zb), ALU.mult, KK + B12.k(), XN.k())
                tt("dve", XN.v[:, n, :, :], XN.v[:, n, :, :], TMP3.v, ALU.add, XN.k() + TMP3.k(), XN.k())
                yield
            MK3 = MASK.v.rearrange("p (a b) -> p a b", b=16)
            memset("pool", MASK.v, 1.0, MASK.k())
            S.op("pool", lambda h: h.affine_select(out=MK3, in_=MK3, pattern=[[-16, 8], [0, 16]], compare_op=ALU.is_ge,
                                                   fill=0.0, base=0, channel_multiplier=1), MASK.k(), MASK.k())
            S.op("pool", lambda h: h.affine_select(out=MK3, in_=MK3, pattern=[[16, 8], [0, 16]], compare_op=ALU.is_ge,
                                                   fill=0.0, base=15, channel_multiplier=-1), MASK.k(), MASK.k())
            for n in range(9):
                tt("dve", TMP3.v, C12.v[:, 0, :, :], bc16(DER.v[:, n, 1, :]), ALU.mult, KK + C12.k() + TMP3.k(), TMP3.k())
                tt("dve", WCF.v, C12.v[:, 1, :, :], bc16(DER.v[:, n, 2, :]), ALU.mult, KK + C12.k() + WCF.k(), WCF.k())
                tt("dve", WCF.v, WCF.v, TMP3.v, ALU.add, WCF.k() + TMP3.k(), WCF.k())
                yield
                if n >= 1:
                    for q in range(4):
                        for gq in range(2):
                            s0 = q * 8 + gq
                            g0 = 2 * q + gq
                            cp("dve",
                               WC2ST.v[:, s0:s0 + 7:2, n - 1, :], WCF.v[:, g0::8, :],
                               WCF.k() + WC2ST.k(), WC2ST.k())
                if n <= 7:
                    for ct in range(4):
                        bq = bank()
                        mm(PS[:, 512 * bq:512 * bq + 128], XN.v[:, 0, 8 * ct:8 * ct + 8, :].rearrange("p a b -> p (a b)"),
                           WCF.v[:, 8 * ct:8 * ct + 8, :].rearrange("p a b -> p (a b)"), True, True, XN.k() + WCF.k(), psk(bq))
                        if n == 0:
                            tk = T1[ct % 2]
                            tt("dve", tk.v, PS[:, 512 * bq:512 * bq + 128], MASK.v, ALU.mult, psk(bq) + MASK.k() + tk.k(), tk.k())
                            stt("dve", KDST.v[:, ct, n, :], ident_f[:], dcol[:, ct:ct + 1], tk.v,
                                ALU.mult, ALU.add, tk.k() + ["ident_f", "dcol"], KDST.k())
                        else:
                            tt("dve", KDST.v[:, ct, n, :], PS[:, 512 * bq:512 * bq + 128], MASK.v, ALU.mult,
                               psk(bq) + MASK.k(), KDST.k())
                        yield
            dma("sp", wb_view("kd"), KDST.v.rearrange("p a b c -> p (a b c)"), KDST.k(), wk("kd"), "s_kd")
            dma("sp", wb_view("wc"), WC2ST.v.rearrange("p a b c -> p (a b c)"), WC2ST.k(), wk("wc"), "s_wc0")
            for yb in YB:
                memset("pool", yb.v, 0.0, yb.k())
            for ct in range(4):
                for gq in range(2):
                    for jb in range(2):
                        bq = bank()
                        for jj in range(4):
                            j = 4 * jb + jj
                            yb = YB[4 * gq + jj]
                            cp("dve", yb.v.rearrange("p (q c) -> p q c", c=32)[:, :, 16 * gq:16 * gq + 16],
                               XN.v[:, 7 - j, ct * 8 + gq:ct * 8 + gq + 7:2, :], XN.k() + yb.k(), yb.k())
                            tr(PS[:, 512 * bq + 128 * jj:512 * bq + 128 * jj + 128], yb.v, yb.k() + ["ident_f"], psk(bq),
                               ident=ident_f[:])
                        cp("act", WAST.v[:, ct, gq, 4 * jb:4 * jb + 4, :].rearrange("p a b -> p (a b)"), psv(bq), psk(bq), WAST.k())
                        yield
            dma("sp", wb_view("wa0"), WAST.v[:, 0:2].rearrange("p a b c d -> p (a b c d)"), WAST.k(), wk("wa0"), "s_wa0")
            dma("sp", wb_view("wa1"), WAST.v[:, 2:4].rearrange("p a b c d -> p (a b c d)"), WAST.k(), wk("wa1"), "s_wa1")

            lvl_pw = [8, 9, 10, 11, 12, 13]
            T1B = Buf(WCP.off // KB, 8 * KB, BF16, [32, 128])
            T2B = Buf(WCP.off // KB + 8, 8 * KB, BF16, [32, 128])
            rs_, ss_ = kgt(), kgt()
            idb = ident_bf[:].unsqueeze(1).to_broadcast([128, 32, 128])
            swb = swap_bf[:].unsqueeze(1).to_broadcast([128, 32, 128])
            for m in range(6):
                n = lvl_pw[m]
                st = ATST[0]
                cp("dve", rs_.rearrange("p (q c g) -> p q c g", q=4, c=4, g=2),
                   PW.v[:, n, 0, :].rearrange("p (c q g) -> p q c g", c=4, q=4, g=2), KK, KK)
                cp("dve", ss_.rearrange("p (q c g) -> p q c g", q=4, c=4, g=2),
                   DER.v[:, n, 0, :].rearrange("p (c q g) -> p q c g", c=4, q=4, g=2), KK, KK)
                tt("dve", T1B.v, idb, rs_.unsqueeze(2).to_broadcast([128, 32, 128]), ALU.mult,
                   KK + ["ident"] + T1B.k(), T1B.k())
                tt("dve", T2B.v, swb, ss_.unsqueeze(2).to_broadcast([128, 32, 128]), ALU.mult,
                   KK + ["swap_bf"] + T2B.k(), T2B.k())
                tt("dve", st.v, T1B.v, T2B.v, ALU.add, T1B.k() + T2B.k() + st.k(), st.k())
                dma("sp", wb_view(f"at{m}"), st.v.rearrange("p a b -> p (a b)"), st.k(), wk(f"at{m}"), "s_at%d" % (m % 2))
                yield

        g1, g2 = gen_conv(), gen_s5()
        live = [g1, g2, g2]
        while live:
            for g in list(live):
                try:
                    next(g)
                except StopIteration:
                    live = [x for x in live if x is not g]

        tile_order = (["in0", "in1", "in2", "in3", "in4", "in5", "in6", "wa0", "wa1", "at0", "gm0", "at1", "at2", "gm1", "at3", "at4", "in7", "at5",
                       "kd", "wc", "glu", "bs0", "in8", "bs1",
                       "mo0", "mo1", "q0", "q1", "o0", "o1"] + [f"gu{s}" for s in range(11)]
                      + ["dn00", "dn01", "dn02", "dn10", "dn11", "dn12"])
        if stop_after == "mixer":
            tile_order = tile_order[:tile_order.index("q0")]
        elif stop_after == "attn":
            tile_order = tile_order[:tile_order.index("gu0")]
        seq_order = ["kv0", "kv1", "kv2", "kv3"] + tile_order * (nseq * tps)
        ring_pos = [0]
        ring_loaded = [0]

        def ring_load(k):
            nm = seq_order[k]
            off, n, _ = CAT[nm]
            slot = RING[k % 4]
            dma("sp", slot.v[:, 0:n], wb_d[:, off:off + n], wk(nm), slot.k(), "ring%d" % (k % 4))

        ring_held = {}

        def ring_get(expect, hold=False):
            k = ring_pos[0]
            assert seq_order[k] == expect, (seq_order[k], expect)
            assert k < (min(ring_held.values()) if ring_held else k) + 4, (expect, ring_held)
            while ring_loaded[0] <= k:
                ring_load(ring_loaded[0])
                ring_loaded[0] += 1
            ring_pos[0] += 1
            if hold:
                ring_held[expect] = k
            return RING[k % 4].v, RING[k % 4].k()

        def ring_unhold(name):
            ring_held.pop(name)

        def ring_prefetch():
            lim = (min(ring_held.values()) if ring_held else ring_pos[0]) + 4
            while ring_loaded[0] < min(len(seq_order), lim):
                ring_load(ring_loaded[0])
                ring_loaded[0] += 1

        def rms_rstd(src_ap, src_keys, ncols, junk):
            s = stat_slot(2)
            act(junk.v[:, 0:ncols], src_ap, AF.Square, src_keys + junk.k(), junk.k() + stk(s), accum=stat[:, s:s + 1])
            ts("pool", stat[:, s:s + 1], stat[:, s:s + 1], 1.0 / ncols, EPS, ALU.mult, ALU.add, stk(s), stk(s))
            tt("pool", stat[:, s + 1:s + 2], stat[:, s:s + 1], mhalf[:, 0:1], ALU.pow, stk(s) + ["mhalf"], stk(s + 1))
            return s + 1

        def norm_sub_a(xs, sub, junk, ht):
            r = rms_rstd(xs.v[:, sub, :], xs.k(sub), 1024, junk)
            ts("dve", ht.v, xs.v[:, sub, :], stat[:, r:r + 1], None, ALU.mult, None, xs.k(sub) + stk(r), ht.k())

        def norm_sub_b(sub, ht):
            b = bank()
            for kc in range(8):
                tr(psbf(b)[:, 128 * kc:128 * kc + 128], ht.v[:, 128 * kc:128 * kc + 128], ht.k(), psk(b))
            cp("act" if sub % 2 else "dve", HT.v[:, :, 128 * sub:128 * sub + 128],
               psbf(b).rearrange("p (a b) -> p a b", b=128), psk(b), HT.k())

        def norm_to_hT(xs, hT_cols_fn, nsub, src_fn, junk):
            for sub in range(nsub):
                src, skeys = src_fn(sub)
                r = rms_rstd(src, skeys, 1024, junk)
                ht = HTOK[sub % 2]
                ts("dve", ht.v, src, stat[:, r:r + 1], None, ALU.mult, None, skeys + stk(r), ht.k())
                b = bank()
                for kc in range(8):
                    tr(psbf(b)[:, 128 * kc:128 * kc + 128], ht.v[:, 128 * kc:128 * kc + 128], ht.k(), psk(b))
                dst, dkeys = hT_cols_fn(sub)
                cp("act" if sub % 2 else "dve", dst, psbf(b).rearrange("p (a b) -> p a b", b=128), psk(b), dkeys)

        def post_norm_residual(xs, sub, ps_ap, ps_keys, gi):
            r = rms_rstd(ps_ap, ps_keys, 1024, JUNK)
            tmp = TMPA[sub % 2]
            stt("dve", tmp.v, ps_ap, stat[:, r:r + 1], gpost[:, gi, :], ALU.mult, ALU.mult, ps_keys + stk(r) + ["gpost"], tmp.k())
            tt("dve", xs.v[:, sub, :], xs.v[:, sub, :], tmp.v, ALU.add, xs.k(sub) + tmp.k(), xs.k(sub))

        def fm_proj(slab, skeys, ncol_chunks, col0, rhsbuf, nk, consume):
            sv = slab.rearrange("p (a b) -> p a b", b=512)
            for c in range(ncol_chunks):
                b = bank()
                for kc in range(nk):
                    mm(psv(b), sv[:, kc, col0 + 128 * c:col0 + 128 * c + 128], rhsbuf.v[:, kc, :], kc == 0, kc == nk - 1,
                       skeys + rhsbuf.k(kc), psk(b))
                consume(c, b)

        MEMT = Buf(U0 + 0, nseq * 4 * KB, BF16, [nseq, 8, 256], 4 * KB)
        MTOK = [Buf(U0 + 24, 4 * KB, F32, [1024]), Buf(U0 + 28, 4 * KB, F32, [1024])]
        for sq in range(nseq):
            for mc in range(2):
                mt = MTOK[mc]
                dma("pool", mt.v, mem_d[sq * 256 + mc * 128:sq * 256 + mc * 128 + 128, :], (), mt.k(), "mtok%d" % mc)
                r = rms_rstd(mt.v, mt.k(), 1024, JUNK)
                ht = HTOK[mc]
                ts("dve", ht.v, mt.v, stat[:, r:r + 1], None, ALU.mult, None, mt.k() + stk(r), ht.k())
                b = bank()
                for kc in range(8):
                    tr(psbf(b)[:, 128 * kc:128 * kc + 128], ht.v[:, 128 * kc:128 * kc + 128], ht.k(), psk(b))
                cp("dve", MEMT.v[:, sq, :, 128 * mc:128 * mc + 128], psbf(b).rearrange("p (a b) -> p a b", b=128),
                   psk(b), MEMT.k(sq))
        for s_ in range(4):
            slab, skeys = ring_get(f"kv{s_}")
            sv = slab.rearrange("p (a b) -> p a b", b=512)
            for sq in range(nseq):
                if s_ < 2:
                    for c in range(4):
                        b = bank()
                        for kc in range(8):
                            mm(PS[:, 512 * b:512 * b + 256], sv[:, kc, 128 * c:128 * c + 128], MEMT.v[:, sq, kc, :],
                               kc == 0, kc == 7, skeys + MEMT.k(sq), psk(b))
                        cp("act" if c % 2 else "dve", KT[:, sq, 4 * s_ + c, :], PS[:, 512 * b:512 * b + 256], psk(b),
                           ["KT%d_%d" % (sq, 4 * s_ + c)])
                else:
                    hf = s_ - 2
                    for mc in range(2):
                        b = bank()
                        for kc in range(8):
                            mm(psv(b), MEMT.v[:, sq, kc, 128 * mc:128 * mc + 128], sv[:, kc, :], kc == 0, kc == 7,
                               skeys + MEMT.k(sq), psk(b))
                        cp("act" if mc % 2 else "dve", VV[:, sq, mc, 512 * hf:512 * hf + 512], psv(b), psk(b),
                           ["VV%d_%d_%d" % (sq, mc, hf)])
            ring_prefetch()
        KTK = lambda sq: ["KT%d_%d" % (sq, c) for c in range(8)]
        VVK = lambda sq: ["VV%d_%d_%d" % (sq, mc, hf) for mc in range(2) for hf in range(2)]

        store_ops = []
        ntiles = nseq * tps

        def x_rows(ti):
            sq, i = divmod(ti, tps)
            r0 = sq * SEQ + i * TT
            return r0

        def load_x(ti):
            xs = XS[ti % 2]
            r0 = x_rows(ti)
            dma("pool", xs.v, x_d[r0:r0 + TT, :].rearrange("(s p) d -> p s d", p=128), (), xs.k(), "xs%d" % (ti % 2))

        mixer_norm_done = [False]

        def mixer_norm(tj):
            xj = XS[tj % 2]
            for sub in range(4):
                norm_sub_a(xj, sub, JUNK2, HTOK[sub % 2])
                norm_sub_b(sub, HTOK[sub % 2])

        load_x(0)
        for ti in range(ntiles):
            sq, itile = divmod(ti, tps)
            xs = XS[ti % 2]
            if ti + 1 < ntiles:
                load_x(ti + 1)
            e_prev, e_cur = EALL[ti % 2], EALL[(ti + 1) % 2]
            ek_prev, ek_cur = ["eall%d" % (ti % 2)], ["eall%d" % ((ti + 1) % 2)]
            if itile == 0:
                memset("pool", e_prev[:, :, 64:65], 0.0, ek_prev)

            if not mixer_norm_done[0]:
                mixer_norm(ti)
            mixer_norm_done[0] = False
            for s_ in range(2):
                slab, skeys = ring_get(f"in{s_}")
                fm_proj(slab, skeys, 4, 0, HT, 8,
                        lambda c, b, s_=s_: act(UT.v[:, 4 * s_ + c, :], psv(b), AF.Gelu, psk(b), UT.k(4 * s_ + c)))
                ring_prefetch()
            slab2, sk2 = ring_get("in2")
            slab3, sk3 = ring_get("in3")
            vstat = {}

            def v_mm(sub):
                vt = VT[sub % 2]
                for hf, (slab, skeys) in enumerate(((slab2, sk2), (slab3, sk3))):
                    sv = slab.rearrange("p (a b) -> p a b", b=512)
                    b = bank()
                    for kc in range(8):
                        mm(psv(b), HT.v[:, kc, 128 * sub:128 * sub + 128], sv[:, kc, :], kc == 0, kc == 7,
                           skeys + HT.k(kc), psk(b))
                    act(vt.v[:, 512 * hf:512 * hf + 512], psv(b), AF.Gelu, psk(b), vt.k())
                s = stat_slot(16)
                vstat[sub] = s
                bsv = stat[:, s:s + 12].rearrange("p (a b) -> p a b", b=6)
                for hf in range(2):
                    S.op("dve", lambda h, hf=hf, vt=vt, bsv=bsv: h.bn_stats(out=bsv[:, hf, :], in_=vt.v[:, 512 * hf:512 * hf + 512]),
                         vt.k(), stk(s + 6 * hf, 6))
                S.op("dve", lambda h, s=s, bsv=bsv: h.bn_aggr(out=stat[:, s + 12:s + 14], in_=bsv), stk(s, 12), stk(s + 12, 2))

            def v_ln(sub):
                vt = VT[sub % 2]
                s = vstat[sub]
                ts("pool", stat[:, s + 14:s + 15], stat[:, s + 13:s + 14], EPS, None, ALU.add, None, stk(s + 13), stk(s + 14))
                tt("pool", stat[:, s + 15:s + 16], stat[:, s + 14:s + 15], mhalf[:, 0:1], ALU.pow, stk(s + 14) + ["mhalf"], stk(s + 15))
                vh = VHAT[sub % 2]
                ts("dve", vh.v, vt.v, stat[:, s + 12:s + 13], stat[:, s + 15:s + 16], ALU.subtract, ALU.mult,
                   vt.k() + stk(s + 12) + stk(s + 15), vh.k())

            def gmlp(sub):
                vh = VHAT[sub % 2]
                b = bank(2)
                for g in range(8):
                    mm(PS[:, 512 * b + 128 * g:512 * b + 128 * g + 128], vh.v[:, 128 * g:128 * g + 128], wst_bf[:, g, :],
                       True, True, vh.k() + ["wst_bf"], psk(b + g // 4))
                tmp = TMPA[sub % 2]
                for g in range(8):
                    stt("dve", tmp.v[:, 128 * g:128 * g + 128], PS[:, 512 * b + 128 * g:512 * b + 128 * g + 128],
                        glng[:, g:g + 1], tb[:, g, :], ALU.mult, ALU.add, psk(b + g // 4) + ["glng", "tb"], tmp.k())
                tt("dve", YGM.v[:, :, 128 * sub:128 * sub + 128], tmp.v.rearrange("p (a b) -> p a b", b=128),
                   UT.v[:, :, 128 * sub:128 * sub + 128], ALU.mult, tmp.k() + UT.k(), YGM.k())

            def fm_chunk(slab, skeys, c, rhsbuf, nk, consume, t0=0, tn=512, b=None):
                sv = slab.rearrange("p (a b) -> p a b", b=512)
                b = bank() if b is None else b
                for kc in range(nk):
                    mm(PS[:, 512 * b:512 * b + tn], sv[:, kc, 128 * c:128 * c + 128], rhsbuf.v[:, kc, t0:t0 + tn],
                       kc == 0, kc == nk - 1, skeys + rhsbuf.k(kc), psk(b))
                consume(c, b)

            def z_cons(c, b):
                cp("act", ZS5.v[:, c, :], psv(b), psk(b), ZS5.k(c))

            def ga_cons(s_):
                return lambda c, b: act(SGA.v[:, 4 * s_ + c, :], psv(b), AF.Tanh, psk(b), SGA.k(4 * s_ + c), scale=0.5)

            v_mm(0)
            v_mm(1)
            slab4, sk4 = ring_get("in4")
            fm_chunk(slab4, sk4, 0, HT, 8, z_cons)
            fm_chunk(slab4, sk4, 1, HT, 8, z_cons)
            v_ln(0)
            gmlp(0)
            v_mm(2)
            fm_chunk(slab4, sk4, 2, HT, 8, z_cons)
            fm_chunk(slab4, sk4, 3, HT, 8, z_cons)
            slab5, sk5 = ring_get("in5")
            v_ln(1)
            gmlp(1)
            v_mm(3)
            for c in range(4):
                fm_chunk(slab5, sk5, c, HT, 8, ga_cons(0))
            v_ln(2)
            gmlp(2)
            slab6, sk6 = ring_get("in6")
            fm_chunk(slab6, sk6, 0, HT, 8, ga_cons(1))
            fm_chunk(slab6, sk6, 1, HT, 8, ga_cons(1))
            v_ln(3)
            gmlp(3)
            fm_chunk(slab6, sk6, 2, HT, 8, ga_cons(1))
            fm_chunk(slab6, sk6, 3, HT, 8, ga_cons(1))
            ring_prefetch()

            for c in range(4):
                cp("dve", ZS5J.v[:, c, :].rearrange("p (j t) -> p j t", j=8), ZS5.v[:, c, :].rearrange("p (t j) -> p j t", j=8),
                   ZS5.k(c), ZS5J.k(c))
            wa = []
            for s_ in range(2):
                slab, skeys = ring_get(f"wa{s_}")
                wa.append((slab.rearrange("p (a b c d) -> p a b c d", b=2, c=8, d=128), skeys))
            bx = bank(4)
            assert bx % 4 == 0
            for r_ in range(8):
                ct, gq = divmod(r_, 2)
                wv, wkeys = wa[ct // 2]
                for j in range(8):
                    for q in range(4):
                        o_ = PS[:, 512 * (bx + q) + 64 * r_:512 * (bx + q) + 64 * r_ + 64]
                        mm(o_, wv[32 * q:32 * q + 32, ct % 2, gq, j, :], ZS5J.v[32 * q:32 * q + 32, ct, 64 * j:64 * j + 64],
                           j == 0, j == 7, wkeys + ZS5J.k(ct), psk(bx + q), tp=(32 * q, 0))
            for q in range(4):
                cp("act" if q % 2 else "dve", XA.v[:, 8 * q:8 * q + 8, :], psv(bx + q).rearrange("p (a b) -> p a b", b=64),
                   psk(bx + q), XA.k(8 * q, 8))
            ring_prefetch()
            at0, at0k = ring_get("at0")
            at0v = at0.rearrange("p (a b) -> p a b", b=128)
            bc_ = bank()
            for sg in range(32):
                mm(PS[:, 512 * bc_ + sg:512 * bc_ + sg + 1], at0v[:, sg, :], e_prev[:, sg, 64:65], True, True,
                   at0k + ek_prev, psk(bc_))
            tt("dve", XA.v[:, :, 0], XA.v[:, :, 0], PS[:, 512 * bc_:512 * bc_ + 32], ALU.add, XA.k() + psk(bc_), XA.k())
            cp("pool", e_cur[:, :, 0:1], e_prev[:, :, 64:65], ek_prev, ek_cur)
            fb = (bx + 4) % 8
            fctr = [0]
            pa_slab = {}

            def pa_chunk(c8):
                s_, c = divmod(c8, 4)
                slab, skeys = pa_slab[s_]
                bsel = fb + fctr[0] % 4
                fctr[0] += 1
                fm_chunk(slab, skeys, c, YGM, 8,
                         lambda c, b, s_=s_: stt("dve", MRG.v[:, 4 * s_ + c, :], SGA.v[:, 4 * s_ + c, :], 1.0, psv(b), ALU.add, ALU.mult,
                                                 psk(b) + SGA.k(4 * s_ + c), MRG.k(4 * s_ + c)), b=bsel)

            for m in range(6):
                if m == 0:
                    atv, atk = at0v, at0k
                    pa_slab[0] = ring_get("gm0", hold=True)
                else:
                    a_, atk = ring_get(f"at{m}")
                    atv = a_.rearrange("p (a b) -> p a b", b=128)
                    if m == 2:
                        pa_slab[1] = ring_get("gm1", hold=True)
                    if m == 4:
                        gb_slab = ring_get("in7", hold=True)
                sh = 1 << m
                for q in range(4):
                    b = bx + q
                    for r_ in range(8):
                        sg = 8 * q + r_
                        base = 512 * b + 64 * r_
                        mm(PS[:, base + sh:base + 64], atv[:, sg, :], XA.v[:, sg, 0:64 - sh], True, True,
                           atk + XA.k(sg), psk(b))
                    tt("dve", XA.v[:, 8 * q:8 * q + 8, sh:64], XA.v[:, 8 * q:8 * q + 8, sh:64],
                       psv(b).rearrange("p (a b) -> p a b", b=64)[:, :, sh:64], ALU.add, psk(b) + XA.k(8 * q, 8), XA.k(8 * q, 8))
                if m < 4:
                    pa_chunk(2 * m)
                    pa_chunk(2 * m + 1)
                if m >= 4:
                    for c in (2 * (m - 4), 2 * (m - 4) + 1):
                        bsel = fb + fctr[0] % 4
                        fctr[0] += 1
                        fm_chunk(gb_slab[0], gb_slab[1], c, HT, 8,
                                 lambda c, b: act(SGA.v[:, c, :], psv(b), AF.Tanh, psk(b), SGA.k(c), scale=0.5), b=bsel)
                    if m == 5:
                        ring_unhold("in7")
                if m == 1:
                    ring_unhold("gm0")
                if m == 3:
                    ring_unhold("gm1")
                ring_prefetch()
            cp("act", e_cur[:, 0:16, 1:65], XA.v[:, 0:16, :], XA.k(0, 16), ek_cur)
            cp("dve", e_cur[:, 16:32, 1:65], XA.v[:, 16:32, :], XA.k(16, 16), ek_cur)
            kd, kdk = ring_get("kd")
            kdv = kd.rearrange("p (a b c) -> p a b c", b=8, c=128)
            wcs, wck = ring_get("wc")
            wcv = wcs.rearrange("p (a b) -> p a b", b=128)
            for ct in range(4):
                b = bx + ct
                for d_ in range(8):
                    mm(PS[:, 512 * b + 64 * d_:512 * b + 512], kdv[:, ct, d_, :],
                       ZS5J.v[:, ct, 0:64 * (8 - d_)], d_ == 0, False, kdk + ZS5J.k(ct), psk(b))
            def c_state(ct):
                by = fb + 2 * (ct % 2)
                for g8 in range(8):
                    q, gq = divmod(g8, 2)
                    sg = q * 8 + ct * 2 + gq
                    mm(PS[0:64, 512 * by + 128 * g8:512 * by + 128 * g8 + 128], e_cur[:, sg, 0:64], wcv[:, sg, :], True, True,
                       wck + ek_cur, psk(by + g8 // 4))
                yt = YTOK[ct % 2]
                cp("act" if ct % 2 else "dve", yt.v[0:64, :].rearrange("p (j g h) -> p g j h", j=8, g=8, h=16),
                   PS[0:64, 512 * by:512 * by + 1024].rearrange("p (g j h) -> p g j h", g=8, j=8, h=16),
                   psk(by, 2), yt.k())

            def c_back(ct):
                b = bx + ct
                yt = YTOK[ct % 2]
                for jp in range(8):
                    mm(PS[:, 512 * b + 64 * jp:512 * b + 64 * jp + 64], yt.v[0:64, 128 * jp:128 * jp + 128],
                       ident_bf[0:64, 0:64], False, jp == 7, yt.k() + ["ident"], psk(b))
                act(YGT.v[:, ct, :], psv(b), AF.Gelu, psk(b), YGT.k(ct))

            c_state(0)
            c_state(1)
            c_back(0)
            c_state(2)
            c_back(1)
            c_state(3)
            c_back(2)
            c_back(3)
            ring_prefetch()
            slab, skeys = ring_get("glu")
            fm_proj(slab, skeys, 4, 0, YGT, 4,
                    lambda c, b: act(SGL.v[:, c, :], psv(b), AF.Tanh, psk(b), SGL.k(c), scale=0.5))
            for c in range(4):
                stt("dve", YS5.v[:, c, :], SGL.v[:, c, :], 1.0, YGT.v[:, c, :], ALU.add, ALU.mult, YGT.k(c) + SGL.k(c), YS5.k(c))
            ring_prefetch()
            for s_ in range(2):
                if s_ == 1:
                    gslab, gkeys = ring_get("in8")
                bslab, bkeys = ring_get(f"bs{s_}")
                for c in range(4):
                    if s_ == 1:
                        fm_chunk(gslab, gkeys, c, HT, 8,
                                 lambda c, b: act(SGA.v[:, 4 + c, :], psv(b), AF.Tanh, psk(b), SGA.k(4 + c), scale=0.5))

                    def cons(c, b, s_=s_):
                        m2 = M2[c % 2]
                        stt("dve", m2.v.rearrange("p (t j) -> p t j", j=8), SGA.v[:, 4 * s_ + c, :].rearrange("p (t j) -> p t j", j=8), 1.0,
                            psv(b).rearrange("p (j t) -> p t j", j=8), ALU.add, ALU.mult, psk(b) + SGA.k(4 * s_ + c), m2.k())
                        tt("dve", MRG.v[:, 4 * s_ + c, :], MRG.v[:, 4 * s_ + c, :], m2.v, ALU.add,
                           MRG.k(4 * s_ + c) + m2.k(), MRG.k(4 * s_ + c))
                    fm_chunk(bslab, bkeys, c, YS5, 4, cons)
                ring_prefetch()

            def stage_b(sub):
                ht = HTOK[sub % 2]
                norm_sub_a(xs, sub, JUNK, ht)
                norm_sub_b(sub, ht)

            def tm_out(names, srcbuf, gi, follow=False, nxt=None):
                s0, k0 = ring_get(names[0])
                s1, k1 = ring_get(names[1])
                def m_a(sub):
                    b = bank(2)
                    for hf, (slab, skeys) in enumerate(((s0, k0), (s1, k1))):
                        sv = slab.rearrange("p (a b) -> p a b", b=512)
                        for kc in range(8):
                            mm(psv(b + hf), srcbuf.v[:, kc, 128 * sub:128 * sub + 128], sv[:, kc, :], kc == 0, kc == 7,
                               skeys + srcbuf.k(kc), psk(b + hf))
                    return b

                def a_(sub, b):
                    post_norm_residual(xs, sub, psv(b, 2), psk(b, 2), gi)

                if not follow:
                    for sub in range(4):
                        a_(sub, m_a(sub))
                    ring_prefetch()
                    return
                a_(0, m_a(0))
                a_(1, m_a(1))
                b2 = m_a(2)
                norm_sub_a(xs, 0, JUNK, HTOK4[0])
                a_(2, b2)
                b3 = m_a(3)
                norm_sub_a(xs, 1, JUNK, HTOK4[1])
                a_(3, b3)
                norm_sub_a(xs, 2, JUNK, HTOK4[2])
                norm_sub_a(xs, 3, JUNK, HTOK4[3])
                for sub in range(4):
                    norm_sub_b(sub, HTOK4[sub])
                if nxt is not None:
                    nxt()
                ring_prefetch()

            def q_first():
                slab, skeys = ring_get("q0")
                fm_proj(slab, skeys, 4, 0, HT, 8, lambda c, b: cp("act", QT.v[:, c, :], psv(b), psk(b), QT.k(c)))

            def gu_chunk(slab, skeys, s_, c, t0=0, tn=512):
                sv = slab.rearrange("p (a b) -> p a b", b=512)
                j = 2 * s_ + c
                bg = bank()
                for kc in range(8):
                    mm(PS[:, 512 * bg:512 * bg + tn], sv[:, kc, 128 * c:128 * c + 128], HT.v[:, kc, t0:t0 + tn], kc == 0, kc == 7,
                       skeys + HT.k(kc), psk(bg))
                si = SILU[j % 2]
                act(si.v[:, 0:tn], PS[:, 512 * bg:512 * bg + tn], AF.Silu, psk(bg), si.k())
                bu = bank()
                for kc in range(8):
                    mm(PS[:, 512 * bu:512 * bu + tn], sv[:, kc, 256 + 128 * c:256 + 128 * c + 128], HT.v[:, kc, t0:t0 + tn],
                       kc == 0, kc == 7, skeys + HT.k(kc), psk(bu))
                tt("dve", ACTB.v[:, j, t0:t0 + tn], PS[:, 512 * bu:512 * bu + tn], si.v[:, 0:tn], ALU.mult,
                   psk(bu) + si.k(), ACTB.k(j))

            def gu_first():
                slab, skeys = ring_get("gu0")
                for c in range(2):
                    gu_chunk(slab, skeys, 0, c)

            q_slab, gu_slab = [], []

            tm_out(("mo0", "mo1"), MRG, 0, follow=(stop_after != "mixer"), nxt=(q_first if stop_after != "mixer" else None))

            if stop_after != "mixer":
                slab, skeys = ring_get("q1")
                fm_proj(slab, skeys, 4, 0, HT, 8,
                        lambda c, b: cp("act", QT.v[:, 4 + c, :], psv(b), psk(b), QT.k(4 + c)))
                ring_prefetch()
                for hh in range(4):
                    for mc in range(2):
                        b = bank()
                        for dc in range(2):
                            mm(psv(b), KT[:, sq, 2 * hh + dc, 128 * mc:128 * mc + 128], QT.v[:, 2 * hh + dc, :], dc == 0, dc == 1,
                               KTK(sq) + QT.k(2 * hh + dc), psk(b))
                        act(PT.v[:, 2 * hh + mc, :], psv(b), AF.Exp, psk(b), PT.k(2 * hh + mc), scale=1.0 / 16.0)
                for hh in range(4):
                    b = bank()
                    for mc in range(2):
                        mm(psv(b), ones_bf[:], PT.v[:, 2 * hh + mc, :], mc == 0, mc == 1, ["ones_bf"] + PT.k(2 * hh + mc), psk(b))
                    rs = RS[hh % 2]
                    act(rs.v, psv(b), AF.Ln, psk(b), rs.k())
                    act(rs.v, rs.v, AF.Exp, rs.k(), rs.k(), scale=-1.0)
                    for dc in range(2):
                        b = bank()
                        for mc in range(2):
                            mm(psv(b), VV[:, sq, mc, 256 * hh + 128 * dc:256 * hh + 128 * dc + 128], PT.v[:, 2 * hh + mc, :],
                               mc == 0, mc == 1, VVK(sq) + PT.k(2 * hh + mc), psk(b))
                        tt("dve", OT.v[:, 2 * hh + dc, :], psv(b), rs.v, ALU.mult, psk(b) + rs.k(), OT.k(2 * hh + dc))
                tm_out(("o0", "o1"), OT, 1, follow=(stop_after == "ffn"), nxt=(gu_first if stop_after == "ffn" else None))

            if stop_after == "ffn":
                for s_ in range(1, 11):
                    slab, skeys = ring_get(f"gu{s_}")
                    for c in range(2):
                        gu_chunk(slab, skeys, s_, c)
                    ring_prefetch()
                    if s_ == 4 and ti + 1 < ntiles:
                        xn = XS[(ti + 1) % 2]
                        for sub in range(4):
                            norm_sub_a(xn, sub, JUNK2, HTOK4[sub])
                for hf in range(2):
                    slabs = [ring_get(f"dn{hf}{kg}") for kg in range(3)]
                    for sub in range(4):
                        b = bank()
                        for j in range(22):
                            slab, skeys = slabs[j // 8]
                            sv = slab.rearrange("p (a b) -> p a b", b=512)
                            mm(psv(b), ACTB.v[:, j, 128 * sub:128 * sub + 128], sv[:, j % 8, :], j == 0, j == 21,
                               skeys + ACTB.k(j), psk(b))
                        cp("act" if sub % 2 else "dve", YBUF.v[:, sub, 512 * hf:512 * hf + 512], psv(b), psk(b), YBUF.k(sub))
                    ring_prefetch()
                if ti + 1 < ntiles:
                    for sub in range(4):
                        norm_sub_b(sub, HTOK4[sub])
                    mixer_norm_done[0] = True
                for sub in range(4):
                    post_norm_residual(xs, sub, YBUF.v[:, sub, :], YBUF.k(sub), 2)

            r0 = x_rows(ti)
            store_ops.append(dma("pool", y_d[r0:r0 + TT, :].rearrange("(s p) d -> p s d", p=128), xs.v, xs.k(), (),
                                 "ys%d" % (ti % 2)))

        S.emit(finals=[("pool", i) for i in store_ops])
    return nc


_NC_CACHE = {}


def kernel(**inputs):
    inp = {k: np.asarray(v) for k, v in inputs.items()}
    x = inp["x"]
    mem = inp["mem"]
    B = x.shape[0]
    nseq = B // NCORES
    wf = _host_weights(inp)
    sm = _host_small(inp)
    key = ("full", nseq)
    if key not in _NC_CACHE:
        _NC_CACHE[key] = build(nseq=nseq, tps=SEQ // TT)
    nc = _NC_CACHE[key]
    in_maps = []
    for c in range(NCORES):
        m = {"x": np.ascontiguousarray(x[c * nseq:(c + 1) * nseq].reshape(nseq * SEQ, D)),
             "mem": np.ascontiguousarray(mem[c * nseq:(c + 1) * nseq].reshape(nseq * 256, D)),
             "wf": wf}
        m.update(sm)
        in_maps.append(m)
    res = run_bass_kernel_spmd(nc, in_maps, core_ids=list(range(NCORES)))
    out = np.concatenate([np.asarray(r["y"]).reshape(nseq, SEQ, D) for r in res.results], axis=0)
    return out.astype(np.float32)
```
